# Optimizing a Trainium2 kernel written in Bass

```python
import math
import jax, jax.numpy as jnp
from jax import lax
import numpy as np

D_MODEL = 1024
BATCH = 32
SEQ = 2048
DEPTH = 2
DEC_BATCH = 16
DEC_SEQ = 2048
PAST_LEN = 128

RET_HEADS = 8
RET_QK_DIM = 64
RET_V_DIM = 128
RET_QK = RET_HEADS * RET_QK_DIM
RET_V = RET_HEADS * RET_V_DIM
RET_CHUNK = 128
ROPE_BASE = 10000.0
SSM_WIDTH = 1024
SSM_GROUP = 16
SSM_GROUPS = SSM_WIDTH // SSM_GROUP
SSM_STATE = 64
DT_MIN = 1e-3
DT_MAX = 1e-1
EIG_CLIP = -1e-4
D_FF = 2816
ALPHA = (2 * DEPTH) ** 0.25
BETA = (8 * DEPTH) ** -0.25
LN_EPS = 1e-5
IN_SPLITS = (RET_QK, 2 * RET_QK, 2 * RET_QK + RET_V, 2 * RET_QK + 2 * RET_V,
             2 * RET_QK + 2 * RET_V + SSM_WIDTH, 2 * RET_QK + 2 * RET_V + SSM_WIDTH + D_MODEL)
IN_COLS = 2 * RET_QK + 2 * RET_V + SSM_WIDTH + 2 * D_MODEL

kernel_name = "hybrid_retention_s5_macaron_encoder"

F32 = jnp.float32


def layer_norm(x, g, b):
    xf = x.astype(F32)
    mu = jnp.mean(xf, -1, keepdims=True)
    var = jnp.mean(jnp.square(xf - mu), -1, keepdims=True)
    return ((xf - mu) * lax.rsqrt(var + LN_EPS) * g.astype(F32) + b.astype(F32)).astype(x.dtype)


def swiglu_ffn(x, w_gu, w_down):
    a, u = jnp.split(x @ w_gu, 2, axis=-1)
    return (jax.nn.silu(a) * u) @ w_down


def rotate(t, pos):
    half = t.shape[-1] // 2
    inv_freq = ROPE_BASE ** (-jnp.arange(half, dtype=F32) / half)
    ang = pos[:, None] * inv_freq[None, :]
    cos = jnp.cos(ang)[None, :, None, :]
    sin = jnp.sin(ang)[None, :, None, :]
    t1, t2 = t[..., :half], t[..., half:]
    return jnp.concatenate([t1 * cos - t2 * sin, t1 * sin + t2 * cos], axis=-1)


def retention_direction(q, k, v, log_gamma, strict):
    idx = jnp.arange(RET_CHUNK, dtype=F32)
    dist = idx[:, None] - idx[None, :]
    mask = (dist > 0) if strict else (dist >= 0)
    decay = jnp.where(mask[None], jnp.exp(log_gamma[:, None, None] * jnp.where(mask, dist, 0.0)[None]), 0.0)
    scores = jnp.einsum('bnihd,bnjhd->bnhij', q, k) * decay[None, None]
    inner = jnp.einsum('bnhij,bnjhe->bnihe', scores, v)
    xi = jnp.exp((idx[:, None] + 1.0) * log_gamma[None, :])
    zeta = jnp.exp((RET_CHUNK - 1.0 - idx[:, None]) * log_gamma[None, :])
    chunk_decay = jnp.exp(RET_CHUNK * log_gamma)[None, :, None, None]
    kv = jnp.einsum('bnjhd,bnjhe->nbhde', k * zeta[:, :, None], v)

    def step(state, kv_n):
        return chunk_decay * state + kv_n, state

    _, prev = lax.scan(step, jnp.zeros_like(kv[0]), kv)
    cross = jnp.einsum('bnihd,nbhde->bnihe', q * xi[:, :, None], prev)
    return inner + cross


def retention_branch(q, k, v, g, w_o):
    B, L, _ = q.shape
    nc = L // RET_CHUNK
    pos = jnp.arange(L, dtype=F32)
    qh = rotate(q.astype(F32).reshape(B, L, RET_HEADS, RET_QK_DIM), pos)
    kh = rotate(k.astype(F32).reshape(B, L, RET_HEADS, RET_QK_DIM), pos) * (RET_QK_DIM ** -0.5)
    vh = v.astype(F32).reshape(B, L, RET_HEADS, RET_V_DIM)
    log_gamma = jnp.log1p(-jnp.exp2(-5.0 - jnp.arange(RET_HEADS, dtype=F32)))
    to_chunks = lambda t: t.reshape(B, nc, RET_CHUNK, *t.shape[2:])
    rev = lambda t: jnp.flip(t, axis=1)
    o_fwd = retention_direction(to_chunks(qh), to_chunks(kh), to_chunks(vh), log_gamma, False)
    o_bwd = retention_direction(to_chunks(rev(qh)), to_chunks(rev(kh)), to_chunks(rev(vh)), log_gamma, True)
    o = o_fwd.reshape(B, L, RET_HEADS, RET_V_DIM) + rev(o_bwd.reshape(B, L, RET_HEADS, RET_V_DIM))
    mu = jnp.mean(o, -1, keepdims=True)
    var = jnp.mean(jnp.square(o - mu), -1, keepdims=True)
    o = (o - mu) * lax.rsqrt(var + LN_EPS)
    out = jax.nn.silu(g.astype(F32)) * o.reshape(B, L, RET_V)
    return out.astype(q.dtype) @ w_o


def _complex_affine_combine(e1, e2):
    a1r, a1i, b1r, b1i = e1
    a2r, a2i, b2r, b2i = e2
    return (a1r * a2r - a1i * a2i,
            a1r * a2i + a1i * a2r,
            a2r * b1r - a2i * b1i + b2r,
            a2r * b1i + a2i * b1r + b2i)


def s5_direction(us, lam_re, lam_im, log_dt, b_re, b_im, c_re, c_im, reverse):
    L = us.shape[0]
    lr = jnp.minimum(lam_re.astype(F32), EIG_CLIP)
    li = lam_im.astype(F32)
    dt = jnp.exp(log_dt.astype(F32))[:, None]
    mag = jnp.exp(lr * dt)
    ab_re = mag * jnp.cos(li * dt)
    ab_im = mag * jnp.sin(li * dt)
    den = lr * lr + li * li
    num_re = ab_re - 1.0
    f_re = (num_re * lr + ab_im * li) / den
    f_im = (ab_im * lr - num_re * li) / den
    br, bi = b_re.astype(F32), b_im.astype(F32)
    bb_re = f_re[..., None] * br - f_im[..., None] * bi
    bb_im = f_re[..., None] * bi + f_im[..., None] * br
    bu_re = jnp.einsum('lbgp,gnp->lbgn', us, bb_re)
    bu_im = jnp.einsum('lbgp,gnp->lbgn', us, bb_im)
    a_re = jnp.broadcast_to(ab_re[None, None], (L, 1) + ab_re.shape)
    a_im = jnp.broadcast_to(ab_im[None, None], (L, 1) + ab_im.shape)
    _, _, h_re, h_im = lax.associative_scan(_complex_affine_combine, (a_re, a_im, bu_re, bu_im),
                                            reverse=reverse, axis=0)
    return (jnp.einsum('gpn,lbgn->lbgp', c_re.astype(F32), h_re)
            - jnp.einsum('gpn,lbgn->lbgp', c_im.astype(F32), h_im))


def s5_branch(u, lam_re, lam_im, log_dt, b_re, b_im, c_re, c_im, d_skip, w_glu):
    B, L, _ = u.shape
    us = jnp.swapaxes(u.astype(F32), 0, 1).reshape(L, B, SSM_GROUPS, SSM_GROUP)
    y_f = s5_direction(us, lam_re[0], lam_im[0], log_dt[0], b_re[0], b_im[0], c_re[0], c_im[0], False)
    y_b = s5_direction(us, lam_re[1], lam_im[1], log_dt[1], b_re[1], b_im[1], c_re[1], c_im[1], True)
    y = (y_f + y_b + us * d_skip.astype(F32).reshape(SSM_GROUPS, SSM_GROUP)).reshape(L, B, SSM_WIDTH)
    z = jax.nn.gelu(jnp.swapaxes(y, 0, 1)).astype(u.dtype)
    val, gate = jnp.split(z @ w_glu, 2, axis=-1)
    return val * jax.nn.sigmoid(gate)


def hybrid_mixer(x, w_in, b_gate, ret_w_o, s5_lam_re, s5_lam_im, s5_log_dt, s5_b_re, s5_b_im,
                 s5_c_re, s5_c_im, s5_d, s5_w_glu, w_out):
    h = x @ w_in
    q, k, v, g, u, gate_ret, gate_ssm = jnp.split(h, IN_SPLITS, axis=-1)
    y_ret = retention_branch(q, k, v, g, ret_w_o)
    y_ssm = s5_branch(u, s5_lam_re, s5_lam_im, s5_log_dt, s5_b_re, s5_b_im, s5_c_re, s5_c_im, s5_d, s5_w_glu)
    bg_ret, bg_ssm = jnp.split(b_gate.astype(F32), 2)
    g_ret = jax.nn.sigmoid(gate_ret.astype(F32) + bg_ret)
    g_ssm = jax.nn.sigmoid(gate_ssm.astype(F32) + bg_ssm)
    merged = g_ret * y_ret.astype(F32) + g_ssm * y_ssm.astype(F32)
    return merged.astype(x.dtype) @ w_out


def trunk(x, params):
    (ffn1_w_gu, ffn1_w_down, ln1_g, ln1_b, w_in, b_gate, ret_w_o, s5_lam_re, s5_lam_im, s5_log_dt,
     s5_b_re, s5_b_im, s5_c_re, s5_c_im, s5_d, s5_w_glu, w_out, ln2_g, ln2_b,
     ffn2_w_gu, ffn2_w_down, ln3_g, ln3_b) = params
    for l in range(DEPTH):
        x = layer_norm(ALPHA * x + 0.5 * swiglu_ffn(x, ffn1_w_gu[l], ffn1_w_down[l]), ln1_g[l], ln1_b[l])
        mix = hybrid_mixer(x, w_in[l], b_gate[l], ret_w_o[l], s5_lam_re[l], s5_lam_im[l], s5_log_dt[l],
                           s5_b_re[l], s5_b_im[l], s5_c_re[l], s5_c_im[l], s5_d[l], s5_w_glu[l], w_out[l])
        x = layer_norm(ALPHA * x + mix, ln2_g[l], ln2_b[l])
        x = layer_norm(ALPHA * x + 0.5 * swiglu_ffn(x, ffn2_w_gu[l], ffn2_w_down[l]), ln3_g[l], ln3_b[l])
    return x


def setup_inputs(seed: int = 0) -> dict:
    key = jax.random.key(seed)
    ks = jax.random.split(key, 28)
    nrm = lambda k, shape, scale: jax.random.normal(k, shape, F32) * scale
    gain = lambda k: 1.0 + 0.02 * jax.random.normal(k, (DEPTH, D_MODEL), F32)
    bias = lambda k: 0.02 * jax.random.normal(k, (DEPTH, D_MODEL), F32)
    n_idx = jnp.arange(SSM_STATE, dtype=F32)
    lam_shape = (DEPTH, 2, SSM_GROUPS, SSM_STATE)
    return {
        "x_prompt": jax.random.normal(ks[0], (BATCH, SEQ, D_MODEL), F32),
        "x_sample": jax.random.normal(ks[1], (DEC_BATCH, DEC_SEQ, D_MODEL), F32),
        "ffn1_w_gu": nrm(ks[2], (DEPTH, D_MODEL, 2 * D_FF), D_MODEL ** -0.5),
        "ffn1_w_down": nrm(ks[3], (DEPTH, D_FF, D_MODEL), BETA * D_FF ** -0.5),
        "ln1_g": gain(ks[4]),
        "ln1_b": bias(ks[5]),
        "w_in": nrm(ks[6], (DEPTH, D_MODEL, IN_COLS), D_MODEL ** -0.5),
        "b_gate": nrm(ks[7], (DEPTH, 2 * D_MODEL), 0.01),
        "ret_w_o": nrm(ks[8], (DEPTH, RET_V, D_MODEL), RET_V ** -0.5),
        "s5_lam_re": -0.5 + 0.01 * jax.random.normal(ks[9], lam_shape, F32),
        "s5_lam_im": math.pi * n_idx + 0.01 * jax.random.normal(ks[10], lam_shape, F32),
        "s5_log_dt": jax.random.uniform(ks[11], (DEPTH, 2, SSM_GROUPS), F32,
                                        math.log(DT_MIN), math.log(DT_MAX)),
        "s5_b_re": nrm(ks[12], (DEPTH, 2, SSM_GROUPS, SSM_STATE, SSM_GROUP), (2 * SSM_GROUP) ** -0.5),
        "s5_b_im": nrm(ks[13], (DEPTH, 2, SSM_GROUPS, SSM_STATE, SSM_GROUP), (2 * SSM_GROUP) ** -0.5),
        "s5_c_re": nrm(ks[14], (DEPTH, 2, SSM_GROUPS, SSM_GROUP, SSM_STATE), (2 * SSM_STATE) ** -0.5),
        "s5_c_im": nrm(ks[15], (DEPTH, 2, SSM_GROUPS, SSM_GROUP, SSM_STATE), (2 * SSM_STATE) ** -0.5),
        "s5_d": nrm(ks[16], (DEPTH, SSM_WIDTH), 1.0),
        "s5_w_glu": nrm(ks[17], (DEPTH, SSM_WIDTH, 2 * D_MODEL), SSM_WIDTH ** -0.5),
        "w_out": nrm(ks[18], (DEPTH, D_MODEL, D_MODEL), BETA * D_MODEL ** -0.5),
        "ln2_g": gain(ks[19]),
        "ln2_b": bias(ks[20]),
        "ffn2_w_gu": nrm(ks[21], (DEPTH, D_MODEL, 2 * D_FF), D_MODEL ** -0.5),
        "ffn2_w_down": nrm(ks[22], (DEPTH, D_FF, D_MODEL), BETA * D_FF ** -0.5),
        "ln3_g": gain(ks[23]),
        "ln3_b": bias(ks[24]),
    }


def reference(x_prompt, x_sample, ffn1_w_gu, ffn1_w_down, ln1_g, ln1_b, w_in, b_gate, ret_w_o,
              s5_lam_re, s5_lam_im, s5_log_dt, s5_b_re, s5_b_im, s5_c_re, s5_c_im, s5_d, s5_w_glu,
              w_out, ln2_g, ln2_b, ffn2_w_gu, ffn2_w_down, ln3_g, ln3_b):
    params = (ffn1_w_gu, ffn1_w_down, ln1_g, ln1_b, w_in, b_gate, ret_w_o, s5_lam_re, s5_lam_im, s5_log_dt,
              s5_b_re, s5_b_im, s5_c_re, s5_c_im, s5_d, s5_w_glu, w_out, ln2_g, ln2_b,
              ffn2_w_gu, ffn2_w_down, ln3_g, ln3_b)
    y_prompt = trunk(x_prompt, params)
    y_sample = trunk(x_sample, params)
    return (y_prompt, y_sample)
```

```python
import math
import numpy as np
import concourse.bass as bass
import concourse.mybir as mybir
from concourse.bass_utils import run_bass_kernel_spmd

F32 = mybir.dt.float32
BF16 = mybir.dt.bfloat16
AF = mybir.ActivationFunctionType
ALU = mybir.AluOpType

D = 1024
DFF = 2816
NJ = DFF // 128
L_SEQ = 2048
DEPTH = 2
ALPHA = (2 * DEPTH) ** 0.25
EPS = 1e-5
NH = 8
T = 512


class Dep:
    __slots__ = ("w", "r", "rd", "wd")

    def __init__(self):
        self.w = None
        self.r = {}
        self.rd = []
        self.wd = []


class KB:
    EPOCH = 30000
    KD = 24

    def __init__(self):
        nc = bass.Bass("TRN2", target_bir_lowering=False)
        self.nc = nc
        self.eng = {"pe": nc.tensor, "act": nc.scalar, "dve": nc.vector, "pool": nc.gpsimd, "sp": nc.sync}
        self.cnt = {e: 0 for e in self.eng}
        self.sems = {e: [] for e in self.eng}
        self.waited = {e: {} for e in self.eng}
        self.dsems = [nc.alloc_semaphore(name=f"dma{i}") for i in range(self.KD)]
        self.ndma = 0
        self.nins = 0

    SB_LO = 16512
    SB_HI = 229344

    def sb(self, name, shape, dt):
        if not hasattr(self, "sb_off"):
            self.sb_off = self.SB_LO
            self.sb_peak = self.SB_LO
        n = 1
        for s in shape[1:]:
            n *= s
        nbytes = (n * mybir.dt.size(dt) + 31) // 32 * 32
        off = self.sb_off
        assert off + nbytes <= self.SB_HI, f"SBUF overflow allocating {name}: {off + nbytes - self.SB_HI} bytes over"
        self.sb_off += nbytes
        self.sb_peak = max(self.sb_peak, self.sb_off)
        return self.nc.alloc_sbuf_tensor_at(name, list(shape), dt, offset=off)

    def mark(self):
        if not hasattr(self, "sb_off"):
            self.sb_off = self.SB_LO
            self.sb_peak = self.SB_LO
        return self.sb_off

    def release(self, m):
        self.sb_off = m

    def barrier(self, dma=True):
        evs = [("d", i) for i in range(max(0, self.ndma - self.KD), self.ndma)] if dma else []
        for e in self.eng:
            if self.cnt[e] > 0:
                evs.append(("c", e, self.cnt[e]))
        for e in self.eng:
            self._wait(e, [ev for ev in evs if not (ev[0] == "c" and ev[1] == e)])

    def ps(self, name, shape, dt=F32):
        return self.nc.alloc_psum_tensor(name, list(shape), dt)

    def _sem(self, e, epoch):
        while len(self.sems[e]) <= epoch:
            self.sems[e].append(self.nc.alloc_semaphore(name=f"s_{e}_{len(self.sems[e])}"))
        return self.sems[e][epoch]

    def _wait(self, e, evs):
        best = {}
        for ev in evs:
            if ev[0] == "c":
                _, e2, c = ev
                if e2 == e and e == "pe":
                    continue
                key = ("c", e2, (c - 1) // self.EPOCH)
                val = (c - 1) % self.EPOCH + 1
            else:
                i = ev[1]
                key = ("d", i % self.KD)
                val = 16 * (i // self.KD + 1)
            if val > best.get(key, 0):
                best[key] = val
        wd = self.waited[e]
        for key, val in best.items():
            if wd.get(key, 0) >= val:
                continue
            if key[0] == "c":
                if any(k[0] == "c" and k[1] == key[1] and k[2] > key[2] for k in wd):
                    continue
                wd[key] = val
                self.eng[e].wait_ge(self._sem(key[1], key[2]), val)
            else:
                wd[key] = val
                self.eng[e].wait_ge(self.dsems[key[1]], val)
            self.nins += 1

    def _deps(self, r, w, e):
        evs = []
        for d in r:
            if d.w is not None:
                evs.append(d.w)
            evs.extend(d.wd)
        for d in w:
            if d.w is not None:
                evs.append(d.w)
            evs.extend(d.wd)
            for k, ev in d.r.items():
                if k != e:
                    evs.append(ev)
            evs.extend(d.rd)
        return evs

    def op(self, e, fn, r=(), w=(), extra=()):
        self._wait(e, self._deps(r, w, e) + list(extra))
        ins = fn(self.eng[e])
        self.cnt[e] += 1
        c = self.cnt[e]
        ins.then_inc(self._sem(e, (c - 1) // self.EPOCH), 1)
        ev = ("c", e, c)
        for d in r:
            d.r[e] = ev
        for d in w:
            d.w = ev
            d.wd = []
            d.r = {}
            d.rd = []
        self.nins += 1
        return ev

    def dma(self, q, out, in_, r=(), w=(), **kw):
        self._wait(q, self._deps(r, w, "dma"))
        i = self.ndma
        self.ndma += 1
        if i >= self.KD:
            key = ("d", i % self.KD)
            val = 16 * (i // self.KD)
            if self.waited[q].get(key, 0) < val:
                self.waited[q][key] = val
                self.eng[q].wait_ge(self.dsems[i % self.KD], val)
        self.eng[q].dma_start(out=out, in_=in_, **kw).then_inc(self.dsems[i % self.KD], 16)
        ev = ("d", i)
        for d in r:
            d.rd.append(ev)
            if len(d.rd) > self.KD:
                del d.rd[0]
        for d in w:
            d.wd.append(ev)
            if len(d.wd) > self.KD:
                del d.wd[0]
            d.r = {}
            d.rd = []
        self.nins += 1
        return ev

    def finish(self):
        evs = [("d", i) for i in range(max(0, self.ndma - self.KD), self.ndma)]
        for e in self.eng:
            if self.cnt[e] > 0 and e != "sp":
                evs.append(("c", e, self.cnt[e]))
        self._wait("sp", evs)


class Rot:
    def __init__(self, kb, name, n, shape, dt, psum=False):
        self.t = [(kb.ps if psum else kb.sb)(f"{name}{i}", shape, dt) for i in range(n)]
        self.d = [Dep() for _ in range(n)]
        self.i = 0

    def next(self):
        k = self.i % len(self.t)
        self.i += 1
        return self.t[k], self.d[k]


def _layout():
    off = {}
    o = 0

    def add(name, n):
        nonlocal o
        off[name] = o
        o += n

    for f in (1, 2):
        for jb in range(11):
            add(f"f{f}gu{jb}", 8 * 512)
        add(f"f{f}d", NJ * 1024)
    for nm in ("wq", "wk", "wv", "wg", "wu", "wgr", "wgs", "wo", "glu_v", "glu_g", "wout"):
        add(nm, 8 * 1024)
    return off, o


W_OFF, W_TOT = _layout()


def _kmajor(w):
    K, C = w.shape
    return w.reshape(K // 128, 128, C).transpose(1, 0, 2)


def pack_weights(inp, l):
    out = np.empty((128, W_TOT), np.float32)

    def put(name, arr):
        a = arr.reshape(128, -1)
        out[:, W_OFF[name]:W_OFF[name] + a.shape[1]] = a

    for f in (1, 2):
        gu = _kmajor(inp[f"ffn{f}_w_gu"][l])
        for jb in range(11):
            g = gu[:, :, jb * 256:(jb + 1) * 256]
            u = gu[:, :, DFF + jb * 256:DFF + (jb + 1) * 256]
            put(f"f{f}gu{jb}", np.concatenate([g, u], axis=2))
        put(f"f{f}d", _kmajor(inp[f"ffn{f}_w_down"][l]))
    win = _kmajor(inp["w_in"][l])
    swap = np.arange(512).reshape(8, 2, 32)[:, ::-1, :].reshape(512)
    q = win[:, :, 0:512]
    k = win[:, :, 512:1024]
    put("wq", np.concatenate([q, q[:, :, swap]], axis=2))
    put("wk", np.concatenate([k, k[:, :, swap]], axis=2))
    put("wv", win[:, :, 1024:2048])
    put("wg", win[:, :, 2048:3072])
    perm = np.arange(1024).reshape(64, 16).T.reshape(1024)
    put("wu", win[:, :, 3072:4096][:, :, perm])
    put("wgr", win[:, :, 4096:5120])
    put("wgs", win[:, :, 5120:6144])
    put("wo", _kmajor(inp["ret_w_o"][l]))
    glu = inp["s5_w_glu"][l][perm, :]
    glu = _kmajor(glu)
    put("glu_v", glu[:, :, 0:1024])
    put("glu_g", glu[:, :, 1024:2048])
    put("wout", _kmajor(inp["w_out"][l]))
    return out


def rep128(v):
    return np.ascontiguousarray(np.broadcast_to(np.asarray(v, np.float32)[None, :], (128, v.shape[0])))


CT_DEC = 0
CT_ZB = CT_DEC + 1024
CT_ZF = CT_ZB + 512
CT_XF = CT_ZF + 512
CT_XB = CT_XF + 512
CT_G128 = CT_XB + 512
CT_MF = CT_G128 + 512
CT_MB = CT_MF + 128
CT_TOT = CT_MB + 128
NB = 4
C1 = 8 * NB
NK = L_SEQ // C1
NBLK = L_SEQ // 8
NT1 = 2 * NB
NT2 = 1 + 2 * (NB - 1) + 2 * NB
S5P_W = 3 * 64 + 4 * 1024


def const_tables():
    lg = np.log1p(-np.exp2(-5.0 - np.arange(NH, dtype=np.float64)))
    ct = np.zeros((128, CT_TOT), np.float64)
    j = np.arange(128)[:, None]
    i = np.arange(128)[None, :]
    for h in range(NH):
        ct[:, CT_DEC + h * 128:CT_DEC + (h + 1) * 128] = 0.125 * np.exp(lg[h] * np.abs(i - j))
        ct[:, CT_ZB + h * 64:CT_ZB + (h + 1) * 64] = 0.125 * np.exp(lg[h] * j)
        ct[:, CT_ZF + h * 64:CT_ZF + (h + 1) * 64] = 0.125 * np.exp(lg[h] * (127 - j))
    for p in range(128):
        h2 = p // 64
        for hp in range(4):
            h = 2 * hp + h2
            ii = np.arange(128)
            ct[p, CT_XF + hp * 128:CT_XF + (hp + 1) * 128] = np.exp(lg[h] * (ii + 1))
            ct[p, CT_XB + hp * 128:CT_XB + (hp + 1) * 128] = np.exp(lg[h] * (128 - ii))
            ct[p, CT_G128 + hp * 128:CT_G128 + (hp + 1) * 128] = np.exp(lg[h] * 128.0)
    m = np.arange(128)[:, None] // 16
    q = np.arange(128)[None, :] // 16
    ct[:, CT_MF:CT_MF + 128] = (q >= m)
    ct[:, CT_MB:CT_MB + 128] = (m >= q)
    half = 32
    inv_freq = (10000.0 ** (-np.arange(half, dtype=np.float32) / half)).astype(np.float32)
    pos = np.arange(L_SEQ, dtype=np.float32)
    ang = pos[None, :] * inv_freq[:, None]
    cos = np.cos(ang.astype(np.float32)).astype(np.float32)
    sin = np.sin(ang.astype(np.float32)).astype(np.float32)
    rope = np.zeros((2, 128, L_SEQ), np.float32)
    for p in range(128):
        d = p % 64
        rope[0, p] = cos[d % 32]
        rope[1, p] = -sin[d % 32] if d < 32 else sin[d % 32]
    return ct.astype(np.float32), rope


def s5_host_layout(inp, l):
    out = np.zeros((128, S5P_W), np.float32)

    def lay(a):
        sh = a.shape
        a = a.reshape(2, 2, 32, 64, -1)
        a = a.transpose(1, 3, 0, 2, 4)
        return a.reshape(128, -1)

    out[:, 0:64] = lay(inp["s5_lam_re"][l])
    out[:, 64:128] = lay(inp["s5_lam_im"][l])
    ldt = np.broadcast_to(inp["s5_log_dt"][l][:, :, None], (2, 64, 64))
    out[:, 128:192] = lay(np.ascontiguousarray(ldt))
    o = 192
    for nm in ("s5_b_re", "s5_b_im"):
        out[:, o:o + 1024] = lay(inp[nm][l])
        o += 1024
    for nm in ("s5_c_re", "s5_c_im"):
        out[:, o:o + 1024] = lay(inp[nm][l].transpose(0, 1, 3, 2))
        o += 1024
    dcol = np.zeros((128, 64), np.float32)
    dd = inp["s5_d"][l].reshape(64, 16)
    for s in range(8):
        dcol[s * 16:(s + 1) * 16, :] = dd.T
    return out, dcol


class Prog:
    def __init__(self, nseq, cfg):
        self.nseq = nseq
        self.cfg = cfg
        kb = self.kb = KB()
        nc = self.nc = kb.nc
        dt = nc.dram_tensor
        self.x = dt("x", [nseq, L_SEQ, D], F32, kind="ExternalInput").ap()
        self.y = dt("y", [nseq, L_SEQ, D], F32, kind="ExternalOutput").ap()
        self.wpack = dt("wpack", [DEPTH, 128, W_TOT], F32, kind="ExternalInput").ap()
        self.lntab = dt("lntab", [DEPTH, 3, 2, 128, D], F32, kind="ExternalInput").ap()
        self.ctab = dt("ctab", [128, CT_TOT], F32, kind="ExternalInput").ap()
        self.rope = dt("rope", [2, 128, L_SEQ], F32, kind="ExternalInput").ap()
        self.bgate = dt("bgate", [DEPTH, 128, 16], F32, kind="ExternalInput").ap()
        self.s5p = dt("s5p", [DEPTH, 128, S5P_W], F32, kind="ExternalInput").ap()
        self.s5d = dt("s5d", [DEPTH, 128, 64], F32, kind="ExternalInput").ap()
        self.wbf = dt("wbf", [DEPTH, 128, W_TOT], BF16, kind="Internal").ap()
        self.s5m = dt("s5m", [DEPTH, 64, 128, NT1 + NT2, 128], BF16, kind="Internal").ap()
        self.xa = dt("xa", [L_SEQ, D], F32, kind="Internal").ap()
        self.xb = dt("xb", [L_SEQ, D], F32, kind="Internal").ap()
        self.xts = dt("xts", [128, 8, L_SEQ], BF16, kind="Internal").ap()
        self.ks = dt("ks", [128, 4, L_SEQ], BF16, kind="Internal").ap()
        self.vs = dt("vs", [L_SEQ, D], BF16, kind="Internal").ap()
        self.us = dt("us", [L_SEQ, D], BF16, kind="Internal").ap()
        self.zs = dt("zs", [L_SEQ, D], BF16, kind="Internal").ap()
        self.d_wbf = [Dep() for _ in range(DEPTH)]
        self.d_s5m = [Dep() for _ in range(DEPTH)]
        self.d_xa = Dep()
        self.d_xb = Dep()
        self.d_in = Dep()
        self.d_scr = {k: Dep() for k in ("xts", "ks", "vs", "us", "zs")}
        self.dbg = {}
        if cfg.get("debug"):
            for nm, shp, dty in (("dbg_ks", [128, 4, L_SEQ], BF16), ("dbg_vs", [L_SEQ, D], BF16),
                                 ("dbg_us", [L_SEQ, D], BF16), ("dbg_zs", [L_SEQ, D], BF16),
                                 ("dbg_bs", [128, 16, 512], BF16),
                                 ("dbg_s5m", [64, 128, NT1 + NT2, 128], BF16)):
                self.dbg[nm] = dt(nm, shp, dty, kind="ExternalOutput").ap()
        self.alloc_persistent()

    def alloc_persistent(self):
        kb = self.kb
        self.idf = kb.sb("idf", [128, 128], F32)
        self.idb = kb.sb("idb", [128, 128], BF16)
        self.ct = kb.sb("ct", [128, CT_TOT], F32)
        self.d_const = Dep()
        self.pA = Rot(kb, "pA", 2, [128, 512], F32, psum=True)
        self.pU = Rot(kb, "pU", 2, [128, 512], F32, psum=True)
        self.pO = Rot(kb, "pO", 2, [128, 512], F32, psum=True)
        self.pT = Rot(kb, "pT", 2, [128, 512], F32, psum=True)
        self.base_mark = kb.mark()

    def ctv(self, off, n):
        return self.ct[:, off:off + n]

    def consts(self):
        kb = self.kb
        for t in (self.idf, self.idb):
            kb.op("pool", lambda e, t=t: e.memset(t[:], 1.0), w=[self.d_const])
            kb.op("pool", lambda e, t=t: e.affine_select(out=t[:], in_=t[:], pattern=[[-1, 128]],
                                                         compare_op=ALU.is_equal, fill=0.0, base=0,
                                                         channel_multiplier=1), r=[self.d_const], w=[self.d_const])
        kb.dma("sp", self.ct[:], self.ctab, w=[self.d_const])

    def cast_weights_gen(self):
        kb = self.kb
        PC = 2048
        pieces = []
        for l in range(DEPTH):
            o = 0
            while o < W_TOT:
                n = min(PC, W_TOT - o)
                pieces.append((l, o, n))
                o += n
        NBUF = 3
        i = 0
        while i < len(pieces):
            fin = Rot(kb, "cst_in", NBUF, [128, PC], F32)
            fout = Rot(kb, "cst_out", NBUF, [128, PC], BF16)
            self.cast_realloc = False
            pend = []
            while i < len(pieces) and not self.cast_realloc:
                while len(pend) < NBUF - 1 and i + len(pend) < len(pieces):
                    l, o, n = pieces[i + len(pend)]
                    bi, dbi = fin.next()
                    kb.dma("sp", bi[:, 0:n], self.wpack[l, :, o:o + n], w=[dbi])
                    pend.append((bi, dbi))
                l, o, n = pieces[i]
                bi, dbi = pend.pop(0)
                bo, dbo = fout.next()
                kb.op("act", lambda e, bi=bi, bo=bo, n=n: e.activation(out=bo[:, 0:n], in_=bi[:, 0:n], func=AF.Copy),
                      r=[dbi], w=[dbo])
                kb.dma("sp", self.wbf[l, :, o:o + n], bo[:, 0:n], r=[dbo], w=[self.d_wbf[l]])
                i += 1
                yield
        return

    def phase_begin(self, dma=True):
        self._last_phase = None
        self.kb.barrier(dma=dma)
        self.kb.release(self.base_mark)
        self.cast_realloc = True

    def wblock(self, l, name, buf, dbuf, q="sp"):
        o = W_OFF[name]
        for h in range(2):
            self.kb.dma(q, buf[:, 4 * h:4 * h + 4, :],
                        self.wbf[l][:, o + h * 4096:o + (h + 1) * 4096].rearrange("p (k c) -> p k c", c=1024),
                        r=[self.d_wbf[l]], w=[dbuf])

    def transposes_to_xT(self, xt, dxt, xT, dxT):
        kb = self.kb
        for k in range(8):
            pT, dpT = self.pT.next()
            for c in range(4):
                kb.op("pe", lambda e, c=c, k=k, pT=pT: e.transpose(pT[:, c * 128:(c + 1) * 128],
                                                                  xt[:, c, k * 128:(k + 1) * 128], self.idf[:]),
                      r=[dxt, self.d_const], w=[dpT])
            if k % 2 == 0:
                kb.op("act", lambda e, k=k, pT=pT: e.activation(out=xT[:, k, :], in_=pT[:], func=AF.Copy),
                      r=[dpT], w=[dxT])
            else:
                kb.op("dve", lambda e, k=k, pT=pT: e.tensor_copy(out=xT[:, k, :], in_=pT[:]), r=[dpT], w=[dxT])

    def layer_norm(self, rr, drr, eps, dst_ap, dst_dep):
        kb = self.kb
        st, dst_ = self.st.next()
        dsa, dsb = Dep(), Dep()
        kb.op("dve", lambda e: e.bn_stats(out=st[:, 0:6], in_=rr[:, 0:512]), r=[drr], w=[dsa, dst_])
        kb.op("dve", lambda e: e.bn_stats(out=st[:, 6:12], in_=rr[:, 512:1024]), r=[drr], w=[dsb, dst_])
        kb.op("dve", lambda e: e.bn_aggr(out=st[:, 12:14], in_=st[:, 0:12]), r=[dsa, dsb], w=[dst_])
        kb.op("dve", lambda e: e.tensor_scalar(out=st[:, 14:15], in0=st[:, 13:14], scalar1=float(eps), scalar2=None,
                                              op0=ALU.add), r=[dst_], w=[dst_])
        kb.op("act", lambda e: e.activation(out=st[:, 14:15], in_=st[:, 14:15], func=AF.Sqrt), r=[dst_], w=[dst_])
        kb.op("dve", lambda e: e.reciprocal(out=st[:, 15:16], in_=st[:, 14:15]), r=[dst_], w=[dst_])
        kb.op("dve", lambda e: e.tensor_scalar(out=rr[:], in0=rr[:], scalar1=st[:, 12:13], scalar2=st[:, 15:16],
                                              op0=ALU.subtract, op1=ALU.mult), r=[dst_, drr], w=[drr])
        xo, dxo = self.xo.next()
        kb.op("pool", lambda e: e.tensor_tensor(out=rr[:], in0=rr[:], in1=self.lnG[:], op=ALU.mult),
              r=[drr, self.d_ln], w=[drr])
        kb.op("pool", lambda e: e.tensor_tensor(out=xo[:], in0=rr[:], in1=self.lnB[:], op=ALU.add),
              r=[drr, self.d_ln], w=[dxo])
        kb.dma("pool", dst_ap, xo[:], r=[dxo], w=[dst_dep])

    def alloc_ln(self, l, ln_idx, nrr=2):
        kb = self.kb
        self.lnG = kb.sb("lnG", [128, D], F32)
        self.lnB = kb.sb("lnB", [128, D], F32)
        self.d_ln = Dep()
        self.st = Rot(kb, "st", 2, [128, 16], F32)
        self.rr = Rot(kb, "rr", nrr, [128, D], F32)
        self.xo = Rot(kb, "xo", 2, [128, D], F32)
        kb.dma("sp", self.lnG[:], self.lntab[l, ln_idx, 0], w=[self.d_ln])
        kb.dma("sp", self.lnB[:], self.lntab[l, ln_idx, 1], w=[self.d_ln])

    def ffn(self, l, f, src, src_dep, dst, dst_dep, ln_idx):
        kb = self.kb
        wbf = self.wbf[l]
        dW = self.d_wbf[l]
        if getattr(self, "_last_phase", None) == "ffn":
            (xt_r, xT_r, hid, d_hid, wgu_r, wd_r, silu_r) = self._ffn_ctx
            kb.dma("sp", self.lnG[:], self.lntab[l, ln_idx, 0], w=[self.d_ln])
            kb.dma("sp", self.lnB[:], self.lntab[l, ln_idx, 1], w=[self.d_ln])
        else:
            self.phase_begin()
            xt_r = Rot(kb, "xt", 2, [128, 4, D], F32)
            xT_r = Rot(kb, "xT", 1, [128, 8, T], BF16)
            hid = kb.sb("hid", [128, NJ, T], BF16)
            d_hid = [Dep() for _ in range(NJ)]
            wgu_r = Rot(kb, "wgu", 4, [128, 8, 512], BF16)
            wd_r = Rot(kb, "wd", 1, [128, NJ, 1024], BF16)
            silu_r = Rot(kb, "silu", 2, [128, T], F32)
            self.alloc_ln(l, ln_idx)
            self._ffn_ctx = (xt_r, xT_r, hid, d_hid, wgu_r, wd_r, silu_r)
        self._last_phase = "ffn"
        gu_off = [W_OFF[f"f{f}gu{jb}"] for jb in range(11)]
        d_off = W_OFF[f"f{f}d"]

        def load_x(t):
            xt, dxt = xt_r.next()
            kb.dma("sp", xt[:], src(t).rearrange("(c p) f -> p c f", p=128), r=[src_dep], w=[dxt])
            return xt, dxt

        NTL = L_SEQ // T
        NBLK_GU = 11 * NTL
        blks = {}

        def load_gu(i):
            if i >= NBLK_GU or i in blks:
                return
            jb = i % 11
            wg, dwg = wgu_r.next()
            kb.dma("sp", wg[:], wbf[:, gu_off[jb]:gu_off[jb] + 4096].rearrange("p (k c) -> p k c", c=512),
                   r=[dW], w=[dwg])
            blks[i] = (wg, dwg)

        nxt = load_x(0)
        load_gu(0)
        load_gu(1)
        load_gu(2)
        for t in range(NTL):
            xt, dxt = nxt
            xT, dxT = xT_r.next()
            self.transposes_to_xT(xt, dxt, xT, dxT)
            wd, dwd = wd_r.next()
            for jb in range(11):
                i = t * 11 + jb
                if jb == 1:
                    for j0 in range(0, NJ, 2):
                        kb.dma("sp", wd[:, j0:j0 + 2, :],
                               wbf[:, d_off + j0 * 1024:d_off + (j0 + 2) * 1024].rearrange("p (j c) -> p j c", c=1024),
                               r=[dW], w=[dwd])
                    if t + 1 < NTL:
                        nxt = load_x(t + 1)
                load_gu(i + 3)
                wg, dwg = blks.pop(i)
                for jj in range(2):
                    j = 2 * jb + jj
                    pA, dpA = self.pA.next()
                    pU, dpU = self.pU.next()
                    for k in range(8):
                        kb.op("pe", lambda e, k=k, pA=pA, wg=wg, jj=jj: e.matmul(
                            pA[:], wg[:, k, jj * 128:(jj + 1) * 128], xT[:, k, :], start=(k == 0), stop=(k == 7)),
                              r=[dwg, dxT], w=[dpA])
                    for k in range(8):
                        kb.op("pe", lambda e, k=k, pU=pU, wg=wg, jj=jj: e.matmul(
                            pU[:], wg[:, k, 256 + jj * 128:256 + (jj + 1) * 128], xT[:, k, :], start=(k == 0),
                            stop=(k == 7)), r=[dwg, dxT], w=[dpU])
                    sl, dsl = silu_r.next()
                    kb.op("act", lambda e, sl=sl, pA=pA: e.activation(out=sl[:], in_=pA[:], func=AF.Silu),
                          r=[dpA], w=[dsl])
                    kb.op("dve", lambda e, sl=sl, pU=pU, j=j: e.tensor_tensor(out=hid[:, j, :], in0=sl[:],
                                                                             in1=pU[:], op=ALU.mult),
                          r=[dsl, dpU], w=[d_hid[j]])
            for c in range(4):
                pO0, dpO0 = self.pO.next()
                pO1, dpO1 = self.pO.next()
                for half, (pO, dpO) in enumerate(((pO0, dpO0), (pO1, dpO1))):
                    for j in range(NJ):
                        kb.op("pe", lambda e, j=j, pO=pO, half=half, c=c: e.matmul(
                            pO[:], hid[:, j, c * 128:(c + 1) * 128], wd[:, j, half * 512:(half + 1) * 512],
                            start=(j == 0), stop=(j == NJ - 1)), r=[d_hid[j], dwd], w=[dpO])
                rr, drr = self.rr.next()
                for half, (pO, dpO) in enumerate(((pO0, dpO0), (pO1, dpO1))):
                    kb.op("dve", lambda e, pO=pO, half=half, rr=rr, c=c: e.scalar_tensor_tensor(
                        out=rr[:, half * 512:(half + 1) * 512], in0=xt[:, c, half * 512:(half + 1) * 512],
                        scalar=2.0 * ALPHA, in1=pO[:], op0=ALU.mult, op1=ALU.add), r=[dxt, dpO], w=[drr])
                self.layer_norm(rr, drr, 4.0 * EPS, dst(t)[c * 128:(c + 1) * 128, :], dst_dep)

    def rotary(self, pX, dpX, pXs, dpXs, cos, sin, dtab, tmp_r, out_ap, out_dep, also=None):
        kb = self.kb
        t1, dt1 = tmp_r.next()
        t2, dt2 = tmp_r.next()
        kb.op("dve", lambda e: e.tensor_tensor(out=t1[:], in0=pX[:], in1=cos[:], op=ALU.mult), r=[dpX, dtab], w=[dt1])
        kb.op("dve", lambda e: e.tensor_tensor(out=t2[:], in0=pXs[:], in1=sin[:], op=ALU.mult), r=[dpXs, dtab],
              w=[dt2])
        if also is None:
            kb.op("pool", lambda e: e.tensor_tensor(out=out_ap, in0=t1[:], in1=t2[:], op=ALU.add), r=[dt1, dt2],
                  w=[out_dep])
        else:
            kb.op("pool", lambda e: e.tensor_tensor(out=t1[:], in0=t1[:], in1=t2[:], op=ALU.add), r=[dt1, dt2],
                  w=[dt1])
            kb.op("act", lambda e: e.activation(out=out_ap, in_=t1[:], func=AF.Copy), r=[dt1], w=[out_dep])
            for fn in also:
                fn(t1, dt1)

    def kv_state_step(self, kz, dkz, v, dv, st, dst_):
        kb = self.kb
        pT, dpT = self.pT.next()
        for h in range(NH):
            b = (h % 2) * 64
            hp = h // 2
            kb.op("pe", lambda e, h=h, b=b, hp=hp: e.matmul(pT[b:b + 64, hp * 128:(hp + 1) * 128],
                                                           kz[:, h * 64:(h + 1) * 64], v[:, h * 128:(h + 1) * 128],
                                                           start=True, stop=True), r=[dkz, dv], w=[dpT])
        kb.op("pool", lambda e: e.tensor_tensor(out=st[:], in0=st[:], in1=self.ctv(CT_G128, 512), op=ALU.mult),
              r=[dst_, self.d_const], w=[dst_])
        kb.op("dve", lambda e: e.tensor_tensor(out=st[:], in0=st[:], in1=pT[:], op=ALU.add), r=[dst_, dpT],
              w=[dst_])

    def k_transpose_scaled(self, kT, dkT, c, ztab_off, kz_r):
        kb = self.kb
        pT, dpT = self.pT.next()
        pTb = pT[:].bitcast(BF16)
        for f in range(4):
            kb.op("pe", lambda e, f=f: e.transpose(pTb[:, f * 128:(f + 1) * 128], kT[:, f, c * 128:(c + 1) * 128],
                                                    self.idb[:]), r=[dkT, self.d_const], w=[dpT])
        kz, dkz = kz_r.next()
        kb.op("dve", lambda e: e.tensor_tensor(out=kz[:], in0=pTb[:, 0:512], in1=self.ctv(ztab_off, 512),
                                              op=ALU.mult), r=[dpT, self.d_const], w=[dkz])
        return kz, dkz

    def mixer_a(self, l, src, src_dep):
        kb = self.kb
        self.phase_begin()
        u2a = kb.sb("u2a", [128, NBLK, 64], BF16)
        self.d_u2a = Dep()
        d_us_tile = [Dep() for _ in range(L_SEQ // T)]
        usv = self.us.rearrange("(b s) (p g) -> (s p) b g", s=8, p=16)
        xt_r = Rot(kb, "xt", 2, [128, 4, D], F32)
        xT_r = Rot(kb, "xT", 2, [128, 8, T], BF16)
        wk = kb.sb("wk", [128, 8, 1024], BF16)
        wv = kb.sb("wv", [128, 8, 1024], BF16)
        wu = kb.sb("wu", [128, 8, 1024], BF16)
        dwk, dwv, dwu = Dep(), Dep(), Dep()
        rope_r = Rot(kb, "rope", 2, [128, 2, T], F32)
        kT_r = Rot(kb, "kT", 2, [128, 4, T], BF16)
        tmp_r = Rot(kb, "rtmp", 4, [128, T], F32)
        v_r = Rot(kb, "v", 2, [128, D], BF16)
        u_r = Rot(kb, "u", 2, [128, D], BF16)
        kz_r = Rot(kb, "kz", 2, [128, 512], BF16)
        st = self.bstate
        dst_ = self.d_bstate
        kb.op("pool", lambda e: e.memset(st[:], 0.0), w=[dst_])

        def load_x(t):
            xt, dxt = xt_r.next()
            kb.dma("sp", xt[:], src(t).rearrange("(c p) f -> p c f", p=128), r=[src_dep], w=[dxt])
            rp, drp = rope_r.next()
            kb.dma("sp", rp[:], self.rope[:, :, t * T:(t + 1) * T].rearrange("a p t -> p a t"), w=[drp])
            return xt, dxt, rp, drp

        NTL = L_SEQ // T
        nxt = load_x(NTL - 1)
        self.wblock(l, "wk", wk, dwk)
        self.wblock(l, "wv", wv, dwv)
        self.wblock(l, "wu", wu, dwu)
        for t in range(NTL - 1, -1, -1):
            xt, dxt, rp, drp = nxt
            if t > 0:
                nxt = load_x(t - 1)
            xT, dxT = xT_r.next()
            self.transposes_to_xT(xt, dxt, xT, dxT)
            kb.dma("pool", self.xts[:, :, t * T:(t + 1) * T], xT[:], r=[dxT], w=[self.d_scr["xts"]])
            kT, dkT = kT_r.next()
            for f in range(4):
                pA, dpA = self.pA.next()
                pU, dpU = self.pU.next()
                for k in range(8):
                    kb.op("pe", lambda e, k=k, f=f, pA=pA: e.matmul(pA[:], wk[:, k, f * 128:(f + 1) * 128], xT[:, k, :],
                                                                  start=(k == 0), stop=(k == 7)), r=[dwk, dxT], w=[dpA])
                for k in range(8):
                    kb.op("pe", lambda e, k=k, f=f, pU=pU: e.matmul(pU[:], wk[:, k, 512 + f * 128:512 + (f + 1) * 128],
                                                                  xT[:, k, :], start=(k == 0), stop=(k == 7)),
                          r=[dwk, dxT], w=[dpU])
                self.rotary(pA, dpA, pU, dpU, rp[:, 0, :], rp[:, 1, :], drp, tmp_r, kT[:, f, :], dkT)
            kb.dma("pool", self.ks[:, :, t * T:(t + 1) * T], kT[:], r=[dkT], w=[self.d_scr["ks"]])
            for c in range(3, -1, -1):
                cg = 4 * t + c
                rows = slice(cg * 128, (cg + 1) * 128)
                v, dv = v_r.next()
                u, du = u_r.next()
                for (wX, dwX, dstt, ddst, eng) in ((wv, dwv, v, dv, "act"), (wu, dwu, u, du, "dve")):
                    for half in range(2):
                        pO, dpO = self.pO.next()
                        for k in range(8):
                            kb.op("pe", lambda e, k=k, pO=pO, wX=wX, half=half, c=c: e.matmul(
                                pO[:], xT[:, k, c * 128:(c + 1) * 128], wX[:, k, half * 512:(half + 1) * 512],
                                start=(k == 0), stop=(k == 7)), r=[dxT, dwX], w=[dpO])
                        if eng == "act":
                            kb.op("act", lambda e, pO=pO, dstt=dstt, half=half: e.activation(
                                out=dstt[:, half * 512:(half + 1) * 512], in_=pO[:], func=AF.Copy), r=[dpO], w=[ddst])
                        else:
                            kb.op("dve", lambda e, pO=pO, dstt=dstt, half=half: e.tensor_copy(
                                out=dstt[:, half * 512:(half + 1) * 512], in_=pO[:]), r=[dpO], w=[ddst])
                kb.dma("pool", self.vs[rows, :], v[:], r=[dv], w=[self.d_scr["vs"]])
                kb.dma("pool", self.us[rows, :], u[:], r=[du], w=[self.d_scr["us"], d_us_tile[t]])
                kz, dkz = self.k_transpose_scaled(kT, dkT, c, CT_ZB, kz_r)
                kb.op("act", lambda e, cg=cg: e.activation(out=self.bs[:, cg, :], in_=st[:], func=AF.Copy),
                      r=[dst_], w=[self.d_bs[cg]])
                self.kv_state_step(kz, dkz, v, dv, st, dst_)
            with self.nc.allow_non_contiguous_dma(reason="128B runs"):
                for b0 in range(t * 64, (t + 1) * 64, 16):
                    kb.dma("sp", u2a[:, b0:b0 + 16, :], usv[:, b0:b0 + 16, :], r=[d_us_tile[t]], w=[self.d_u2a])

    def alloc_seq_persistent(self):
        kb = self.kb
        kb.barrier()
        kb.release(self.base_mark)
        self.bs = kb.sb("bs", [128, 16, 512], BF16)
        self.d_bs = [Dep() for _ in range(16)]
        self.bstate = kb.sb("bstate", [128, 512], F32)
        self.d_bstate = Dep()
        self.fstate = kb.sb("fstate", [128, 512], F32)
        self.d_fstate = Dep()
        self.fbf = kb.sb("fbf", [128, 512], BF16)
        self.d_fbf = Dep()
        self.base_mark = kb.mark()

    def dbg_copy(self, name, src_ap, dep):
        if name in self.dbg:
            self.kb.dma("sp", self.dbg[name], src_ap, r=[dep])

    def build(self):
        cfg = self.cfg
        only = cfg.get("only")
        self.consts()
        self.cast_gen = self.cast_weights_gen()
        if only in (None, "s5pre", "mixer", "m2"):
            self.s5_prologue()
        for _ in self.cast_gen:
            pass
        self.alloc_seq_persistent()
        for s in range(self.nseq):
            xin = lambda t, s=s: self.x[s, t * T:(t + 1) * T, :]
            yout = lambda t, s=s: self.y[s, t * T:(t + 1) * T, :]
            xa = lambda t: self.xa[t * T:(t + 1) * T, :]
            xb = lambda t: self.xb[t * T:(t + 1) * T, :]
            dy = Dep()
            if only == "ffn1":
                self.ffn(0, 1, xin, self.d_in, yout, dy, 0)
                continue
            if only == "m1":
                self.mixer_a(0, xin, self.d_in)
                continue
            if only == "mixer":
                self.mixer_a(0, xin, self.d_in)
                self.mixer_s5(0)
                self.mixer_b(0, xin, self.d_in, yout, dy)
                continue
            if only == "m2":
                self.mixer_a(0, xin, self.d_in)
                self.mixer_s5(0)
                continue
            if only == "s5pre":
                continue
            A = (xa, self.d_xa)
            Bf = (xb, self.d_xb)
            for l in range(DEPTH):
                src, sdep = (xin, self.d_in) if l == 0 else A
                P, Q = (A, Bf) if l == 0 else (Bf, A)
                last = l == DEPTH - 1
                self.ffn(l, 1, src, sdep, P[0], P[1], 0)
                self.mixer_a(l, P[0], P[1])
                self.mixer_s5(l)
                self.mixer_b(l, P[0], P[1], Q[0], Q[1])
                if last:
                    self.ffn(l, 2, Q[0], Q[1], yout, dy, 2)
                else:
                    self.ffn(l, 2, Q[0], Q[1], A[0], A[1], 2)
        self.kb.barrier()
        self.dbg_copy("dbg_ks", self.ks, self.d_scr["ks"])
        self.dbg_copy("dbg_vs", self.vs, self.d_scr["vs"])
        self.dbg_copy("dbg_us", self.us, self.d_scr["us"])
        self.dbg_copy("dbg_zs", self.zs, self.d_scr["zs"])
        if "dbg_bs" in self.dbg:
            self.kb.dma("sp", self.dbg["dbg_bs"], self.bs[:], r=self.d_bs)
        self.dbg_copy("dbg_s5m", self.s5m[0], self.d_s5m[0])
        self.kb.finish()
        return self.nc


def host_inputs(inp):
    wpack = np.stack([pack_weights(inp, l) for l in range(DEPTH)])
    lntab = np.empty((DEPTH, 3, 2, 128, D), np.float32)
    for l in range(DEPTH):
        for i, nm in enumerate(("ln1", "ln2", "ln3")):
            lntab[l, i, 0] = rep128(inp[f"{nm}_g"][l])
            lntab[l, i, 1] = rep128(inp[f"{nm}_b"][l])
    ct, rope = const_tables()
    bgate = np.stack([np.ascontiguousarray(inp["b_gate"][l].reshape(16, 128).T) for l in range(DEPTH)])
    s5 = [s5_host_layout(inp, l) for l in range(DEPTH)]
    return {"wpack": wpack, "lntab": lntab, "ctab": ct, "rope": rope, "bgate": bgate.astype(np.float32),
            "s5p": np.stack([a for a, _ in s5]), "s5d": np.stack([b for _, b in s5])}

TI_SO_RE = 0
TI_SO_IM = NB
TI_T0 = 2 * NB
TI_TF = TI_T0
TI_TB = TI_T0 + NB - 1
TI_SI_RE = TI_T0 + 1 + 2 * (NB - 1)
TI_SI_IM = TI_SI_RE + NB
GCH = 4


def _s5_prologue(self):
    kb = self.kb
    self.s5a = self.nc.dram_tensor("s5a", [DEPTH, 2, 128, 64], F32, kind="Internal").ap()
    self.d_s5a = [Dep() for _ in range(DEPTH)]
    for l in range(DEPTH):
        self._s5_prologue_layer(l)


def _s5_prologue_layer(self, l):
    kb = self.kb
    self.phase_begin()
    NDG = 64
    prm = kb.sb("prm", [128, S5P_W], F32)
    dprm = Dep()
    kb.dma("sp", prm[:], self.s5p[l], w=[dprm])
    dcol = kb.sb("dcol", [128, 64], F32)
    ddcol = Dep()
    kb.dma("sp", dcol[:], self.s5d[l], w=[ddcol])
    lre = prm[:, 0:64]
    lim = prm[:, 64:128]
    ldt = prm[:, 128:192]
    bre = prm[:, 192:192 + 1024].rearrange("n (a p) -> n a p", p=16)
    bim = prm[:, 192 + 1024:192 + 2048].rearrange("n (a p) -> n a p", p=16)
    cre = prm[:, 192 + 2048:192 + 3072].rearrange("n (a p) -> n a p", p=16)
    cim = prm[:, 192 + 3072:192 + 4096].rearrange("n (a p) -> n a p", p=16)

    W = {}
    dW = {}

    def wt(name, shape=(128, NDG)):
        W[name] = kb.sb("s5w_" + name, list(shape), F32)
        dW[name] = Dep()
        return W[name]

    for nm in ("lr", "dt", "lrdt", "mag", "imag", "th", "c", "s", "t1", "t2", "abr", "abi", "ivr", "ivi", "den",
               "fre", "fim", "nre", "hpi"):
        wt(nm)
    wt("bbr", (128, NDG, 16))
    wt("bbi", (128, NDG, 16))
    wt("b1", (128, NDG, 16))
    wt("b2", (128, NDG, 16))
    PAr = kb.sb("PAr", [128, C1 + 1, NDG], F32)
    PAi = kb.sb("PAi", [128, C1 + 1, NDG], F32)
    PDr = kb.sb("PDr", [128, C1 + 1, NDG], F32)
    PDi = kb.sb("PDi", [128, C1 + 1, NDG], F32)
    Nr = kb.sb("Nr", [128, 8, NDG], F32)
    Ni = kb.sb("Ni", [128, 8, NDG], F32)
    dP = Dep()

    def tt(eng, out, a, b, op, r, w):
        kb.op(eng, lambda e: e.tensor_tensor(out=out, in0=a, in1=b, op=op), r=r, w=w)

    def ts(eng, out, a, s1, op0, r, w):
        kb.op(eng, lambda e: e.tensor_scalar(out=out, in0=a, scalar1=s1, scalar2=None, op0=op0), r=r, w=w)

    def act(out, a, func, r, w, scale=None, bias=None):
        kw = {}
        if scale is not None:
            kw["scale"] = scale
        if bias is not None:
            kw["bias"] = bias
        kb.op("act", lambda e: e.activation(out=out, in_=a, func=func, **kw), r=r, w=w)

    A_ = lambda n: W[n][:]
    d_ = lambda n: dW[n]
    ts("dve", A_("lr"), lre, -1e-4, ALU.min, [dprm], [d_("lr")])
    act(A_("dt"), ldt, AF.Exp, [dprm], [d_("dt")])
    tt("dve", A_("lrdt"), A_("lr"), A_("dt"), ALU.mult, [d_("lr"), d_("dt")], [d_("lrdt")])
    act(A_("mag"), A_("lrdt"), AF.Exp, [d_("lrdt")], [d_("mag")])
    act(A_("imag"), A_("lrdt"), AF.Exp, [d_("lrdt")], [d_("imag")], scale=-1.0)
    tt("dve", A_("th"), lim, A_("dt"), ALU.mult, [dprm, d_("dt")], [d_("th")])
    kb.op("pool", lambda e: e.memset(W["hpi"][:], math.pi / 2), w=[d_("hpi")])
    act(A_("s"), A_("th"), AF.Sin, [d_("th")], [d_("s")], scale=1.0 / 16)
    act(A_("c"), A_("th"), AF.Sin, [d_("th"), d_("hpi")], [d_("c")], scale=1.0 / 16, bias=W["hpi"][:, 0:1])
    for _ in range(4):
        tt("dve", A_("t1"), A_("c"), A_("c"), ALU.mult, [d_("c")], [d_("t1")])
        tt("pool", A_("t2"), A_("s"), A_("s"), ALU.mult, [d_("s")], [d_("t2")])
        tt("dve", A_("t2"), A_("t1"), A_("t2"), ALU.subtract, [d_("t1"), d_("t2")], [d_("t2")])
        tt("pool", A_("t1"), A_("c"), A_("s"), ALU.mult, [d_("c"), d_("s")], [d_("t1")])
        ts("dve", A_("s"), A_("t1"), 2.0, ALU.mult, [d_("t1")], [d_("s")])
        kb.op("act", lambda e: e.activation(out=W["c"][:], in_=W["t2"][:], func=AF.Copy), r=[d_("t2")], w=[d_("c")])
    tt("dve", A_("abr"), A_("mag"), A_("c"), ALU.mult, [d_("mag"), d_("c")], [d_("abr")])
    tt("pool", A_("abi"), A_("mag"), A_("s"), ALU.mult, [d_("mag"), d_("s")], [d_("abi")])
    tt("dve", A_("ivr"), A_("imag"), A_("c"), ALU.mult, [d_("imag"), d_("c")], [d_("ivr")])
    tt("pool", A_("ivi"), A_("imag"), A_("s"), ALU.mult, [d_("imag"), d_("s")], [d_("ivi")])
    ts("dve", A_("ivi"), A_("ivi"), -1.0, ALU.mult, [d_("ivi")], [d_("ivi")])
    tt("dve", A_("t1"), A_("lr"), A_("lr"), ALU.mult, [d_("lr")], [d_("t1")])
    tt("pool", A_("t2"), lim, lim, ALU.mult, [dprm], [d_("t2")])
    tt("dve", A_("den"), A_("t1"), A_("t2"), ALU.add, [d_("t1"), d_("t2")], [d_("den")])
    kb.op("dve", lambda e: e.reciprocal(out=W["den"][:], in_=W["den"][:]), r=[d_("den")], w=[d_("den")])
    ts("dve", A_("nre"), A_("abr"), -1.0, ALU.add, [d_("abr")], [d_("nre")])
    tt("dve", A_("t1"), A_("nre"), A_("lr"), ALU.mult, [d_("nre"), d_("lr")], [d_("t1")])
    tt("pool", A_("t2"), A_("abi"), lim, ALU.mult, [d_("abi"), dprm], [d_("t2")])
    tt("dve", A_("t1"), A_("t1"), A_("t2"), ALU.add, [d_("t1"), d_("t2")], [d_("t1")])
    tt("dve", A_("fre"), A_("t1"), A_("den"), ALU.mult, [d_("t1"), d_("den")], [d_("fre")])
    tt("dve", A_("t1"), A_("abi"), A_("lr"), ALU.mult, [d_("abi"), d_("lr")], [d_("t1")])
    tt("pool", A_("t2"), A_("nre"), lim, ALU.mult, [d_("nre"), dprm], [d_("t2")])
    tt("dve", A_("t1"), A_("t1"), A_("t2"), ALU.subtract, [d_("t1"), d_("t2")], [d_("t1")])
    tt("dve", A_("fim"), A_("t1"), A_("den"), ALU.mult, [d_("t1"), d_("den")], [d_("fim")])
    fre_b = W["fre"][:].unsqueeze(2).broadcast_to([128, NDG, 16])
    fim_b = W["fim"][:].unsqueeze(2).broadcast_to([128, NDG, 16])
    tt("dve", A_("b1"), bre, fre_b, ALU.mult, [dprm, d_("fre")], [d_("b1")])
    tt("pool", A_("b2"), bim, fim_b, ALU.mult, [dprm, d_("fim")], [d_("b2")])
    tt("dve", A_("bbr"), A_("b1"), A_("b2"), ALU.subtract, [d_("b1"), d_("b2")], [d_("bbr")])
    tt("dve", A_("b1"), bim, fre_b, ALU.mult, [dprm, d_("fre")], [d_("b1")])
    tt("pool", A_("b2"), bre, fim_b, ALU.mult, [dprm, d_("fim")], [d_("b2")])
    tt("dve", A_("bbi"), A_("b1"), A_("b2"), ALU.add, [d_("b1"), d_("b2")], [d_("bbi")])

    def powers(Pr, Pi, n, br, bi, dbr, dbi):
        kb.op("pool", lambda e: e.memset(Pr[:, 0, :], 1.0), w=[dP])
        kb.op("pool", lambda e: e.memset(Pi[:, 0, :], 0.0), w=[dP])
        kb.op("act", lambda e: e.activation(out=Pr[:, 1, :], in_=br, func=AF.Copy), r=[dbr], w=[dP])
        kb.op("act", lambda e: e.activation(out=Pi[:, 1, :], in_=bi, func=AF.Copy), r=[dbi], w=[dP])
        for j in range(1, n - 1):
            tt("dve", A_("t1"), Pr[:, j, :], br, ALU.mult, [dP, dbr], [d_("t1")])
            tt("pool", A_("t2"), Pi[:, j, :], bi, ALU.mult, [dP, dbi], [d_("t2")])
            tt("dve", Pr[:, j + 1, :], A_("t1"), A_("t2"), ALU.subtract, [d_("t1"), d_("t2")], [dP])
            tt("dve", A_("t1"), Pr[:, j, :], bi, ALU.mult, [dP, dbi], [d_("t1")])
            tt("pool", A_("t2"), Pi[:, j, :], br, ALU.mult, [dP, dbr], [d_("t2")])
            tt("dve", Pi[:, j + 1, :], A_("t1"), A_("t2"), ALU.add, [d_("t1"), d_("t2")], [dP])

    powers(PAr, PAi, C1 + 1, A_("abr"), A_("abi"), d_("abr"), d_("abi"))
    powers(Nr, Ni, 8, A_("ivr"), A_("ivi"), d_("ivr"), d_("ivi"))
    for j in range(C1 + 1):
        kb.op("act", lambda e, j=j: e.activation(out=PDr[:, j, :], in_=PAr[:, C1 - j, :], func=AF.Copy), r=[dP],
              w=[dP])
        kb.op("pool", lambda e, j=j: e.tensor_copy(out=PDi[:, j, :], in_=PAi[:, C1 - j, :]), r=[dP], w=[dP])
    for ri, PX in enumerate((PAr, PAi)):
        for gh in range(2):
            for dr in range(2):
                kb.dma("sp", self.s5a[l, ri, dr * 64:(dr + 1) * 64, gh * 32:(gh + 1) * 32],
                       PX[gh * 64:(gh + 1) * 64, C1, dr * 32:(dr + 1) * 32], r=[dP], w=[self.d_s5a[l]])

    G = GCH
    pl_r = Rot(kb, "pl", 8, [128, G, 8, 16], F32)
    pk_r = Rot(kb, "plk", 4, [128, G, 8, 16], F32)
    tm_r = Rot(kb, "ptm", 8, [128, G, 8, 16], F32)
    stg = [kb.sb(f"stg{gh}", [128, G, TI_SI_RE, 128], BF16) for gh in range(2)]
    dstg = [Dep(), Dep()]
    sib_r = Rot(kb, "sib", 4, [128, G, 128], BF16)
    t0_r = [Rot(kb, f"t0t{gh}", 2, [128, 128], F32) for gh in range(2)]
    pX = [self.pT, self.pO]

    pending = []

    def flush_pending():
        while pending:
            pending.pop(0)()

    def outer(Mr, Mi, dM, Pr, Pi, j0, dg0, neg_im=False, keep=False):
        next(self.cast_gen, None)
        mr = Mr[:, dg0:dg0 + G, :].unsqueeze(2).broadcast_to([128, G, 8, 16])
        mi = Mi[:, dg0:dg0 + G, :].unsqueeze(2).broadcast_to([128, G, 8, 16])
        pr = Pr[:, j0:j0 + 8, dg0:dg0 + G].rearrange("n j g -> n g j").unsqueeze(3).broadcast_to([128, G, 8, 16])
        pi = Pi[:, j0:j0 + 8, dg0:dg0 + G].rearrange("n j g -> n g j").unsqueeze(3).broadcast_to([128, G, 8, 16])
        re, dre = (pk_r if keep else pl_r).next()
        im, dim = (pk_r if keep else pl_r).next()
        a, da = tm_r.next()
        b, db = tm_r.next()
        a2, da2 = tm_r.next()
        b2, db2 = tm_r.next()
        tt("dve", a[:], mr, pr, ALU.mult, dM + [dP], [da])
        tt("pool", b[:], mi, pi, ALU.mult, dM + [dP], [db])
        tt("pool", a2[:], mr, pi, ALU.mult, dM + [dP], [da2])
        tt("dve", b2[:], mi, pr, ALU.mult, dM + [dP], [db2])
        flush_pending()

        def fin():
            tt("dve", re[:], a[:], b[:], ALU.subtract, [da, db], [dre])
            if neg_im:
                kb.op("dve", lambda e: e.scalar_tensor_tensor(out=im[:], in0=a2[:], scalar=-1.0, in1=b2[:],
                                                             op0=ALU.mult, op1=ALU.subtract), r=[da2, db2], w=[dim])
            else:
                tt("dve", im[:], a2[:], b2[:], ALU.add, [da2, db2], [dim])

        pending.append(fin)
        return re, dre, im, dim

    BB = (W["bbr"], W["bbi"], [dW["bbr"], dW["bbi"]])
    CC = (cre, cim, [dprm])

    def fl(p, gh, gi):
        return p[gh * 64:(gh + 1) * 64, gi].rearrange("n s c -> n (s c)")

    def toep(K, Q, gh, gi):
        flush_pending()
        kre, dkre, kim, dkim = K
        qre, dqre, qim, dqim = Q
        pT, dpT = pX[gh].next()
        kb.op("pe", lambda e: e.matmul(pT[:, 0:128], fl(kre, gh, gi), fl(qre, gh, gi), start=True, stop=False),
              r=[dkre, dqre], w=[dpT])
        kb.op("pe", lambda e: e.matmul(pT[:, 0:128], fl(kim, gh, gi), fl(qim, gh, gi), start=False, stop=True),
              r=[dkim, dqim], w=[dpT])
        return pT, dpT

    castgen = self.cast_gen
    for g0 in range(0, 32, G):
        f0, b0 = g0, 32 + g0
        K0f = outer(*BB[:2], BB[2], Nr, Ni, 0, f0)
        Q0f = outer(*CC[:2], CC[2], PAr, PAi, 0, f0, neg_im=True)
        K0b = outer(*BB[:2], BB[2], PAr, PAi, 0, b0)
        Q0b = outer(*CC[:2], CC[2], Nr, Ni, 0, b0, neg_im=True)
        for gi in range(G):
            for gh in range(2):
                g = gh * 32 + g0 + gi
                pT, dpT = toep(K0f, Q0f, gh, gi)
                t0, dt0 = t0_r[gh].next()
                tt("dve", t0[:], pT[:, 0:128], self.ctv(CT_MF, 128), ALU.mult, [dpT, self.d_const], [dt0])
                pT2, dpT2 = toep(K0b, Q0b, gh, gi)
                t1_, dt1_ = t0_r[gh].next()
                tt("dve", t1_[:], pT2[:, 0:128], self.ctv(CT_MB, 128), ALU.mult, [dpT2, self.d_const], [dt1_])
                tt("pool", t0[:], t0[:], t1_[:], ALU.add, [dt0, dt1_], [dt0])
                kb.op("dve", lambda e, gi=gi, gh=gh, g=g, t0=t0: e.scalar_tensor_tensor(
                    out=stg[gh][:, gi, TI_T0, :], in0=self.idf[:], scalar=dcol[:, g:g + 1], in1=t0[:],
                    op0=ALU.mult, op1=ALU.add), r=[dt0, ddcol, self.d_const], w=[dstg[gh]])
        def so_tiles(X, dr, J):
            flush_pending()
            for ri in range(2):
                pl, dpl = X[2 * ri], X[2 * ri + 1]
                for gi in range(G):
                    for gh in range(2):
                        pT, dpT = pX[gh].next()
                        kb.op("pe", lambda e, pl=pl, gi=gi, gh=gh, pT=pT: e.transpose(
                            pT[:, 0:64], fl(pl, gh, gi), self.idf[gh * 64:(gh + 1) * 64, gh * 64:(gh + 1) * 64]),
                              r=[dpl, self.d_const], w=[dpT])
                        ti = (TI_SO_RE if ri == 0 else TI_SO_IM) + J
                        if gi % 2 == 0:
                            kb.op("act", lambda e, gi=gi, gh=gh, pT=pT, ti=ti, dr=dr: e.activation(
                                out=stg[gh][:, gi, ti, dr * 64:(dr + 1) * 64], in_=pT[:, 0:64], func=AF.Copy),
                                  r=[dpT], w=[dstg[gh]])
                        else:
                            kb.op("dve", lambda e, gi=gi, gh=gh, pT=pT, ti=ti, dr=dr: e.tensor_copy(
                                out=stg[gh][:, gi, ti, dr * 64:(dr + 1) * 64], in_=pT[:, 0:64]), r=[dpT],
                                  w=[dstg[gh]])

        def si_tiles(Y, dr, I):
            flush_pending()
            for ri in range(2):
                pl, dpl = Y[2 * ri], Y[2 * ri + 1]
                sb_, dsb = sib_r.next()
                kb.op("act", lambda e, pl=pl, sb_=sb_: e.activation(
                    out=sb_[:], in_=pl[:].rearrange("n g s c -> n g (s c)"), func=AF.Copy), r=[dpl], w=[dsb])
                ti = (TI_SI_RE if ri == 0 else TI_SI_IM) + I
                for gh in range(2):
                    ga = gh * 32 + g0
                    kb.dma("sp", self.s5m[l, ga:ga + G, dr * 64:(dr + 1) * 64, ti, :].rearrange("g p c -> p g c"),
                           sb_[gh * 64:(gh + 1) * 64], r=[dsb], w=[self.d_s5m[l]])

        so_tiles(K0b, 1, 0)
        KAf = outer(*BB[:2], BB[2], PDr, PDi, C1 - 7, f0, keep=True)
        QAb = outer(*CC[:2], CC[2], PDr, PDi, C1 - 7, b0, neg_im=True, keep=True)
        for Dd in range(1, NB):
            QD = outer(*CC[:2], CC[2], PAr, PAi, 8 * (Dd - 1) + 1, f0, neg_im=True)
            KD_ = outer(*BB[:2], BB[2], PAr, PAi, 8 * (Dd - 1) + 1, b0)
            for gi in range(G):
                for gh in range(2):
                    pT, dpT = toep(KAf, QD, gh, gi)
                    kb.op("act", lambda e, gi=gi, gh=gh, pT=pT, Dd=Dd: e.activation(
                        out=stg[gh][:, gi, TI_TF + Dd, :], in_=pT[:, 0:128], func=AF.Copy), r=[dpT], w=[dstg[gh]])
                    pT2, dpT2 = toep(KD_, QAb, gh, gi)
                    kb.op("act", lambda e, gi=gi, gh=gh, pT2=pT2, Dd=Dd: e.activation(
                        out=stg[gh][:, gi, TI_TB + Dd, :], in_=pT2[:, 0:128], func=AF.Copy), r=[dpT2], w=[dstg[gh]])
            si_tiles(QD, 0, Dd - 1)
        for J in range(NB):
            if J < NB - 1:
                Xf = outer(*BB[:2], BB[2], PDr, PDi, 1 + 8 * J, f0)
            else:
                Xf = KAf
            so_tiles(Xf, 0, J)
            if J >= 1:
                Xb = outer(*BB[:2], BB[2], PAr, PAi, 8 * J, b0)
                so_tiles(Xb, 1, J)
        for gh in range(2):
            ga = gh * 32 + g0
            kb.dma("sp", self.s5m[l, ga:ga + G, :, 0:TI_SI_RE, :].rearrange("g p t c -> p g t c"), stg[gh][:],
                   r=[dstg[gh]], w=[self.d_s5m[l]])
        Yf = outer(*CC[:2], CC[2], PAr, PAi, 8 * (NB - 1) + 1, f0, neg_im=True)
        si_tiles(Yf, 0, NB - 1)
        for I in range(NB):
            Yb = outer(*CC[:2], CC[2], PDr, PDi, 8 * I, b0, neg_im=True)
            si_tiles(Yb, 1, I)


def _mixer_s5(self, l):
    kb = self.kb
    self.phase_begin()
    m0 = kb.mark()
    u2a = kb.sb("u2a", [128, NBLK, 64], BF16)
    m1 = kb.mark()
    kb.release(m0)
    Hb = kb.sb("Hb", [128, NK, 64, 2], F32)
    kb.release(m1)
    d_u2a = self.d_u2a
    u2 = kb.sb("u2", [128, 64, NB, NK], BF16)
    d_u2 = [Dep() for _ in range(64)]
    Sb = kb.sb("Sb", [128, NK, 64, 2], F32)
    Hbf = kb.sb("Hbf", [128, 64, 2, NK], BF16)
    d_Sall = Dep()
    d_Hf, d_Hb2, d_Hbf = Dep(), Dep(), Dep()
    Ar = kb.sb("Ar", [128, 64], F32)
    Ai = kb.sb("Ai", [128, 64], F32)
    dA = Dep()
    G1, G2 = 8, 2
    t1_r = Rot(kb, "s5t1", 2, [128, G1, NT1, 128], BF16)
    t2_r = Rot(kb, "s5t2", 3, [128, G2, NT2, 128], BF16)
    tmpf = [kb.sb(f"scf{i}", [128, 64, 2], F32) for i in range(2)]
    tmpb = [kb.sb(f"scb{i}", [128, 64, 2], F32) for i in range(2)]
    dtf = [Dep(), Dep()]
    dtb = [Dep(), Dep()]
    kb.dma("sp", Ar[:], self.s5a[l, 0], r=[self.d_s5a[l]], w=[dA])
    kb.dma("sp", Ai[:], self.s5a[l, 1], r=[self.d_s5a[l]], w=[dA])
    nc = self.nc

    def load_t1(i):
        t, dt_ = t1_r.next()
        kb.dma("sp", t[:], self.s5m[l, i * G1:(i + 1) * G1, :, 0:NT1, :].rearrange("g p t c -> p g t c"),
               r=[self.d_s5m[l]], w=[dt_])
        return t, dt_

    def load_t2(i):
        t, dt_ = t2_r.next()
        kb.dma("sp", t[:], self.s5m[l, i * G2:(i + 1) * G2, :, NT1:NT1 + NT2, :].rearrange("g p t c -> p g t c"),
               r=[self.d_s5m[l]], w=[dt_])
        return t, dt_

    nx1 = [load_t1(0)]
    u2a_v = u2a[:].rearrange("p (k j) g -> p g j k", j=NB)

    for i in range(64 // G1):
        tl, dtl = nx1[i]
        if i + 1 < 64 // G1:
            nx1.append(load_t1(i + 1))
        for gi in range(G1):
            g = i * G1 + gi
            pGa, dpGa = self.pA.next() if g % 2 == 0 else self.pU.next()
            kb.op("pe", lambda e, g=g, pGa=pGa: e.matmul(pGa[:, 0:NBLK].rearrange("p (j k) -> p j k", k=NK),
                                                        self.idb[:], u2a_v[:, g], start=True, stop=True),
                  r=[d_u2a, self.d_const], w=[dpGa])
            if g % 2 == 0:
                kb.op("act", lambda e, g=g, pGa=pGa: e.activation(out=u2[:, g].rearrange("p j k -> p (j k)"),
                                                                  in_=pGa[:, 0:NBLK], func=AF.Copy), r=[dpGa],
                      w=[d_u2[g]])
            else:
                kb.op("dve", lambda e, g=g, pGa=pGa: e.tensor_copy(out=u2[:, g].rearrange("p j k -> p (j k)"),
                                                                   in_=pGa[:, 0:NBLK]), r=[dpGa], w=[d_u2[g]])
            pS, dpS = self.pO.next()
            for ri in range(2):
                for hf in range(2):
                    for J in range(NB):
                        ti = (TI_SO_RE if ri == 0 else TI_SO_IM) + J
                        rhs = u2[:, g, J, :] if hf == 0 else u2[:, g, J, ::-1]
                        kb.op("pe", lambda e, ri=ri, J=J, ti=ti, gi=gi, pS=pS, tl=tl, hf=hf, rhs=rhs: e.matmul(
                            pS[hf * 64:(hf + 1) * 64, ri * NK:(ri + 1) * NK], tl[:, gi, ti, hf * 64:(hf + 1) * 64], rhs,
                            start=(J == 0), stop=(J == NB - 1)), r=[dtl, d_u2[g]], w=[dpS])
            src_ = pS[:, 0:2 * NK].rearrange("p (r k) -> p k r", k=NK)
            if g % 2 == 1:
                kb.op("act", lambda e, g=g, src_=src_: e.activation(out=Sb[:, :, g, :], in_=src_, func=AF.Copy),
                      r=[dpS], w=[d_Sall])
            else:
                kb.op("dve", lambda e, g=g, src_=src_: e.tensor_copy(out=Sb[:, :, g, :], in_=src_), r=[dpS],
                      w=[d_Sall])
    nx2 = [load_t2(0), load_t2(1)]

    dHs = [d_Hf, d_Hb2]
    GH = [slice(0, 32), slice(32, 64)]
    tmps = [tmpf, tmpb]
    dtmps = [dtf, dtb]
    kb.op("pool", lambda e: e.memset(Hb[:, 0], 0.0), w=[d_Hf, d_Hb2, d_u2a])
    pSc, dpSc = self.pT.t[0], self.pT.d[0]
    ArP = pSc[:, 0:64]
    AiP = pSc[:, 64:128]
    T13P = [pSc[:, 128:256].rearrange("p (g r) -> p g r", r=2), pSc[:, 256:384].rearrange("p (g r) -> p g r", r=2)]
    kb.op("act", lambda e: e.activation(out=ArP, in_=Ar[:], func=AF.Copy), r=[dA], w=[dpSc])
    kb.op("act", lambda e: e.activation(out=AiP, in_=Ai[:], func=AF.Copy), r=[dA], w=[dpSc])
    dT13P = [Dep(), Dep()]
    for k in range(NK - 1):
        ops = [[], []]
        for ci in range(2):
            gs_ = GH[ci]
            T13, T24 = T13P[ci], tmps[ci][1]
            dtmp = [dT13P[ci], dtmps[ci][1]]
            dH = dHs[ci]
            Arb = ArP[:, gs_].unsqueeze(2).broadcast_to([128, 32, 2])
            Aib = AiP[:, gs_].unsqueeze(2).broadcast_to([128, 32, 2])
            ops[ci] = [
                (lambda e, k=k, T13=T13, Arb=Arb, gs_=gs_: e.tensor_tensor(out=T13[:, gs_], in0=Hb[:, k, gs_], in1=Arb,
                                                                           op=ALU.mult), [dH, dpSc], [dtmp[0]]),
                (lambda e, k=k, T24=T24, Aib=Aib, gs_=gs_: e.tensor_tensor(out=T24[:, gs_], in0=Hb[:, k, gs_], in1=Aib,
                                                                           op=ALU.mult), [dH, dpSc], [dtmp[1]]),
                (lambda e, k=k, T13=T13, gs_=gs_: e.tensor_tensor(out=T13[:, gs_], in0=T13[:, gs_], in1=Sb[:, k, gs_],
                                                                  op=ALU.add), [dtmp[0], d_Sall], [dtmp[0]]),
                (lambda e, k=k, T13=T13, T24=T24, gs_=gs_: e.tensor_tensor(
                    out=Hb[:, k + 1, gs_, 0], in0=T13[:, gs_, 0], in1=T24[:, gs_, 1], op=ALU.subtract),
                 [dtmp[0], dtmp[1]], [dH, d_u2a]),
                (lambda e, k=k, T13=T13, T24=T24, gs_=gs_: e.tensor_tensor(
                    out=Hb[:, k + 1, gs_, 1], in0=T13[:, gs_, 1], in1=T24[:, gs_, 0], op=ALU.add),
                 [dtmp[0], dtmp[1]], [dH, d_u2a]),
            ]
        for j in range(5):
            for ci in range(2):
                fn, r_, w_ = ops[ci][j]
                kb.op("dve", fn, r=r_, w=w_)
    kb.op("act", lambda e: e.activation(out=Hbf[0:64], in_=Hb[0:64].rearrange("p k g r -> p g r k"), func=AF.Copy),
          r=[d_Hf, d_Hb2, d_u2a], w=[d_Hbf])
    kb.op("pool", lambda e: e.tensor_copy(out=Hbf[64:128], in_=Hb[64:128, ::-1].rearrange("p k g r -> p g r k")),
          r=[d_Hf, d_Hb2, d_u2a], w=[d_Hbf])

    for i in range(64 // G2):
        tl, dtl = nx2[i]
        if i + 2 < 64 // G2:
            nx2.append(load_t2(i + 2))
        for gi in range(G2):
            g = i * G2 + gi
            pY, dpY = self.pA.next() if g % 2 == 0 else self.pU.next()
            pYv = pY[:, 0:NBLK].rearrange("p (i k) -> p i k", k=NK)
            o = NT1
            mms = [(TI_T0 - o, pYv, u2[:, g, :, :])]
            for Dd in range(1, NB):
                mms.append((TI_TF + Dd - o, pYv[:, Dd:NB, :], u2[:, g, 0:NB - Dd, :]))
                mms.append((TI_TB + Dd - o, pYv[:, 0:NB - Dd, :], u2[:, g, Dd:NB, :]))
            for I in range(NB):
                mms.append((TI_SI_RE + I - o, pYv[:, I, :], Hbf[:, g, 0, :]))
                mms.append((TI_SI_IM + I - o, pYv[:, I, :], Hbf[:, g, 1, :]))
            for n_, (ti, oap, rap) in enumerate(mms):
                kb.op("pe", lambda e, ti=ti, oap=oap, rap=rap, n_=n_, tl=tl, gi=gi: e.matmul(
                    oap, tl[:, gi, ti, :], rap, start=(n_ == 0), stop=(n_ == len(mms) - 1)),
                      r=[dtl, d_u2[g], d_Hbf], w=[dpY])
            kb.op("act", lambda e, g=g, pYv=pYv: e.activation(out=u2a_v[:, g], in_=pYv, func=AF.Gelu_apprx_tanh),
                  r=[dpY, d_Hbf], w=[d_u2a])
    zsv = self.zs.rearrange("(b s) (p g) -> (s p) b g", s=8, p=16)
    self.zstore_evs = []
    with nc.allow_non_contiguous_dma(reason="128B runs"):
        for b0 in range(0, NBLK, 16):
            ev = kb.dma("sp", zsv[:, b0:b0 + 16, :], u2a[:, b0:b0 + 16, :], r=[d_u2a], w=[self.d_scr["zs"]])
            self.zstore_evs.append(ev)


Prog.s5_prologue = _s5_prologue
Prog._s5_prologue_layer = _s5_prologue_layer
Prog.mixer_s5 = _mixer_s5


def _mixer_b(self, l, src, src_dep, dst, dst_dep):
    kb = self.kb
    zev = list(getattr(self, "zstore_evs", []))
    self.phase_begin(dma=False)
    sbt = kb.sb
    z_r = Rot(kb, "zb", 2, [128, D], BF16)
    zT = sbt("zT", [128, 8, T], BF16)
    d_zT = Dep()
    gatedT = sbt("gatedT", [128, 8, T], BF16)
    d_gatedT = Dep()
    m1T = sbt("m1T", [128, 8, T], BF16)
    on = sbt("on", [128, D], F32)
    d_on = Dep()
    assert kb.sb_off - self.base_mark >= NBLK * 64 * 2
    first = {"zt": True, "zT": True, "gatedT": True, "m1T": True, "on": True}

    def xtra(name):
        if first.get(name):
            first[name] = False
            return zev
        return ()

    xT_r = Rot(kb, "xTb", 2, [128, 8, T], BF16)
    wslot = [sbt(f"wsl{i}", [128, 8, 512], BF16) for i in range(4)]
    dslot = [Dep() for _ in range(4)]
    rope_r = Rot(kb, "ropeb", 1, [128, 2, T], F32)
    kT_r = Rot(kb, "kTb", 1, [128, 4, T], BF16)
    qT = sbt("qT", [128, 4, T], BF16)
    qxf = sbt("qxf", [128, 4, T], BF16)
    qxb = sbt("qxb", [128, 4, T], BF16)
    d_qT, d_qxf, d_qxb = Dep(), Dep(), Dep()
    tmp_r = Rot(kb, "rtmpb", 3, [128, T], F32)
    v_r = Rot(kb, "vb", 2, [128, D], BF16)
    xres_r = Rot(kb, "xres", 2, [128, D], F32)
    ST_r = Rot(kb, "ST", 2, [128, 1024], BF16)
    kz_r = Rot(kb, "kzb", 2, [128, 512], BF16)
    sg_r = Rot(kb, "sg", 2, [128, D], F32)
    gated_r = Rot(kb, "gated", 2, [128, D], BF16)
    d_m1 = [Dep() for _ in range(8)]
    gr_r = Rot(kb, "gr", 2, [128, T], F32)
    s2_r = Rot(kb, "s2", 2, [128, T], F32)
    gs_r = Rot(kb, "gs", 2, [128, T], F32)
    bg = sbt("bg", [128, 16], F32)
    d_bg = Dep()
    gst_r = Rot(kb, "gst", 3, [128, 80], F32)
    self.alloc_ln(l, 1, nrr=1)
    kb.dma("sp", bg[:], self.bgate[l], w=[d_bg])
    F_, dF = self.fstate, self.d_fstate
    fbf, dfbf = self.fbf, self.d_fbf
    kb.op("pool", lambda e: e.memset(F_[:], 0.0), w=[dF])
    kb.op("pool", lambda e: e.memset(fbf[:], 0.0), w=[dfbf])
    wbf = self.wbf[l]

    def LW(name, h, slot):
        o = W_OFF[name]
        srcap = wbf[:, o:o + 8192].rearrange("p (k c) -> p k c", c=1024)[:, :, 512 * h:512 * h + 512]
        kb.dma("sp", wslot[slot][:], srcap, r=[self.d_wbf[l]], w=[dslot[slot]])

    def evac_T(pTb, dpT, out_ap, out_dep, eng, extra=()):
        src_ = pTb[:, 0:512].rearrange("p (a b) -> p a b", b=128)
        if eng == "act":
            kb.op("act", lambda e: e.activation(out=out_ap, in_=src_, func=AF.Copy), r=[dpT], w=[out_dep], extra=extra)
        else:
            kb.op("dve", lambda e: e.tensor_copy(out=out_ap, in_=src_), r=[dpT], w=[out_dep], extra=extra)

    NTL = L_SEQ // T

    def load_xT(t):
        xT, dxT = xT_r.next()
        kb.dma("sp", xT[:], self.xts[:, :, t * T:(t + 1) * T], r=[self.d_scr["xts"]], w=[dxT])
        return xT, dxT

    def load_rope(t):
        rp, drp = rope_r.next()
        kb.dma("sp", rp[:], self.rope[:, :, t * T:(t + 1) * T].rearrange("a p t -> p a t"), w=[drp])
        return rp, drp

    def load_kT(t):
        kT, dkT = kT_r.next()
        kb.dma("sp", kT[:], self.ks[:, :, t * T:(t + 1) * T], r=[self.d_scr["ks"]], w=[dkT])
        return kT, dkT

    LW("wq", 0, 0)
    LW("wq", 1, 1)
    nx_xT = load_xT(0)
    nx_rp = load_rope(0)
    nx_kT = load_kT(0)
    for t in range(NTL):
        LW("wg", 0, 2)
        LW("wg", 1, 3)
        xT, dxT = nx_xT
        rp, drp = nx_rp
        kT, dkT = nx_kT
        if t + 1 < NTL:
            nx_xT = load_xT(t + 1)
        for f in range(4):
            pA, dpA = self.pA.next()
            pU, dpU = self.pU.next()
            for k in range(8):
                kb.op("pe", lambda e, k=k, f=f, pA=pA: e.matmul(pA[:], wslot[0][:, k, f * 128:(f + 1) * 128], xT[:, k, :],
                                                              start=(k == 0), stop=(k == 7)), r=[dslot[0], dxT], w=[dpA])
            for k in range(8):
                kb.op("pe", lambda e, k=k, f=f, pU=pU: e.matmul(pU[:], wslot[1][:, k, f * 128:(f + 1) * 128], xT[:, k, :],
                                                              start=(k == 0), stop=(k == 7)), r=[dslot[1], dxT], w=[dpU])

            def mk_also(f):
                def fx(t1, dt1):
                    t1v = t1[:].rearrange("p (c i) -> p c i", i=128)
                    xfv = self.ctv(CT_XF + f * 128, 128).unsqueeze(1).broadcast_to([128, 4, 128])
                    xbv = self.ctv(CT_XB + f * 128, 128).unsqueeze(1).broadcast_to([128, 4, 128])
                    kb.op("dve", lambda e: e.tensor_tensor(out=qxf[:, f, :].rearrange("p (c i) -> p c i", i=128),
                                                          in0=t1v, in1=xfv, op=ALU.mult), r=[dt1, self.d_const],
                          w=[d_qxf])
                    kb.op("pool", lambda e: e.tensor_tensor(out=qxb[:, f, :].rearrange("p (c i) -> p c i", i=128),
                                                           in0=t1v, in1=xbv, op=ALU.mult), r=[dt1, self.d_const],
                          w=[d_qxb])
                return [fx]

            self.rotary(pA, dpA, pU, dpU, rp[:, 0, :], rp[:, 1, :], drp, tmp_r, qT[:, f, :], d_qT, also=mk_also(f))
        LW("glu_v", 0, 0)
        LW("glu_g", 0, 1)
        if t + 1 < NTL:
            nx_rp = load_rope(t + 1)
        st_c = {}

        def stage_A(c):
            cg = 4 * t + c
            rows = slice(cg * 128, (cg + 1) * 128)
            cs = slice(c * 128, (c + 1) * 128)
            v, dv = v_r.next()
            kb.dma("sp", v[:], self.vs[rows, :], r=[self.d_scr["vs"]], w=[dv])
            zt, dzt = z_r.next()
            kb.dma("sp", zt[:], self.zs[rows, :], r=[self.d_scr["zs"]], w=[dzt])
            ST, dST = ST_r.next()
            decv = self.ctv(CT_DEC, 1024).rearrange("p (hp h2 i) -> p hp h2 i", h2=2, i=128)
            pSs = [(self.pO.t[0], self.pO.d[0]), (self.pO.t[1], self.pO.d[1])]
            for hp in range(4):
                for h2 in range(2):
                    pS, dpS = pSs[h2]
                    b = h2 * 64
                    kb.op("pe", lambda e, b=b, hp=hp, pS=pS: e.matmul(
                        pS[:, hp * 128:(hp + 1) * 128], kT[b:b + 64, hp, cs], qT[b:b + 64, hp, cs], start=True,
                        stop=True), r=[dkT, d_qT], w=[dpS])
            for h2 in range(2):
                pS, dpS = pSs[h2]
                kb.op("dve", lambda e, h2=h2, pS=pS: e.tensor_tensor(
                    out=ST[:, h2 * 512:(h2 + 1) * 512].rearrange("p (a i) -> p a i", i=128),
                    in0=pS[:].rearrange("p (a i) -> p a i", i=128), in1=decv[:, :, h2, :], op=ALU.mult),
                      r=[dpS, self.d_const], w=[dST])
            kz, dkz = self.k_transpose_scaled(kT, dkT, c, CT_ZF, kz_r)
            sg, d_sg = sg_r.next()
            for half in range(2):
                pG, dpG = (self.pO.t[half], self.pO.d[half])
                for k in range(8):
                    kb.op("pe", lambda e, k=k, pG=pG, half=half: e.matmul(pG[:], xT[:, k, cs], wslot[2 + half][:, k, :],
                                                                         start=(k == 0), stop=(k == 7)),
                          r=[dxT, dslot[2 + half]], w=[dpG])
                kb.op("act", lambda e, pG=pG, half=half, sg=sg: e.activation(out=sg[:, half * 512:(half + 1) * 512],
                                                                             in_=pG[:], func=AF.Silu), r=[dpG], w=[d_sg])
            for grp in range(2):
                pT, dpT = self.pT.next()
                pTb = pT[:].bitcast(BF16)
                for kk in range(4):
                    kf = 4 * grp + kk
                    kb.op("pe", lambda e, kk=kk, kf=kf, pTb=pTb, zt=zt: e.transpose(
                        pTb[:, kk * 128:(kk + 1) * 128], zt[:, kf * 128:(kf + 1) * 128], self.idb[:]),
                          r=[dzt, self.d_const], w=[dpT])
                evac_T(pTb, dpT, zT[:, 4 * grp:4 * grp + 4, cs], d_zT, "act")
            st_c[c] = dict(v=v, dv=dv, ST=ST, dST=dST, kz=kz, dkz=dkz, sg=sg, d_sg=d_sg)

        def stage_B_pe(c):
            cg = 4 * t + c
            cs = slice(c * 128, (c + 1) * 128)
            S = st_c[c]
            v, dv, ST, dST = S["v"], S["dv"], S["ST"], S["dST"]
            pOs = [(self.pA.t[c % 2], self.pA.d[c % 2]), (self.pU.t[c % 2], self.pU.d[c % 2])]
            for hp in range(4):
                for h2 in range(2):
                    pOo, dpOo = pOs[h2]
                    h = 2 * hp + h2
                    b = h2 * 64
                    oap = pOo[:, hp * 128:(hp + 1) * 128]
                    kb.op("pe", lambda e, oap=oap, h2=h2, hp=hp, h=h: e.matmul(
                        oap, ST[:, h2 * 512 + hp * 128:h2 * 512 + (hp + 1) * 128], v[:, h * 128:(h + 1) * 128],
                        start=True, stop=False), r=[dST, dv], w=[dpOo])
                    kb.op("pe", lambda e, oap=oap, b=b, hp=hp: e.matmul(oap, qxf[b:b + 64, hp, cs],
                                                                       fbf[b:b + 64, hp * 128:(hp + 1) * 128],
                                                                       start=False, stop=False),
                          r=[d_qxf, dfbf], w=[dpOo])
                    kb.op("pe", lambda e, oap=oap, b=b, hp=hp, cg=cg: e.matmul(
                        oap, qxb[b:b + 64, hp, cs], self.bs[b:b + 64, cg, hp * 128:(hp + 1) * 128], start=False,
                        stop=True), r=[d_qxb, self.d_bs[cg]], w=[dpOo])
            S["pOs"] = pOs
            self.kv_state_step(S["kz"], S["dkz"], v, dv, F_, dF)
            kb.op("act", lambda e: e.activation(out=fbf[:], in_=F_[:], func=AF.Copy), r=[dF], w=[dfbf])

        def stage_B_post1(c):
            S = st_c[c]
            pOs = S["pOs"]
            gst, dgst = gst_r.next()
            dsth = [Dep() for _ in range(8)]
            dagh = [Dep() for _ in range(8)]
            for hg in range(2):
                pOo, dpOo = pOs[hg]
                for hh in range(4):
                    h = 2 * hh + hg
                    kb.op("dve", lambda e, h=h, hh=hh, pOo=pOo: e.bn_stats(out=gst[:, 6 * h:6 * h + 6],
                                                                           in_=pOo[:, hh * 128:(hh + 1) * 128]),
                          r=[dpOo], w=[dsth[h], dgst])
            for h in range(8):
                kb.op("dve", lambda e, h=h: e.bn_aggr(out=gst[:, 48 + 2 * h:50 + 2 * h],
                                                      in_=gst[:, 6 * h:6 * h + 6]), r=[dsth[h]], w=[dagh[h]])
            varv = gst[:, 48:64].rearrange("p (h t) -> p h t", t=2)[:, :, 1]
            drs = Dep()
            kb.op("dve", lambda e: e.tensor_scalar(out=gst[:, 64:72], in0=varv, scalar1=float(EPS), scalar2=None,
                                                  op0=ALU.add), r=dagh, w=[drs])
            kb.op("act", lambda e: e.activation(out=gst[:, 64:72], in_=gst[:, 64:72], func=AF.Sqrt), r=[drs],
                  w=[drs])
            S.update(gst=gst, dgst=dgst, dagh=dagh, drs=drs)

        def stage_B_post2(c):
            S = st_c[c]
            pOs, gst, dgst, dagh, drs = S["pOs"], S["gst"], S["dgst"], S["dagh"], S["drs"]
            kb.op("dve", lambda e: e.reciprocal(out=gst[:, 72:80], in_=gst[:, 64:72]), r=[drs], w=[drs])
            for hg in range(2):
                pOo, dpOo = pOs[hg]
                for hh in range(4):
                    h = 2 * hh + hg
                    kb.op("dve", lambda e, h=h, hh=hh, pOo=pOo: e.tensor_scalar(
                        out=on[:, h * 128:(h + 1) * 128], in0=pOo[:, hh * 128:(hh + 1) * 128],
                        scalar1=gst[:, 48 + 2 * h:49 + 2 * h], scalar2=gst[:, 72 + h:73 + h], op0=ALU.subtract,
                        op1=ALU.mult), r=[dpOo, drs, dagh[h]], w=[d_on, dgst], extra=xtra("on"))
            gated, dgated = gated_r.next()
            kb.op("pool", lambda e: e.tensor_tensor(out=gated[:], in0=S["sg"][:], in1=on[:], op=ALU.mult),
                  r=[S["d_sg"], d_on], w=[dgated])
            S["gated"], S["dgated"] = gated, dgated

        def stage_C(c):
            cs = slice(c * 128, (c + 1) * 128)
            S = st_c[c]
            gated, dgated = S["gated"], S["dgated"]
            for grp in range(2):
                pT, dpT = self.pT.next()
                pTb = pT[:].bitcast(BF16)
                for kk in range(4):
                    kf = 4 * grp + kk
                    kb.op("pe", lambda e, kk=kk, kf=kf, pTb=pTb: e.transpose(
                        pTb[:, kk * 128:(kk + 1) * 128], gated[:, kf * 128:(kf + 1) * 128], self.idb[:]),
                          r=[dgated, self.d_const], w=[dpT])
                evac_T(pTb, dpT, gatedT[:, 4 * grp:4 * grp + 4, cs], d_gatedT, "dve", extra=xtra("gatedT"))

        stage_A(0)
        for c in range(4):
            stage_B_pe(c)
            if c >= 1:
                stage_B_post2(c - 1)
            if c + 1 < 4:
                stage_A(c + 1)
            stage_B_post1(c)
            if c >= 1:
                stage_C(c - 1)
        stage_B_post2(3)
        LW("wgs", 0, 2)
        LW("glu_v", 1, 3)
        if t + 1 < NTL:
            nx_kT = load_kT(t + 1)
        xrs = []
        for c in range(4):
            xr, dxr = xres_r.next()
            xrs.append((xr, dxr))
        for c in range(2):
            kb.dma("sp", xrs[c][0][:], src(t)[c * 128:(c + 1) * 128, :], r=[src_dep], w=[xrs[c][1]])
        for fo in range(8):
            hs = fo // 4
            fc = slice((fo % 4) * 128, (fo % 4 + 1) * 128)
            gv_s, gg_s, gs_s = (0, 1, 2) if hs == 0 else (3, 0, 1)
            pV2, dpV2 = self.pA.next()
            pG2, dpG2 = self.pU.next()
            pGs, dpGs = self.pO.next()
            for (pX, dpX, sl_, rhsT, drhs) in ((pV2, dpV2, gv_s, zT, d_zT), (pG2, dpG2, gg_s, zT, d_zT),
                                               (pGs, dpGs, gs_s, xT, dxT)):
                for k in range(8):
                    kb.op("pe", lambda e, k=k, pX=pX, sl_=sl_, rhsT=rhsT, fc=fc: e.matmul(
                        pX[:], wslot[sl_][:, k, fc], rhsT[:, k, :], start=(k == 0), stop=(k == 7)),
                          r=[dslot[sl_], drhs], w=[dpX])
            s2, ds2 = s2_r.next()
            kb.op("act", lambda e, s2=s2, pG2=pG2: e.activation(out=s2[:], in_=pG2[:], func=AF.Sigmoid), r=[dpG2],
                  w=[ds2])
            kb.op("dve", lambda e, s2=s2, pV2=pV2: e.tensor_tensor(out=s2[:], in0=s2[:], in1=pV2[:], op=ALU.mult),
                  r=[ds2, dpV2], w=[ds2])
            gs, dgs = gs_r.next()
            kb.op("act", lambda e, gs=gs, pGs=pGs, fo=fo: e.activation(out=gs[:], in_=pGs[:], func=AF.Sigmoid,
                                                                       bias=bg[:, 8 + fo:9 + fo]), r=[dpGs, d_bg],
                  w=[dgs])
            kb.op("pool", lambda e, gs=gs, s2=s2, fo=fo: e.tensor_tensor(out=m1T[:, fo, :], in0=gs[:], in1=s2[:],
                                                                         op=ALU.mult), r=[dgs, ds2], w=[d_m1[fo]])
            if fo == 3:
                LW("glu_g", 1, 0)
                LW("wgs", 1, 1)
                LW("wo", 0, 2)
        stage_C(3)
        LW("wgr", 0, 3)
        LW("wo", 1, 0)
        LW("wgr", 1, 1)
        for fo in range(8):
            hs = fo // 4
            fc = slice((fo % 4) * 128, (fo % 4 + 1) * 128)
            wo_s, wgr_s = (2, 3) if hs == 0 else (0, 1)
            pY, dpY = self.pA.next()
            pGr, dpGr = self.pU.next()
            for k in range(8):
                kb.op("pe", lambda e, k=k, pGr=pGr, wgr_s=wgr_s, fc=fc: e.matmul(pGr[:], wslot[wgr_s][:, k, fc],
                                                                               xT[:, k, :], start=(k == 0),
                                                                               stop=(k == 7)),
                      r=[dslot[wgr_s], dxT], w=[dpGr])
            for k in range(8):
                kb.op("pe", lambda e, k=k, pY=pY, wo_s=wo_s, fc=fc: e.matmul(pY[:], wslot[wo_s][:, k, fc], gatedT[:, k, :],
                                                                           start=(k == 0), stop=(k == 7)),
                      r=[dslot[wo_s], d_gatedT], w=[dpY])
            gr, dgr = gr_r.next()
            kb.op("act", lambda e, gr=gr, pGr=pGr, fo=fo: e.activation(out=gr[:], in_=pGr[:], func=AF.Sigmoid,
                                                                       bias=bg[:, fo:fo + 1]), r=[dpGr, d_bg], w=[dgr])
            kb.op("dve", lambda e, gr=gr, pY=pY: e.tensor_tensor(out=gr[:], in0=gr[:], in1=pY[:], op=ALU.mult),
                  r=[dgr, dpY], w=[dgr])
            kb.op("pool", lambda e, gr=gr, fo=fo: e.tensor_tensor(out=m1T[:, fo, :], in0=m1T[:, fo, :], in1=gr[:],
                                                                  op=ALU.add), r=[dgr, d_m1[fo]], w=[d_m1[fo]])
            if fo == 3:
                LW("wout", 0, 2)
                LW("wout", 1, 3)
        if t + 1 < NTL:
            LW("wq", 0, 0)
            LW("wq", 1, 1)
        for c in range(4):
            cg = 4 * t + c
            cs = slice(c * 128, (c + 1) * 128)
            xr, dxr = xrs[c]
            if c >= 2:
                kb.dma("sp", xr[:], src(t)[c * 128:(c + 1) * 128, :], r=[src_dep], w=[dxr])
            rr, drr = self.rr.next()
            for half in range(2):
                pM, dpM = (self.pA if half == 0 else self.pU).next()
                for k in range(8):
                    kb.op("pe", lambda e, k=k, pM=pM, half=half: e.matmul(
                        pM[:], m1T[:, k, cs], wslot[2 + half][:, k, :], start=(k == 0), stop=(k == 7)),
                          r=[d_m1[k], dslot[2 + half]], w=[dpM])
                kb.op("dve", lambda e, pM=pM, half=half, rr=rr, xr=xr: e.scalar_tensor_tensor(
                    out=rr[:, half * 512:(half + 1) * 512], in0=xr[:, half * 512:(half + 1) * 512], scalar=float(ALPHA),
                    in1=pM[:], op0=ALU.mult, op1=ALU.add), r=[dxr, dpM], w=[drr])
            self.layer_norm(rr, drr, EPS, dst(t)[c * 128:(c + 1) * 128, :], dst_dep)


Prog.mixer_b = _mixer_b


N_CORES = 8
_PROG_CACHE = {}


def kernel(**inputs):
    xp = np.asarray(inputs["x_prompt"], np.float32)
    xs = np.asarray(inputs["x_sample"], np.float32)
    nb_p, nb_s = xp.shape[0], xs.shape[0]
    assert nb_p % N_CORES == 0 and nb_s % N_CORES == 0
    pp, ps_ = nb_p // N_CORES, nb_s // N_CORES
    nseq = pp + ps_
    inp = {k: np.asarray(v) for k, v in inputs.items() if not k.startswith("x_")}
    hi = host_inputs(inp)
    if nseq not in _PROG_CACHE:
        _PROG_CACHE[nseq] = Prog(nseq, {}).build()
    nc = _PROG_CACHE[nseq]
    in_maps = []
    for c in range(N_CORES):
        xc = np.concatenate([xp[c * pp:(c + 1) * pp], xs[c * ps_:(c + 1) * ps_]], axis=0)
        m = dict(hi)
        m["x"] = np.ascontiguousarray(xc)
        in_maps.append(m)
    res = run_bass_kernel_spmd(nc, in_maps, core_ids=list(range(N_CORES)))
    yp = np.concatenate([res.results[c]["y"][:pp] for c in range(N_CORES)], axis=0)
    ys = np.concatenate([res.results[c]["y"][pp:] for c in range(N_CORES)], axis=0)
    return (np.ascontiguousarray(yp, np.float32), np.ascontiguousarray(ys, np.float32))
```

```python
import math
import numpy as np
import concourse.bass as bass
import concourse.mybir as mybir
from concourse.bass_utils import run_bass_kernel_spmd

F32 = mybir.dt.float32
BF16 = mybir.dt.bfloat16
AF = mybir.ActivationFunctionType
ALU = mybir.AluOpType

D = 1024
DFF = 2816
NJ = DFF // 128
L_SEQ = 2048
DEPTH = 2
ALPHA = (2 * DEPTH) ** 0.25
EPS = 1e-5
NH = 8
T = 512


class Dep:
    __slots__ = ("w", "r", "rd", "wd")

    def __init__(self):
        self.w = None
        self.r = {}
        self.rd = []
        self.wd = []


class KB:
    EPOCH = 30000
    KD = 24

    def __init__(self):
        nc = bass.Bass("TRN2", target_bir_lowering=False)
        self.nc = nc
        self.eng = {"pe": nc.tensor, "act": nc.scalar, "dve": nc.vector, "pool": nc.gpsimd, "sp": nc.sync}
        self.cnt = {e: 0 for e in self.eng}
        self.sems = {e: [] for e in self.eng}
        self.waited = {e: {} for e in self.eng}
        self.dsems = [nc.alloc_semaphore(name=f"dma{i}") for i in range(self.KD)]
        self.ndma = 0
        self.nins = 0

    SB_LO = 16512
    SB_HI = 229344

    def sb(self, name, shape, dt):
        if not hasattr(self, "sb_off"):
            self.sb_off = self.SB_LO
            self.sb_peak = self.SB_LO
        n = 1
        for s in shape[1:]:
            n *= s
        nbytes = (n * mybir.dt.size(dt) + 31) // 32 * 32
        off = self.sb_off
        assert off + nbytes <= self.SB_HI, f"SBUF overflow allocating {name}: {off + nbytes - self.SB_HI} bytes over"
        self.sb_off += nbytes
        self.sb_peak = max(self.sb_peak, self.sb_off)
        return self.nc.alloc_sbuf_tensor_at(name, list(shape), dt, offset=off)

    def mark(self):
        if not hasattr(self, "sb_off"):
            self.sb_off = self.SB_LO
            self.sb_peak = self.SB_LO
        return self.sb_off

    def release(self, m):
        self.sb_off = m

    def barrier(self, dma=True):
        evs = [("d", i) for i in range(max(0, self.ndma - self.KD), self.ndma)] if dma else []
        for e in self.eng:
            if self.cnt[e] > 0:
                evs.append(("c", e, self.cnt[e]))
        for e in self.eng:
            self._wait(e, [ev for ev in evs if not (ev[0] == "c" and ev[1] == e)])

    def ps(self, name, shape, dt=F32):
        return self.nc.alloc_psum_tensor(name, list(shape), dt)

    def _sem(self, e, epoch):
        while len(self.sems[e]) <= epoch:
            self.sems[e].append(self.nc.alloc_semaphore(name=f"s_{e}_{len(self.sems[e])}"))
        return self.sems[e][epoch]

    def _wait(self, e, evs):
        best = {}
        for ev in evs:
            if ev[0] == "c":
                _, e2, c = ev
                if e2 == e and e == "pe":
                    continue
                key = ("c", e2, (c - 1) // self.EPOCH)
                val = (c - 1) % self.EPOCH + 1
            else:
                i = ev[1]
                key = ("d", i % self.KD)
                val = 16 * (i // self.KD + 1)
            if val > best.get(key, 0):
                best[key] = val
        wd = self.waited[e]
        for key, val in best.items():
            if wd.get(key, 0) >= val:
                continue
            if key[0] == "c":
                if any(k[0] == "c" and k[1] == key[1] and k[2] > key[2] for k in wd):
                    continue
                wd[key] = val
                self.eng[e].wait_ge(self._sem(key[1], key[2]), val)
            else:
                wd[key] = val
                self.eng[e].wait_ge(self.dsems[key[1]], val)
            self.nins += 1

    def _deps(self, r, w, e):
        evs = []
        for d in r:
            if d.w is not None:
                evs.append(d.w)
            evs.extend(d.wd)
        for d in w:
            if d.w is not None:
                evs.append(d.w)
            evs.extend(d.wd)
            for k, ev in d.r.items():
                if k != e:
                    evs.append(ev)
            evs.extend(d.rd)
        return evs

    def op(self, e, fn, r=(), w=(), extra=()):
        self._wait(e, self._deps(r, w, e) + list(extra))
        ins = fn(self.eng[e])
        self.cnt[e] += 1
        c = self.cnt[e]
        ins.then_inc(self._sem(e, (c - 1) // self.EPOCH), 1)
        ev = ("c", e, c)
        for d in r:
            d.r[e] = ev
        for d in w:
            d.w = ev
            d.wd = []
            d.r = {}
            d.rd = []
        self.nins += 1
        return ev

    def dma(self, q, out, in_, r=(), w=(), **kw):
        self._wait(q, self._deps(r, w, "dma"))
        i = self.ndma
        self.ndma += 1
        if i >= self.KD:
            key = ("d", i % self.KD)
            val = 16 * (i // self.KD)
            if self.waited[q].get(key, 0) < val:
                self.waited[q][key] = val
                self.eng[q].wait_ge(self.dsems[i % self.KD], val)
        self.eng[q].dma_start(out=out, in_=in_, **kw).then_inc(self.dsems[i % self.KD], 16)
        ev = ("d", i)
        for d in r:
            d.rd.append(ev)
            if len(d.rd) > self.KD:
                del d.rd[0]
        for d in w:
            d.wd.append(ev)
            if len(d.wd) > self.KD:
                del d.wd[0]
            d.r = {}
            d.rd = []
        self.nins += 1
        return ev

    def finish(self):
        evs = [("d", i) for i in range(max(0, self.ndma - self.KD), self.ndma)]
        for e in self.eng:
            if self.cnt[e] > 0 and e != "sp":
                evs.append(("c", e, self.cnt[e]))
        self._wait("sp", evs)


class Rot:
    def __init__(self, kb, name, n, shape, dt, psum=False):
        self.t = [(kb.ps if psum else kb.sb)(f"{name}{i}", shape, dt) for i in range(n)]
        self.d = [Dep() for _ in range(n)]
        self.i = 0

    def next(self):
        k = self.i % len(self.t)
        self.i += 1
        return self.t[k], self.d[k]


def _layout():
    off = {}
    o = 0

    def add(name, n):
        nonlocal o
        off[name] = o
        o += n

    for f in (1, 2):
        for jb in range(11):
            add(f"f{f}gu{jb}", 8 * 512)
        add(f"f{f}d", NJ * 1024)
    for nm in ("wq", "wk", "wv", "wg", "wu", "wgr", "wgs", "wo", "glu_v", "glu_g", "wout"):
        add(nm, 8 * 1024)
    return off, o


W_OFF, W_TOT = _layout()


def _kmajor(w):
    K, C = w.shape
    return w.reshape(K // 128, 128, C).transpose(1, 0, 2)


def pack_weights(inp, l):
    out = np.empty((128, W_TOT), np.float32)

    def put(name, arr):
        a = arr.reshape(128, -1)
        out[:, W_OFF[name]:W_OFF[name] + a.shape[1]] = a

    for f in (1, 2):
        gu = _kmajor(inp[f"ffn{f}_w_gu"][l])
        for jb in range(11):
            g = gu[:, :, jb * 256:(jb + 1) * 256]
            u = gu[:, :, DFF + jb * 256:DFF + (jb + 1) * 256]
            put(f"f{f}gu{jb}", np.concatenate([g, u], axis=2))
        put(f"f{f}d", _kmajor(inp[f"ffn{f}_w_down"][l]))
    win = _kmajor(inp["w_in"][l])
    swap = np.arange(512).reshape(8, 2, 32)[:, ::-1, :].reshape(512)
    q = win[:, :, 0:512]
    k = win[:, :, 512:1024]
    put("wq", np.concatenate([q, q[:, :, swap]], axis=2))
    put("wk", np.concatenate([k, k[:, :, swap]], axis=2))
    put("wv", win[:, :, 1024:2048])
    put("wg", win[:, :, 2048:3072])
    perm = np.arange(1024).reshape(64, 16).T.reshape(1024)
    put("wu", win[:, :, 3072:4096][:, :, perm])
    put("wgr", win[:, :, 4096:5120])
    put("wgs", win[:, :, 5120:6144])
    put("wo", _kmajor(inp["ret_w_o"][l]))
    glu = inp["s5_w_glu"][l][perm, :]
    glu = _kmajor(glu)
    put("glu_v", glu[:, :, 0:1024])
    put("glu_g", glu[:, :, 1024:2048])
    put("wout", _kmajor(inp["w_out"][l]))
    return out


def rep128(v):
    return np.ascontiguousarray(np.broadcast_to(np.asarray(v, np.float32)[None, :], (128, v.shape[0])))


CT_DEC = 0
CT_ZB = CT_DEC + 1024
CT_ZF = CT_ZB + 512
CT_XF = CT_ZF + 512
CT_XB = CT_XF + 512
CT_G128 = CT_XB + 512
CT_MF = CT_G128 + 512
CT_MB = CT_MF + 128
CT_TOT = CT_MB + 128
NB = 4
C1 = 8 * NB
NK = L_SEQ // C1
NBLK = L_SEQ // 8
NT1 = 2 * NB
NT2 = 1 + 2 * (NB - 1) + 2 * NB
S5P_W = 3 * 64 + 4 * 1024


def const_tables():
    lg = np.log1p(-np.exp2(-5.0 - np.arange(NH, dtype=np.float64)))
    ct = np.zeros((128, CT_TOT), np.float64)
    j = np.arange(128)[:, None]
    i = np.arange(128)[None, :]
    for h in range(NH):
        ct[:, CT_DEC + h * 128:CT_DEC + (h + 1) * 128] = 0.125 * np.exp(lg[h] * np.abs(i - j))
        ct[:, CT_ZB + h * 64:CT_ZB + (h + 1) * 64] = 0.125 * np.exp(lg[h] * j)
        ct[:, CT_ZF + h * 64:CT_ZF + (h + 1) * 64] = 0.125 * np.exp(lg[h] * (127 - j))
    for p in range(128):
        h2 = p // 64
        for hp in range(4):
            h = 2 * hp + h2
            ii = np.arange(128)
            ct[p, CT_XF + hp * 128:CT_XF + (hp + 1) * 128] = np.exp(lg[h] * (ii + 1))
            ct[p, CT_XB + hp * 128:CT_XB + (hp + 1) * 128] = np.exp(lg[h] * (128 - ii))
            ct[p, CT_G128 + hp * 128:CT_G128 + (hp + 1) * 128] = np.exp(lg[h] * 128.0)
    m = np.arange(128)[:, None] // 16
    q = np.arange(128)[None, :] // 16
    ct[:, CT_MF:CT_MF + 128] = (q >= m)
    ct[:, CT_MB:CT_MB + 128] = (m >= q)
    half = 32
    inv_freq = (10000.0 ** (-np.arange(half, dtype=np.float32) / half)).astype(np.float32)
    pos = np.arange(L_SEQ, dtype=np.float32)
    ang = pos[None, :] * inv_freq[:, None]
    cos = np.cos(ang.astype(np.float32)).astype(np.float32)
    sin = np.sin(ang.astype(np.float32)).astype(np.float32)
    rope = np.zeros((2, 128, L_SEQ), np.float32)
    for p in range(128):
        d = p % 64
        rope[0, p] = cos[d % 32]
        rope[1, p] = -sin[d % 32] if d < 32 else sin[d % 32]
    return ct.astype(np.float32), rope


def s5_host_layout(inp, l):
    out = np.zeros((128, S5P_W), np.float32)

    def lay(a):
        sh = a.shape
        a = a.reshape(2, 2, 32, 64, -1)
        a = a.transpose(1, 3, 0, 2, 4)
        return a.reshape(128, -1)

    out[:, 0:64] = lay(inp["s5_lam_re"][l])
    out[:, 64:128] = lay(inp["s5_lam_im"][l])
    ldt = np.broadcast_to(inp["s5_log_dt"][l][:, :, None], (2, 64, 64))
    out[:, 128:192] = lay(np.ascontiguousarray(ldt))
    o = 192
    for nm in ("s5_b_re", "s5_b_im"):
        out[:, o:o + 1024] = lay(inp[nm][l])
        o += 1024
    for nm in ("s5_c_re", "s5_c_im"):
        out[:, o:o + 1024] = lay(inp[nm][l].transpose(0, 1, 3, 2))
        o += 1024
    dcol = np.zeros((128, 64), np.float32)
    dd = inp["s5_d"][l].reshape(64, 16)
    for s in range(8):
        dcol[s * 16:(s + 1) * 16, :] = dd.T
    return out, dcol


class Prog:
    def __init__(self, nseq, cfg):
        self.nseq = nseq
        self.cfg = cfg
        kb = self.kb = KB()
        nc = self.nc = kb.nc
        dt = nc.dram_tensor
        self.x = dt("x", [nseq, L_SEQ, D], F32, kind="ExternalInput").ap()
        self.y = dt("y", [nseq, L_SEQ, D], F32, kind="ExternalOutput").ap()
        self.wpack = dt("wpack", [DEPTH, 128, W_TOT], F32, kind="ExternalInput").ap()
        self.lntab = dt("lntab", [DEPTH, 3, 2, 128, D], F32, kind="ExternalInput").ap()
        self.ctab = dt("ctab", [128, CT_TOT], F32, kind="ExternalInput").ap()
        self.rope = dt("rope", [2, 128, L_SEQ], F32, kind="ExternalInput").ap()
        self.bgate = dt("bgate", [DEPTH, 128, 16], F32, kind="ExternalInput").ap()
        self.s5p = dt("s5p", [DEPTH, 128, S5P_W], F32, kind="ExternalInput").ap()
        self.s5d = dt("s5d", [DEPTH, 128, 64], F32, kind="ExternalInput").ap()
        self.wbf = dt("wbf", [DEPTH, 128, W_TOT], BF16, kind="Internal").ap()
        self.s5m = dt("s5m", [DEPTH, 64, 128, NT1 + NT2, 128], BF16, kind="Internal").ap()
        self.xa = dt("xa", [L_SEQ, D], F32, kind="Internal").ap()
        self.xb = dt("xb", [L_SEQ, D], F32, kind="Internal").ap()
        self.xts = dt("xts", [128, 8, L_SEQ], BF16, kind="Internal").ap()
        self.ks = dt("ks", [128, 4, L_SEQ], BF16, kind="Internal").ap()
        self.vs = dt("vs", [L_SEQ, D], BF16, kind="Internal").ap()
        self.us = dt("us", [L_SEQ, D], BF16, kind="Internal").ap()
        self.zs = dt("zs", [L_SEQ, D], BF16, kind="Internal").ap()
        self.d_wbf = [Dep() for _ in range(DEPTH)]
        self.d_s5m = [Dep() for _ in range(DEPTH)]
        self.d_xa = Dep()
        self.d_xb = Dep()
        self.d_in = Dep()
        self.d_scr = {k: Dep() for k in ("xts", "ks", "vs", "us", "zs")}
        self.dbg = {}
        if cfg.get("debug"):
            for nm, shp, dty in (("dbg_ks", [128, 4, L_SEQ], BF16), ("dbg_vs", [L_SEQ, D], BF16),
                                 ("dbg_us", [L_SEQ, D], BF16), ("dbg_zs", [L_SEQ, D], BF16),
                                 ("dbg_bs", [128, 16, 512], BF16),
                                 ("dbg_s5m", [64, 128, NT1 + NT2, 128], BF16)):
                self.dbg[nm] = dt(nm, shp, dty, kind="ExternalOutput").ap()
        self.alloc_persistent()

    def alloc_persistent(self):
        kb = self.kb
        self.idf = kb.sb("idf", [128, 128], F32)
        self.idb = kb.sb("idb", [128, 128], BF16)
        self.ct = kb.sb("ct", [128, CT_TOT], F32)
        self.d_const = Dep()
        self.pA = Rot(kb, "pA", 2, [128, 512], F32, psum=True)
        self.pU = Rot(kb, "pU", 2, [128, 512], F32, psum=True)
        self.pO = Rot(kb, "pO", 2, [128, 512], F32, psum=True)
        self.pT = Rot(kb, "pT", 2, [128, 512], F32, psum=True)
        self.base_mark = kb.mark()

    def ctv(self, off, n):
        return self.ct[:, off:off + n]

    def consts(self):
        kb = self.kb
        for t in (self.idf, self.idb):
            kb.op("pool", lambda e, t=t: e.memset(t[:], 1.0), w=[self.d_const])
            kb.op("pool", lambda e, t=t: e.affine_select(out=t[:], in_=t[:], pattern=[[-1, 128]],
                                                         compare_op=ALU.is_equal, fill=0.0, base=0,
                                                         channel_multiplier=1), r=[self.d_const], w=[self.d_const])
        kb.dma("sp", self.ct[:], self.ctab, w=[self.d_const])

    def cast_weights_gen(self):
        kb = self.kb
        PC = 2048
        pieces = []
        for l in range(DEPTH):
            o = 0
            while o < W_TOT:
                n = min(PC, W_TOT - o)
                pieces.append((l, o, n))
                o += n
        NBUF = 4
        i = 0
        while i < len(pieces):
            fin = Rot(kb, "cst_in", NBUF, [128, PC], F32)
            fout = Rot(kb, "cst_out", 2, [128, PC], BF16)
            self.cast_realloc = False
            pend = []
            while i < len(pieces) and not self.cast_realloc:
                while len(pend) < NBUF - 1 and i + len(pend) < len(pieces):
                    l, o, n = pieces[i + len(pend)]
                    bi, dbi = fin.next()
                    kb.dma("sp", bi[:, 0:n], self.wpack[l, :, o:o + n], w=[dbi])
                    pend.append((bi, dbi))
                l, o, n = pieces[i]
                bi, dbi = pend.pop(0)
                bo, dbo = fout.next()
                kb.op("act", lambda e, bi=bi, bo=bo, n=n: e.activation(out=bo[:, 0:n], in_=bi[:, 0:n], func=AF.Copy),
                      r=[dbi], w=[dbo])
                kb.dma("sp", self.wbf[l, :, o:o + n], bo[:, 0:n], r=[dbo], w=[self.d_wbf[l]])
                i += 1
                yield
        return

    def phase_begin(self, dma=True):
        self._last_phase = None
        self.kb.barrier(dma=dma)
        self.kb.release(self.base_mark)
        self.cast_realloc = True

    def wblock(self, l, name, buf, dbuf, q="sp"):
        o = W_OFF[name]
        for h in range(2):
            self.kb.dma(q, buf[:, 4 * h:4 * h + 4, :],
                        self.wbf[l][:, o + h * 4096:o + (h + 1) * 4096].rearrange("p (k c) -> p k c", c=1024),
                        r=[self.d_wbf[l]], w=[dbuf])

    def transposes_to_xT(self, xt, dxt, xT, dxT):
        kb = self.kb
        for k in range(8):
            pT, dpT = self.pT.next()
            for c in range(4):
                kb.op("pe", lambda e, c=c, k=k, pT=pT: e.transpose(pT[:, c * 128:(c + 1) * 128],
                                                                  xt[:, c, k * 128:(k + 1) * 128], self.idf[:]),
                      r=[dxt, self.d_const], w=[dpT])
            if k % 2 == 0:
                kb.op("act", lambda e, k=k, pT=pT: e.activation(out=xT[:, k, :], in_=pT[:], func=AF.Copy),
                      r=[dpT], w=[dxT])
            else:
                kb.op("dve", lambda e, k=k, pT=pT: e.tensor_copy(out=xT[:, k, :], in_=pT[:]), r=[dpT], w=[dxT])

    def layer_norm(self, rr, drr, eps, dst_ap, dst_dep):
        kb = self.kb
        st, dst_ = self.st.next()
        dsa, dsb = Dep(), Dep()
        kb.op("dve", lambda e: e.bn_stats(out=st[:, 0:6], in_=rr[:, 0:512]), r=[drr], w=[dsa, dst_])
        kb.op("dve", lambda e: e.bn_stats(out=st[:, 6:12], in_=rr[:, 512:1024]), r=[drr], w=[dsb, dst_])
        kb.op("dve", lambda e: e.bn_aggr(out=st[:, 12:14], in_=st[:, 0:12]), r=[dsa, dsb], w=[dst_])
        kb.op("dve", lambda e: e.tensor_scalar(out=st[:, 14:15], in0=st[:, 13:14], scalar1=float(eps), scalar2=None,
                                              op0=ALU.add), r=[dst_], w=[dst_])
        kb.op("act", lambda e: e.activation(out=st[:, 14:15], in_=st[:, 14:15], func=AF.Sqrt), r=[dst_], w=[dst_])
        kb.op("dve", lambda e: e.reciprocal(out=st[:, 15:16], in_=st[:, 14:15]), r=[dst_], w=[dst_])
        kb.op("dve", lambda e: e.tensor_scalar(out=rr[:], in0=rr[:], scalar1=st[:, 12:13], scalar2=st[:, 15:16],
                                              op0=ALU.subtract, op1=ALU.mult), r=[dst_, drr], w=[drr])
        xo, dxo = self.xo.next()
        kb.op("pool", lambda e: e.tensor_tensor(out=rr[:], in0=rr[:], in1=self.lnG[:], op=ALU.mult),
              r=[drr, self.d_ln], w=[drr])
        kb.op("pool", lambda e: e.tensor_tensor(out=xo[:], in0=rr[:], in1=self.lnB[:], op=ALU.add),
              r=[drr, self.d_ln], w=[dxo])
        kb.dma("pool", dst_ap, xo[:], r=[dxo], w=[dst_dep])

    def alloc_ln(self, l, ln_idx, nrr=2):
        kb = self.kb
        self.lnG = kb.sb("lnG", [128, D], F32)
        self.lnB = kb.sb("lnB", [128, D], F32)
        self.d_ln = Dep()
        self.st = Rot(kb, "st", 2, [128, 16], F32)
        self.rr = Rot(kb, "rr", nrr, [128, D], F32)
        self.xo = Rot(kb, "xo", 2, [128, D], F32)
        kb.dma("sp", self.lnG[:], self.lntab[l, ln_idx, 0], w=[self.d_ln])
        kb.dma("sp", self.lnB[:], self.lntab[l, ln_idx, 1], w=[self.d_ln])

    def ffn(self, l, f, src, src_dep, dst, dst_dep, ln_idx):
        kb = self.kb
        wbf = self.wbf[l]
        dW = self.d_wbf[l]
        if getattr(self, "_last_phase", None) == "ffn":
            (xt_r, xT_r, hid, d_hid, wgu_r, wd_r, silu_r) = self._ffn_ctx
            kb.dma("sp", self.lnG[:], self.lntab[l, ln_idx, 0], w=[self.d_ln])
            kb.dma("sp", self.lnB[:], self.lntab[l, ln_idx, 1], w=[self.d_ln])
        else:
            self.phase_begin()
            xt_r = Rot(kb, "xt", 2, [128, 4, D], F32)
            xT_r = Rot(kb, "xT", 1, [128, 8, T], BF16)
            hid = kb.sb("hid", [128, NJ, T], BF16)
            d_hid = [Dep() for _ in range(NJ)]
            wgu_r = Rot(kb, "wgu", 4, [128, 8, 512], BF16)
            wd_r = Rot(kb, "wd", 1, [128, NJ, 1024], BF16)
            silu_r = Rot(kb, "silu", 2, [128, T], F32)
            self.alloc_ln(l, ln_idx)
            self._ffn_ctx = (xt_r, xT_r, hid, d_hid, wgu_r, wd_r, silu_r)
        self._last_phase = "ffn"
        gu_off = [W_OFF[f"f{f}gu{jb}"] for jb in range(11)]
        d_off = W_OFF[f"f{f}d"]

        def load_x(t):
            xt, dxt = xt_r.next()
            kb.dma("sp", xt[:], src(t).rearrange("(c p) f -> p c f", p=128), r=[src_dep], w=[dxt])
            return xt, dxt

        NTL = L_SEQ // T
        NBLK_GU = 11 * NTL
        blks = {}

        def load_gu(i):
            if i >= NBLK_GU or i in blks:
                return
            jb = i % 11
            wg, dwg = wgu_r.next()
            kb.dma("sp", wg[:], wbf[:, gu_off[jb]:gu_off[jb] + 4096].rearrange("p (k c) -> p k c", c=512),
                   r=[dW], w=[dwg])
            blks[i] = (wg, dwg)

        nxt = load_x(0)
        load_gu(0)
        load_gu(1)
        load_gu(2)
        for t in range(NTL):
            xt, dxt = nxt
            xT, dxT = xT_r.next()
            self.transposes_to_xT(xt, dxt, xT, dxT)
            wd, dwd = wd_r.next()
            for jb in range(11):
                i = t * 11 + jb
                if jb == 1:
                    for j0 in range(0, NJ, 2):
                        kb.dma("sp", wd[:, j0:j0 + 2, :],
                               wbf[:, d_off + j0 * 1024:d_off + (j0 + 2) * 1024].rearrange("p (j c) -> p j c", c=1024),
                               r=[dW], w=[dwd])
                    if t + 1 < NTL:
                        nxt = load_x(t + 1)
                load_gu(i + 3)
                wg, dwg = blks.pop(i)
                for jj in range(2):
                    j = 2 * jb + jj
                    pA, dpA = self.pA.next()
                    pU, dpU = self.pU.next()
                    for k in range(8):
                        kb.op("pe", lambda e, k=k, pA=pA, wg=wg, jj=jj: e.matmul(
                            pA[:], wg[:, k, jj * 128:(jj + 1) * 128], xT[:, k, :], start=(k == 0), stop=(k == 7)),
                              r=[dwg, dxT], w=[dpA])
                    for k in range(8):
                        kb.op("pe", lambda e, k=k, pU=pU, wg=wg, jj=jj: e.matmul(
                            pU[:], wg[:, k, 256 + jj * 128:256 + (jj + 1) * 128], xT[:, k, :], start=(k == 0),
                            stop=(k == 7)), r=[dwg, dxT], w=[dpU])
                    sl, dsl = silu_r.next()
                    kb.op("act", lambda e, sl=sl, pA=pA: e.activation(out=sl[:], in_=pA[:], func=AF.Silu),
                          r=[dpA], w=[dsl])
                    kb.op("dve", lambda e, sl=sl, pU=pU, j=j: e.tensor_tensor(out=hid[:, j, :], in0=sl[:],
                                                                             in1=pU[:], op=ALU.mult),
                          r=[dsl, dpU], w=[d_hid[j]])
            for c in range(4):
                pO0, dpO0 = self.pO.next()
                pO1, dpO1 = self.pO.next()
                for half, (pO, dpO) in enumerate(((pO0, dpO0), (pO1, dpO1))):
                    for j in range(NJ):
                        kb.op("pe", lambda e, j=j, pO=pO, half=half, c=c: e.matmul(
                            pO[:], hid[:, j, c * 128:(c + 1) * 128], wd[:, j, half * 512:(half + 1) * 512],
                            start=(j == 0), stop=(j == NJ - 1)), r=[d_hid[j], dwd], w=[dpO])
                rr, drr = self.rr.next()
                for half, (pO, dpO) in enumerate(((pO0, dpO0), (pO1, dpO1))):
                    kb.op("dve", lambda e, pO=pO, half=half, rr=rr, c=c: e.scalar_tensor_tensor(
                        out=rr[:, half * 512:(half + 1) * 512], in0=xt[:, c, half * 512:(half + 1) * 512],
                        scalar=2.0 * ALPHA, in1=pO[:], op0=ALU.mult, op1=ALU.add), r=[dxt, dpO], w=[drr])
                self.layer_norm(rr, drr, 4.0 * EPS, dst(t)[c * 128:(c + 1) * 128, :], dst_dep)

    def rotary(self, pX, dpX, pXs, dpXs, cos, sin, dtab, tmp_r, out_ap, out_dep, also=None):
        kb = self.kb
        t1, dt1 = tmp_r.next()
        t2, dt2 = tmp_r.next()
        kb.op("dve", lambda e: e.tensor_tensor(out=t1[:], in0=pX[:], in1=cos[:], op=ALU.mult), r=[dpX, dtab], w=[dt1])
        kb.op("dve", lambda e: e.tensor_tensor(out=t2[:], in0=pXs[:], in1=sin[:], op=ALU.mult), r=[dpXs, dtab],
              w=[dt2])
        if also is None:
            kb.op("pool", lambda e: e.tensor_tensor(out=out_ap, in0=t1[:], in1=t2[:], op=ALU.add), r=[dt1, dt2],
                  w=[out_dep])
        else:
            kb.op("pool", lambda e: e.tensor_tensor(out=t1[:], in0=t1[:], in1=t2[:], op=ALU.add), r=[dt1, dt2],
                  w=[dt1])
            kb.op("act", lambda e: e.activation(out=out_ap, in_=t1[:], func=AF.Copy), r=[dt1], w=[out_dep])
            for fn in also:
                fn(t1, dt1)

    def kv_state_step(self, kz, dkz, v, dv, st, dst_):
        kb = self.kb
        pT, dpT = self.pT.next()
        for h in range(NH):
            b = (h % 2) * 64
            hp = h // 2
            kb.op("pe", lambda e, h=h, b=b, hp=hp: e.matmul(pT[b:b + 64, hp * 128:(hp + 1) * 128],
                                                           kz[:, h * 64:(h + 1) * 64], v[:, h * 128:(h + 1) * 128],
                                                           start=True, stop=True), r=[dkz, dv], w=[dpT])
        kb.op("pool", lambda e: e.tensor_tensor(out=st[:], in0=st[:], in1=self.ctv(CT_G128, 512), op=ALU.mult),
              r=[dst_, self.d_const], w=[dst_])
        kb.op("dve", lambda e: e.tensor_tensor(out=st[:], in0=st[:], in1=pT[:], op=ALU.add), r=[dst_, dpT],
              w=[dst_])

    def k_transpose_scaled(self, kT, dkT, c, ztab_off, kz_r):
        kb = self.kb
        pT, dpT = self.pT.next()
        pTb = pT[:].bitcast(BF16)
        for f in range(4):
            kb.op("pe", lambda e, f=f: e.transpose(pTb[:, f * 128:(f + 1) * 128], kT[:, f, c * 128:(c + 1) * 128],
                                                    self.idb[:]), r=[dkT, self.d_const], w=[dpT])
        kz, dkz = kz_r.next()
        kb.op("dve", lambda e: e.tensor_tensor(out=kz[:], in0=pTb[:, 0:512], in1=self.ctv(ztab_off, 512),
                                              op=ALU.mult), r=[dpT, self.d_const], w=[dkz])
        return kz, dkz

    def mixer_a(self, l, src, src_dep):
        kb = self.kb
        self.phase_begin()
        u2a = kb.sb("u2a", [128, NBLK, 64], BF16)
        self.d_u2a = Dep()
        d_us_tile = [Dep() for _ in range(L_SEQ // T)]
        usv = self.us.rearrange("(b s) (p g) -> (s p) b g", s=8, p=16)
        xt_r = Rot(kb, "xt", 2, [128, 4, D], F32)
        xT_r = Rot(kb, "xT", 2, [128, 8, T], BF16)
        wk = kb.sb("wk", [128, 8, 1024], BF16)
        wv = kb.sb("wv", [128, 8, 1024], BF16)
        wu = kb.sb("wu", [128, 8, 1024], BF16)
        dwk, dwv, dwu = Dep(), Dep(), Dep()
        rope_r = Rot(kb, "rope", 2, [128, 2, T], F32)
        kT_r = Rot(kb, "kT", 2, [128, 4, T], BF16)
        tmp_r = Rot(kb, "rtmp", 4, [128, T], F32)
        v_r = Rot(kb, "v", 2, [128, D], BF16)
        u_r = Rot(kb, "u", 2, [128, D], BF16)
        kz_r = Rot(kb, "kz", 2, [128, 512], BF16)
        st = self.bstate
        dst_ = self.d_bstate
        kb.op("pool", lambda e: e.memset(st[:], 0.0), w=[dst_])

        def load_x(t):
            xt, dxt = xt_r.next()
            kb.dma("sp", xt[:], src(t).rearrange("(c p) f -> p c f", p=128), r=[src_dep], w=[dxt])
            rp, drp = rope_r.next()
            kb.dma("sp", rp[:], self.rope[:, :, t * T:(t + 1) * T].rearrange("a p t -> p a t"), w=[drp])
            return xt, dxt, rp, drp

        NTL = L_SEQ // T
        nxt = load_x(NTL - 1)
        self.wblock(l, "wk", wk, dwk)
        self.wblock(l, "wv", wv, dwv)
        self.wblock(l, "wu", wu, dwu)
        for t in range(NTL - 1, -1, -1):
            xt, dxt, rp, drp = nxt
            if t > 0:
                nxt = load_x(t - 1)
            xT, dxT = xT_r.next()
            self.transposes_to_xT(xt, dxt, xT, dxT)
            kb.dma("pool", self.xts[:, :, t * T:(t + 1) * T], xT[:], r=[dxT], w=[self.d_scr["xts"]])
            kT, dkT = kT_r.next()
            for f in range(4):
                pA, dpA = self.pA.next()
                pU, dpU = self.pU.next()
                for k in range(8):
                    kb.op("pe", lambda e, k=k, f=f, pA=pA: e.matmul(pA[:], wk[:, k, f * 128:(f + 1) * 128], xT[:, k, :],
                                                                  start=(k == 0), stop=(k == 7)), r=[dwk, dxT], w=[dpA])
                for k in range(8):
                    kb.op("pe", lambda e, k=k, f=f, pU=pU: e.matmul(pU[:], wk[:, k, 512 + f * 128:512 + (f + 1) * 128],
                                                                  xT[:, k, :], start=(k == 0), stop=(k == 7)),
                          r=[dwk, dxT], w=[dpU])
                self.rotary(pA, dpA, pU, dpU, rp[:, 0, :], rp[:, 1, :], drp, tmp_r, kT[:, f, :], dkT)
            kb.dma("pool", self.ks[:, :, t * T:(t + 1) * T], kT[:], r=[dkT], w=[self.d_scr["ks"]])
            for c in range(3, -1, -1):
                cg = 4 * t + c
                rows = slice(cg * 128, (cg + 1) * 128)
                v, dv = v_r.next()
                u, du = u_r.next()
                for (wX, dwX, dstt, ddst, eng) in ((wv, dwv, v, dv, "act"), (wu, dwu, u, du, "dve")):
                    for half in range(2):
                        pO, dpO = self.pO.next()
                        for k in range(8):
                            kb.op("pe", lambda e, k=k, pO=pO, wX=wX, half=half, c=c: e.matmul(
                                pO[:], xT[:, k, c * 128:(c + 1) * 128], wX[:, k, half * 512:(half + 1) * 512],
                                start=(k == 0), stop=(k == 7)), r=[dxT, dwX], w=[dpO])
                        if eng == "act":
                            kb.op("act", lambda e, pO=pO, dstt=dstt, half=half: e.activation(
                                out=dstt[:, half * 512:(half + 1) * 512], in_=pO[:], func=AF.Copy), r=[dpO], w=[ddst])
                        else:
                            kb.op("dve", lambda e, pO=pO, dstt=dstt, half=half: e.tensor_copy(
                                out=dstt[:, half * 512:(half + 1) * 512], in_=pO[:]), r=[dpO], w=[ddst])
                kb.dma("pool", self.vs[rows, :], v[:], r=[dv], w=[self.d_scr["vs"]])
                kb.dma("pool", self.us[rows, :], u[:], r=[du], w=[self.d_scr["us"], d_us_tile[t]])
                kz, dkz = self.k_transpose_scaled(kT, dkT, c, CT_ZB, kz_r)
                kb.op("act", lambda e, cg=cg: e.activation(out=self.bs[:, cg, :], in_=st[:], func=AF.Copy),
                      r=[dst_], w=[self.d_bs[cg]])
                self.kv_state_step(kz, dkz, v, dv, st, dst_)
            with self.nc.allow_non_contiguous_dma(reason="128B runs"):
                for b0 in range(t * 64, (t + 1) * 64, 16):
                    kb.dma("sp", u2a[:, b0:b0 + 16, :], usv[:, b0:b0 + 16, :], r=[d_us_tile[t]], w=[self.d_u2a])

    def alloc_seq_persistent(self):
        kb = self.kb
        kb.barrier()
        kb.release(self.base_mark)
        self.bs = kb.sb("bs", [128, 16, 512], BF16)
        self.d_bs = [Dep() for _ in range(16)]
        self.bstate = kb.sb("bstate", [128, 512], F32)
        self.d_bstate = Dep()
        self.fstate = kb.sb("fstate", [128, 512], F32)
        self.d_fstate = Dep()
        self.fbf = kb.sb("fbf", [128, 512], BF16)
        self.d_fbf = Dep()
        self.base_mark = kb.mark()

    def dbg_copy(self, name, src_ap, dep):
        if name in self.dbg:
            self.kb.dma("sp", self.dbg[name], src_ap, r=[dep])

    def build(self):
        cfg = self.cfg
        only = cfg.get("only")
        self.consts()
        self.cast_gen = self.cast_weights_gen()
        if only in (None, "s5pre", "mixer", "m2"):
            self.s5_prologue()
        for _ in self.cast_gen:
            pass
        self.alloc_seq_persistent()
        for s in range(self.nseq):
            xin = lambda t, s=s: self.x[s, t * T:(t + 1) * T, :]
            yout = lambda t, s=s: self.y[s, t * T:(t + 1) * T, :]
            xa = lambda t: self.xa[t * T:(t + 1) * T, :]
            xb = lambda t: self.xb[t * T:(t + 1) * T, :]
            dy = Dep()
            if only == "ffn1":
                self.ffn(0, 1, xin, self.d_in, yout, dy, 0)
                continue
            if only == "m1":
                self.mixer_a(0, xin, self.d_in)
                continue
            if only == "mixer":
                self.mixer_a(0, xin, self.d_in)
                self.mixer_s5(0)
                self.mixer_b(0, xin, self.d_in, yout, dy)
                continue
            if only == "m2":
                self.mixer_a(0, xin, self.d_in)
                self.mixer_s5(0)
                continue
            if only == "s5pre":
                continue
            A = (xa, self.d_xa)
            Bf = (xb, self.d_xb)
            for l in range(DEPTH):
                src, sdep = (xin, self.d_in) if l == 0 else A
                P, Q = (A, Bf) if l == 0 else (Bf, A)
                last = l == DEPTH - 1
                self.ffn(l, 1, src, sdep, P[0], P[1], 0)
                self.mixer_a(l, P[0], P[1])
                self.mixer_s5(l)
                self.mixer_b(l, P[0], P[1], Q[0], Q[1])
                if last:
                    self.ffn(l, 2, Q[0], Q[1], yout, dy, 2)
                else:
                    self.ffn(l, 2, Q[0], Q[1], A[0], A[1], 2)
        self.kb.barrier()
        self.dbg_copy("dbg_ks", self.ks, self.d_scr["ks"])
        self.dbg_copy("dbg_vs", self.vs, self.d_scr["vs"])
        self.dbg_copy("dbg_us", self.us, self.d_scr["us"])
        self.dbg_copy("dbg_zs", self.zs, self.d_scr["zs"])
        if "dbg_bs" in self.dbg:
            self.kb.dma("sp", self.dbg["dbg_bs"], self.bs[:], r=self.d_bs)
        self.dbg_copy("dbg_s5m", self.s5m[0], self.d_s5m[0])
        self.kb.finish()
        return self.nc


def host_inputs(inp):
    wpack = np.stack([pack_weights(inp, l) for l in range(DEPTH)])
    lntab = np.empty((DEPTH, 3, 2, 128, D), np.float32)
    for l in range(DEPTH):
        for i, nm in enumerate(("ln1", "ln2", "ln3")):
            lntab[l, i, 0] = rep128(inp[f"{nm}_g"][l])
            lntab[l, i, 1] = rep128(inp[f"{nm}_b"][l])
    ct, rope = const_tables()
    bgate = np.stack([np.ascontiguousarray(inp["b_gate"][l].reshape(16, 128).T) for l in range(DEPTH)])
    s5 = [s5_host_layout(inp, l) for l in range(DEPTH)]
    return {"wpack": wpack, "lntab": lntab, "ctab": ct, "rope": rope, "bgate": bgate.astype(np.float32),
            "s5p": np.stack([a for a, _ in s5]), "s5d": np.stack([b for _, b in s5])}

TI_SO_RE = 0
TI_SO_IM = NB
TI_T0 = 2 * NB
TI_TF = TI_T0
TI_TB = TI_T0 + NB - 1
TI_SI_RE = TI_T0 + 1 + 2 * (NB - 1)
TI_SI_IM = TI_SI_RE + NB
GCH = 4


def _s5_prologue(self):
    kb = self.kb
    self.s5a = self.nc.dram_tensor("s5a", [DEPTH, 2, 128, 64], F32, kind="Internal").ap()
    self.d_s5a = [Dep() for _ in range(DEPTH)]
    for l in range(DEPTH):
        self._s5_prologue_layer(l)


def _s5_prologue_layer(self, l):
    kb = self.kb
    self.phase_begin()
    NDG = 64
    prm = kb.sb("prm", [128, S5P_W], F32)
    dprm = Dep()
    kb.dma("sp", prm[:], self.s5p[l], w=[dprm])
    dcol = kb.sb("dcol", [128, 64], F32)
    ddcol = Dep()
    kb.dma("sp", dcol[:], self.s5d[l], w=[ddcol])
    lre = prm[:, 0:64]
    lim = prm[:, 64:128]
    ldt = prm[:, 128:192]
    bre = prm[:, 192:192 + 1024].rearrange("n (a p) -> n a p", p=16)
    bim = prm[:, 192 + 1024:192 + 2048].rearrange("n (a p) -> n a p", p=16)
    cre = prm[:, 192 + 2048:192 + 3072].rearrange("n (a p) -> n a p", p=16)
    cim = prm[:, 192 + 3072:192 + 4096].rearrange("n (a p) -> n a p", p=16)

    W = {}
    dW = {}

    def wt(name, shape=(128, NDG)):
        W[name] = kb.sb("s5w_" + name, list(shape), F32)
        dW[name] = Dep()
        return W[name]

    for nm in ("lr", "dt", "lrdt", "mag", "imag", "th", "c", "s", "t1", "t2", "abr", "abi", "ivr", "ivi", "den",
               "fre", "fim", "nre", "hpi"):
        wt(nm)
    wt("bbr", (128, NDG, 16))
    wt("bbi", (128, NDG, 16))
    wt("b1", (128, NDG, 16))
    wt("b2", (128, NDG, 16))
    PAr = kb.sb("PAr", [128, C1 + 1, NDG], F32)
    PAi = kb.sb("PAi", [128, C1 + 1, NDG], F32)
    PDr = kb.sb("PDr", [128, C1 + 1, NDG], F32)
    PDi = kb.sb("PDi", [128, C1 + 1, NDG], F32)
    Nr = kb.sb("Nr", [128, 8, NDG], F32)
    Ni = kb.sb("Ni", [128, 8, NDG], F32)
    dP = Dep()

    def tt(eng, out, a, b, op, r, w):
        kb.op(eng, lambda e: e.tensor_tensor(out=out, in0=a, in1=b, op=op), r=r, w=w)

    def ts(eng, out, a, s1, op0, r, w):
        kb.op(eng, lambda e: e.tensor_scalar(out=out, in0=a, scalar1=s1, scalar2=None, op0=op0), r=r, w=w)

    def act(out, a, func, r, w, scale=None, bias=None):
        kw = {}
        if scale is not None:
            kw["scale"] = scale
        if bias is not None:
            kw["bias"] = bias
        kb.op("act", lambda e: e.activation(out=out, in_=a, func=func, **kw), r=r, w=w)

    A_ = lambda n: W[n][:]
    d_ = lambda n: dW[n]
    ts("dve", A_("lr"), lre, -1e-4, ALU.min, [dprm], [d_("lr")])
    act(A_("dt"), ldt, AF.Exp, [dprm], [d_("dt")])
    tt("dve", A_("lrdt"), A_("lr"), A_("dt"), ALU.mult, [d_("lr"), d_("dt")], [d_("lrdt")])
    act(A_("mag"), A_("lrdt"), AF.Exp, [d_("lrdt")], [d_("mag")])
    act(A_("imag"), A_("lrdt"), AF.Exp, [d_("lrdt")], [d_("imag")], scale=-1.0)
    tt("dve", A_("th"), lim, A_("dt"), ALU.mult, [dprm, d_("dt")], [d_("th")])
    kb.op("pool", lambda e: e.memset(W["hpi"][:], math.pi / 2), w=[d_("hpi")])
    act(A_("s"), A_("th"), AF.Sin, [d_("th")], [d_("s")], scale=1.0 / 16)
    act(A_("c"), A_("th"), AF.Sin, [d_("th"), d_("hpi")], [d_("c")], scale=1.0 / 16, bias=W["hpi"][:, 0:1])
    for _ in range(4):
        tt("dve", A_("t1"), A_("c"), A_("c"), ALU.mult, [d_("c")], [d_("t1")])
        tt("pool", A_("t2"), A_("s"), A_("s"), ALU.mult, [d_("s")], [d_("t2")])
        tt("dve", A_("t2"), A_("t1"), A_("t2"), ALU.subtract, [d_("t1"), d_("t2")], [d_("t2")])
        tt("pool", A_("t1"), A_("c"), A_("s"), ALU.mult, [d_("c"), d_("s")], [d_("t1")])
        ts("dve", A_("s"), A_("t1"), 2.0, ALU.mult, [d_("t1")], [d_("s")])
        kb.op("act", lambda e: e.activation(out=W["c"][:], in_=W["t2"][:], func=AF.Copy), r=[d_("t2")], w=[d_("c")])
    tt("dve", A_("abr"), A_("mag"), A_("c"), ALU.mult, [d_("mag"), d_("c")], [d_("abr")])
    tt("pool", A_("abi"), A_("mag"), A_("s"), ALU.mult, [d_("mag"), d_("s")], [d_("abi")])
    tt("dve", A_("ivr"), A_("imag"), A_("c"), ALU.mult, [d_("imag"), d_("c")], [d_("ivr")])
    tt("pool", A_("ivi"), A_("imag"), A_("s"), ALU.mult, [d_("imag"), d_("s")], [d_("ivi")])
    ts("dve", A_("ivi"), A_("ivi"), -1.0, ALU.mult, [d_("ivi")], [d_("ivi")])
    tt("dve", A_("t1"), A_("lr"), A_("lr"), ALU.mult, [d_("lr")], [d_("t1")])
    tt("pool", A_("t2"), lim, lim, ALU.mult, [dprm], [d_("t2")])
    tt("dve", A_("den"), A_("t1"), A_("t2"), ALU.add, [d_("t1"), d_("t2")], [d_("den")])
    kb.op("dve", lambda e: e.reciprocal(out=W["den"][:], in_=W["den"][:]), r=[d_("den")], w=[d_("den")])
    ts("dve", A_("nre"), A_("abr"), -1.0, ALU.add, [d_("abr")], [d_("nre")])
    tt("dve", A_("t1"), A_("nre"), A_("lr"), ALU.mult, [d_("nre"), d_("lr")], [d_("t1")])
    tt("pool", A_("t2"), A_("abi"), lim, ALU.mult, [d_("abi"), dprm], [d_("t2")])
    tt("dve", A_("t1"), A_("t1"), A_("t2"), ALU.add, [d_("t1"), d_("t2")], [d_("t1")])
    tt("dve", A_("fre"), A_("t1"), A_("den"), ALU.mult, [d_("t1"), d_("den")], [d_("fre")])
    tt("dve", A_("t1"), A_("abi"), A_("lr"), ALU.mult, [d_("abi"), d_("lr")], [d_("t1")])
    tt("pool", A_("t2"), A_("nre"), lim, ALU.mult, [d_("nre"), dprm], [d_("t2")])
    tt("dve", A_("t1"), A_("t1"), A_("t2"), ALU.subtract, [d_("t1"), d_("t2")], [d_("t1")])
    tt("dve", A_("fim"), A_("t1"), A_("den"), ALU.mult, [d_("t1"), d_("den")], [d_("fim")])
    fre_b = W["fre"][:].unsqueeze(2).broadcast_to([128, NDG, 16])
    fim_b = W["fim"][:].unsqueeze(2).broadcast_to([128, NDG, 16])
    tt("dve", A_("b1"), bre, fre_b, ALU.mult, [dprm, d_("fre")], [d_("b1")])
    tt("pool", A_("b2"), bim, fim_b, ALU.mult, [dprm, d_("fim")], [d_("b2")])
    tt("dve", A_("bbr"), A_("b1"), A_("b2"), ALU.subtract, [d_("b1"), d_("b2")], [d_("bbr")])
    tt("dve", A_("b1"), bim, fre_b, ALU.mult, [dprm, d_("fre")], [d_("b1")])
    tt("pool", A_("b2"), bre, fim_b, ALU.mult, [dprm, d_("fim")], [d_("b2")])
    tt("dve", A_("bbi"), A_("b1"), A_("b2"), ALU.add, [d_("b1"), d_("b2")], [d_("bbi")])

    def powers(Pr, Pi, n, br, bi, dbr, dbi):
        kb.op("pool", lambda e: e.memset(Pr[:, 0, :], 1.0), w=[dP])
        kb.op("pool", lambda e: e.memset(Pi[:, 0, :], 0.0), w=[dP])
        kb.op("act", lambda e: e.activation(out=Pr[:, 1, :], in_=br, func=AF.Copy), r=[dbr], w=[dP])
        kb.op("act", lambda e: e.activation(out=Pi[:, 1, :], in_=bi, func=AF.Copy), r=[dbi], w=[dP])
        for j in range(1, n - 1):
            tt("dve", A_("t1"), Pr[:, j, :], br, ALU.mult, [dP, dbr], [d_("t1")])
            tt("pool", A_("t2"), Pi[:, j, :], bi, ALU.mult, [dP, dbi], [d_("t2")])
            tt("dve", Pr[:, j + 1, :], A_("t1"), A_("t2"), ALU.subtract, [d_("t1"), d_("t2")], [dP])
            tt("dve", A_("t1"), Pr[:, j, :], bi, ALU.mult, [dP, dbi], [d_("t1")])
            tt("pool", A_("t2"), Pi[:, j, :], br, ALU.mult, [dP, dbr], [d_("t2")])
            tt("dve", Pi[:, j + 1, :], A_("t1"), A_("t2"), ALU.add, [d_("t1"), d_("t2")], [dP])

    powers(PAr, PAi, C1 + 1, A_("abr"), A_("abi"), d_("abr"), d_("abi"))
    powers(Nr, Ni, 8, A_("ivr"), A_("ivi"), d_("ivr"), d_("ivi"))
    for j in range(C1 + 1):
        kb.op("act", lambda e, j=j: e.activation(out=PDr[:, j, :], in_=PAr[:, C1 - j, :], func=AF.Copy), r=[dP],
              w=[dP])
        kb.op("pool", lambda e, j=j: e.tensor_copy(out=PDi[:, j, :], in_=PAi[:, C1 - j, :]), r=[dP], w=[dP])
    for ri, PX in enumerate((PAr, PAi)):
        for gh in range(2):
            for dr in range(2):
                kb.dma("sp", self.s5a[l, ri, dr * 64:(dr + 1) * 64, gh * 32:(gh + 1) * 32],
                       PX[gh * 64:(gh + 1) * 64, C1, dr * 32:(dr + 1) * 32], r=[dP], w=[self.d_s5a[l]])

    G = GCH
    pl_r = Rot(kb, "pl", 8, [128, G, 8, 16], F32)
    pk_r = Rot(kb, "plk", 4, [128, G, 8, 16], F32)
    tm_r = Rot(kb, "ptm", 8, [128, G, 8, 16], F32)
    stg = [kb.sb(f"stg{gh}", [128, G, TI_SI_RE, 128], BF16) for gh in range(2)]
    dstg = [Dep(), Dep()]
    sib_r = Rot(kb, "sib", 4, [128, G, 128], BF16)
    t0_r = [Rot(kb, f"t0t{gh}", 2, [128, 128], F32) for gh in range(2)]
    pX = [self.pT, self.pO]

    pending = []

    def flush_pending():
        while pending:
            pending.pop(0)()

    def outer(Mr, Mi, dM, Pr, Pi, j0, dg0, neg_im=False, keep=False):
        next(self.cast_gen, None)
        mr = Mr[:, dg0:dg0 + G, :].unsqueeze(2).broadcast_to([128, G, 8, 16])
        mi = Mi[:, dg0:dg0 + G, :].unsqueeze(2).broadcast_to([128, G, 8, 16])
        pr = Pr[:, j0:j0 + 8, dg0:dg0 + G].rearrange("n j g -> n g j").unsqueeze(3).broadcast_to([128, G, 8, 16])
        pi = Pi[:, j0:j0 + 8, dg0:dg0 + G].rearrange("n j g -> n g j").unsqueeze(3).broadcast_to([128, G, 8, 16])
        re, dre = (pk_r if keep else pl_r).next()
        im, dim = (pk_r if keep else pl_r).next()
        a, da = tm_r.next()
        b, db = tm_r.next()
        a2, da2 = tm_r.next()
        b2, db2 = tm_r.next()
        tt("dve", a[:], mr, pr, ALU.mult, dM + [dP], [da])
        tt("pool", b[:], mi, pi, ALU.mult, dM + [dP], [db])
        tt("pool", a2[:], mr, pi, ALU.mult, dM + [dP], [da2])
        tt("dve", b2[:], mi, pr, ALU.mult, dM + [dP], [db2])
        flush_pending()

        def fin():
            tt("dve", re[:], a[:], b[:], ALU.subtract, [da, db], [dre])
            if neg_im:
                kb.op("dve", lambda e: e.scalar_tensor_tensor(out=im[:], in0=a2[:], scalar=-1.0, in1=b2[:],
                                                             op0=ALU.mult, op1=ALU.subtract), r=[da2, db2], w=[dim])
            else:
                tt("dve", im[:], a2[:], b2[:], ALU.add, [da2, db2], [dim])

        pending.append(fin)
        return re, dre, im, dim

    BB = (W["bbr"], W["bbi"], [dW["bbr"], dW["bbi"]])
    CC = (cre, cim, [dprm])

    def fl(p, gh, gi):
        return p[gh * 64:(gh + 1) * 64, gi].rearrange("n s c -> n (s c)")

    def toep(K, Q, gh, gi):
        flush_pending()
        kre, dkre, kim, dkim = K
        qre, dqre, qim, dqim = Q
        pT, dpT = pX[gh].next()
        kb.op("pe", lambda e: e.matmul(pT[:, 0:128], fl(kre, gh, gi), fl(qre, gh, gi), start=True, stop=False),
              r=[dkre, dqre], w=[dpT])
        kb.op("pe", lambda e: e.matmul(pT[:, 0:128], fl(kim, gh, gi), fl(qim, gh, gi), start=False, stop=True),
              r=[dkim, dqim], w=[dpT])
        return pT, dpT

    castgen = self.cast_gen
    for g0 in range(0, 32, G):
        f0, b0 = g0, 32 + g0
        K0f = outer(*BB[:2], BB[2], Nr, Ni, 0, f0)
        Q0f = outer(*CC[:2], CC[2], PAr, PAi, 0, f0, neg_im=True)
        K0b = outer(*BB[:2], BB[2], PAr, PAi, 0, b0)
        Q0b = outer(*CC[:2], CC[2], Nr, Ni, 0, b0, neg_im=True)
        for gi in range(G):
            for gh in range(2):
                g = gh * 32 + g0 + gi
                pT, dpT = toep(K0f, Q0f, gh, gi)
                t0, dt0 = t0_r[gh].next()
                tt("dve", t0[:], pT[:, 0:128], self.ctv(CT_MF, 128), ALU.mult, [dpT, self.d_const], [dt0])
                pT2, dpT2 = toep(K0b, Q0b, gh, gi)
                t1_, dt1_ = t0_r[gh].next()
                tt("dve", t1_[:], pT2[:, 0:128], self.ctv(CT_MB, 128), ALU.mult, [dpT2, self.d_const], [dt1_])
                tt("pool", t0[:], t0[:], t1_[:], ALU.add, [dt0, dt1_], [dt0])
                kb.op("dve", lambda e, gi=gi, gh=gh, g=g, t0=t0: e.scalar_tensor_tensor(
                    out=stg[gh][:, gi, TI_T0, :], in0=self.idf[:], scalar=dcol[:, g:g + 1], in1=t0[:],
                    op0=ALU.mult, op1=ALU.add), r=[dt0, ddcol, self.d_const], w=[dstg[gh]])
        def so_tiles(X, dr, J):
            flush_pending()
            for ri in range(2):
                pl, dpl = X[2 * ri], X[2 * ri + 1]
                for gi in range(G):
                    for gh in range(2):
                        pT, dpT = pX[gh].next()
                        kb.op("pe", lambda e, pl=pl, gi=gi, gh=gh, pT=pT: e.transpose(
                            pT[:, 0:64], fl(pl, gh, gi), self.idf[gh * 64:(gh + 1) * 64, gh * 64:(gh + 1) * 64]),
                              r=[dpl, self.d_const], w=[dpT])
                        ti = (TI_SO_RE if ri == 0 else TI_SO_IM) + J
                        if gi % 2 == 0:
                            kb.op("act", lambda e, gi=gi, gh=gh, pT=pT, ti=ti, dr=dr: e.activation(
                                out=stg[gh][:, gi, ti, dr * 64:(dr + 1) * 64], in_=pT[:, 0:64], func=AF.Copy),
                                  r=[dpT], w=[dstg[gh]])
                        else:
                            kb.op("dve", lambda e, gi=gi, gh=gh, pT=pT, ti=ti, dr=dr: e.tensor_copy(
                                out=stg[gh][:, gi, ti, dr * 64:(dr + 1) * 64], in_=pT[:, 0:64]), r=[dpT],
                                  w=[dstg[gh]])

        def si_tiles(Y, dr, I):
            flush_pending()
            for ri in range(2):
                pl, dpl = Y[2 * ri], Y[2 * ri + 1]
                sb_, dsb = sib_r.next()
                kb.op("act", lambda e, pl=pl, sb_=sb_: e.activation(
                    out=sb_[:], in_=pl[:].rearrange("n g s c -> n g (s c)"), func=AF.Copy), r=[dpl], w=[dsb])
                ti = (TI_SI_RE if ri == 0 else TI_SI_IM) + I
                for gh in range(2):
                    ga = gh * 32 + g0
                    kb.dma("sp", self.s5m[l, ga:ga + G, dr * 64:(dr + 1) * 64, ti, :].rearrange("g p c -> p g c"),
                           sb_[gh * 64:(gh + 1) * 64], r=[dsb], w=[self.d_s5m[l]])

        so_tiles(K0b, 1, 0)
        KAf = outer(*BB[:2], BB[2], PDr, PDi, C1 - 7, f0, keep=True)
        QAb = outer(*CC[:2], CC[2], PDr, PDi, C1 - 7, b0, neg_im=True, keep=True)
        for Dd in range(1, NB):
            QD = outer(*CC[:2], CC[2], PAr, PAi, 8 * (Dd - 1) + 1, f0, neg_im=True)
            KD_ = outer(*BB[:2], BB[2], PAr, PAi, 8 * (Dd - 1) + 1, b0)
            for gi in range(G):
                for gh in range(2):
                    pT, dpT = toep(KAf, QD, gh, gi)
                    kb.op("act", lambda e, gi=gi, gh=gh, pT=pT, Dd=Dd: e.activation(
                        out=stg[gh][:, gi, TI_TF + Dd, :], in_=pT[:, 0:128], func=AF.Copy), r=[dpT], w=[dstg[gh]])
                    pT2, dpT2 = toep(KD_, QAb, gh, gi)
                    kb.op("act", lambda e, gi=gi, gh=gh, pT2=pT2, Dd=Dd: e.activation(
                        out=stg[gh][:, gi, TI_TB + Dd, :], in_=pT2[:, 0:128], func=AF.Copy), r=[dpT2], w=[dstg[gh]])
            si_tiles(QD, 0, Dd - 1)
        for J in range(NB):
            if J < NB - 1:
                Xf = outer(*BB[:2], BB[2], PDr, PDi, 1 + 8 * J, f0)
            else:
                Xf = KAf
            so_tiles(Xf, 0, J)
            if J >= 1:
                Xb = outer(*BB[:2], BB[2], PAr, PAi, 8 * J, b0)
                so_tiles(Xb, 1, J)
        for gh in range(2):
            ga = gh * 32 + g0
            kb.dma("sp", self.s5m[l, ga:ga + G, :, 0:TI_SI_RE, :].rearrange("g p t c -> p g t c"), stg[gh][:],
                   r=[dstg[gh]], w=[self.d_s5m[l]])
        Yf = outer(*CC[:2], CC[2], PAr, PAi, 8 * (NB - 1) + 1, f0, neg_im=True)
        si_tiles(Yf, 0, NB - 1)
        for I in range(NB):
            Yb = outer(*CC[:2], CC[2], PDr, PDi, 8 * I, b0, neg_im=True)
            si_tiles(Yb, 1, I)


def _mixer_s5(self, l):
    kb = self.kb
    self.phase_begin()
    m0 = kb.mark()
    u2a = kb.sb("u2a", [128, NBLK, 64], BF16)
    m1 = kb.mark()
    kb.release(m0)
    Hb = kb.sb("Hb", [128, NK, 64, 2], F32)
    kb.release(m1)
    d_u2a = self.d_u2a
    u2 = kb.sb("u2", [128, 64, NB, NK], BF16)
    d_u2 = [Dep() for _ in range(64)]
    Sb = kb.sb("Sb", [128, NK, 64, 2], F32)
    Hbf = kb.sb("Hbf", [128, 64, 2, NK], BF16)
    d_Sall = Dep()
    d_Hf, d_Hb2, d_Hbf = Dep(), Dep(), Dep()
    Ar = kb.sb("Ar", [128, 64], F32)
    Ai = kb.sb("Ai", [128, 64], F32)
    dA = Dep()
    G1, G2 = 8, 2
    t1_r = Rot(kb, "s5t1", 2, [128, G1, NT1, 128], BF16)
    t2_r = Rot(kb, "s5t2", 3, [128, G2, NT2, 128], BF16)
    tmpf = [kb.sb(f"scf{i}", [128, 64, 2], F32) for i in range(2)]
    tmpb = [kb.sb(f"scb{i}", [128, 64, 2], F32) for i in range(2)]
    dtf = [Dep(), Dep()]
    dtb = [Dep(), Dep()]
    kb.dma("sp", Ar[:], self.s5a[l, 0], r=[self.d_s5a[l]], w=[dA])
    kb.dma("sp", Ai[:], self.s5a[l, 1], r=[self.d_s5a[l]], w=[dA])
    nc = self.nc

    def load_t1(i):
        t, dt_ = t1_r.next()
        kb.dma("sp", t[:], self.s5m[l, i * G1:(i + 1) * G1, :, 0:NT1, :].rearrange("g p t c -> p g t c"),
               r=[self.d_s5m[l]], w=[dt_])
        return t, dt_

    def load_t2(i):
        t, dt_ = t2_r.next()
        kb.dma("sp", t[:], self.s5m[l, i * G2:(i + 1) * G2, :, NT1:NT1 + NT2, :].rearrange("g p t c -> p g t c"),
               r=[self.d_s5m[l]], w=[dt_])
        return t, dt_

    nx1 = [load_t1(0)]
    u2a_v = u2a[:].rearrange("p (k j) g -> p g j k", j=NB)

    for i in range(64 // G1):
        tl, dtl = nx1[i]
        if i + 1 < 64 // G1:
            nx1.append(load_t1(i + 1))
        for gi in range(G1):
            g = i * G1 + gi
            pGa, dpGa = self.pA.next() if g % 2 == 0 else self.pU.next()
            kb.op("pe", lambda e, g=g, pGa=pGa: e.matmul(pGa[:, 0:NBLK].rearrange("p (j k) -> p j k", k=NK),
                                                        self.idb[:], u2a_v[:, g], start=True, stop=True),
                  r=[d_u2a, self.d_const], w=[dpGa])
            if g % 2 == 0:
                kb.op("act", lambda e, g=g, pGa=pGa: e.activation(out=u2[:, g].rearrange("p j k -> p (j k)"),
                                                                  in_=pGa[:, 0:NBLK], func=AF.Copy), r=[dpGa],
                      w=[d_u2[g]])
            else:
                kb.op("dve", lambda e, g=g, pGa=pGa: e.tensor_copy(out=u2[:, g].rearrange("p j k -> p (j k)"),
                                                                   in_=pGa[:, 0:NBLK]), r=[dpGa], w=[d_u2[g]])
            pS, dpS = self.pO.next()
            for ri in range(2):
                for hf in range(2):
                    for J in range(NB):
                        ti = (TI_SO_RE if ri == 0 else TI_SO_IM) + J
                        rhs = u2[:, g, J, :] if hf == 0 else u2[:, g, J, ::-1]
                        kb.op("pe", lambda e, ri=ri, J=J, ti=ti, gi=gi, pS=pS, tl=tl, hf=hf, rhs=rhs: e.matmul(
                            pS[hf * 64:(hf + 1) * 64, ri * NK:(ri + 1) * NK], tl[:, gi, ti, hf * 64:(hf + 1) * 64], rhs,
                            start=(J == 0), stop=(J == NB - 1)), r=[dtl, d_u2[g]], w=[dpS])
            src_ = pS[:, 0:2 * NK].rearrange("p (r k) -> p k r", k=NK)
            if g % 2 == 1:
                kb.op("act", lambda e, g=g, src_=src_: e.activation(out=Sb[:, :, g, :], in_=src_, func=AF.Copy),
                      r=[dpS], w=[d_Sall])
            else:
                kb.op("dve", lambda e, g=g, src_=src_: e.tensor_copy(out=Sb[:, :, g, :], in_=src_), r=[dpS],
                      w=[d_Sall])
    nx2 = [load_t2(0), load_t2(1)]

    dHs = [d_Hf, d_Hb2]
    GH = [slice(0, 32), slice(32, 64)]
    tmps = [tmpf, tmpb]
    dtmps = [dtf, dtb]
    kb.op("pool", lambda e: e.memset(Hb[:, 0], 0.0), w=[d_Hf, d_Hb2, d_u2a])
    pSc, dpSc = self.pT.t[0], self.pT.d[0]
    ArP = pSc[:, 0:64]
    AiP = pSc[:, 64:128]
    T13P = [pSc[:, 128:256].rearrange("p (g r) -> p g r", r=2), pSc[:, 256:384].rearrange("p (g r) -> p g r", r=2)]
    kb.op("act", lambda e: e.activation(out=ArP, in_=Ar[:], func=AF.Copy), r=[dA], w=[dpSc])
    kb.op("act", lambda e: e.activation(out=AiP, in_=Ai[:], func=AF.Copy), r=[dA], w=[dpSc])
    dT13P = [Dep(), Dep()]
    for k in range(NK - 1):
        ops = [[], []]
        for ci in range(2):
            gs_ = GH[ci]
            T13, T24 = T13P[ci], tmps[ci][1]
            dtmp = [dT13P[ci], dtmps[ci][1]]
            dH = dHs[ci]
            Arb = ArP[:, gs_].unsqueeze(2).broadcast_to([128, 32, 2])
            Aib = AiP[:, gs_].unsqueeze(2).broadcast_to([128, 32, 2])
            ops[ci] = [
                (lambda e, k=k, T13=T13, Arb=Arb, gs_=gs_: e.tensor_tensor(out=T13[:, gs_], in0=Hb[:, k, gs_], in1=Arb,
                                                                           op=ALU.mult), [dH, dpSc], [dtmp[0]]),
                (lambda e, k=k, T24=T24, Aib=Aib, gs_=gs_: e.tensor_tensor(out=T24[:, gs_], in0=Hb[:, k, gs_], in1=Aib,
                                                                           op=ALU.mult), [dH, dpSc], [dtmp[1]]),
                (lambda e, k=k, T13=T13, gs_=gs_: e.tensor_tensor(out=T13[:, gs_], in0=T13[:, gs_], in1=Sb[:, k, gs_],
                                                                  op=ALU.add), [dtmp[0], d_Sall], [dtmp[0]]),
                (lambda e, k=k, T13=T13, T24=T24, gs_=gs_: e.tensor_tensor(
                    out=Hb[:, k + 1, gs_, 0], in0=T13[:, gs_, 0], in1=T24[:, gs_, 1], op=ALU.subtract),
                 [dtmp[0], dtmp[1]], [dH, d_u2a]),
                (lambda e, k=k, T13=T13, T24=T24, gs_=gs_: e.tensor_tensor(
                    out=Hb[:, k + 1, gs_, 1], in0=T13[:, gs_, 1], in1=T24[:, gs_, 0], op=ALU.add),
                 [dtmp[0], dtmp[1]], [dH, d_u2a]),
            ]
        for j in range(5):
            for ci in range(2):
                fn, r_, w_ = ops[ci][j]
                kb.op("dve", fn, r=r_, w=w_)
    kb.op("act", lambda e: e.activation(out=Hbf[0:64], in_=Hb[0:64].rearrange("p k g r -> p g r k"), func=AF.Copy),
          r=[d_Hf, d_Hb2, d_u2a], w=[d_Hbf])
    kb.op("pool", lambda e: e.tensor_copy(out=Hbf[64:128], in_=Hb[64:128, ::-1].rearrange("p k g r -> p g r k")),
          r=[d_Hf, d_Hb2, d_u2a], w=[d_Hbf])

    for i in range(64 // G2):
        tl, dtl = nx2[i]
        if i + 2 < 64 // G2:
            nx2.append(load_t2(i + 2))
        for gi in range(G2):
            g = i * G2 + gi
            pY, dpY = self.pA.next() if g % 2 == 0 else self.pU.next()
            pYv = pY[:, 0:NBLK].rearrange("p (i k) -> p i k", k=NK)
            o = NT1
            mms = [(TI_T0 - o, pYv, u2[:, g, :, :])]
            for Dd in range(1, NB):
                mms.append((TI_TF + Dd - o, pYv[:, Dd:NB, :], u2[:, g, 0:NB - Dd, :]))
                mms.append((TI_TB + Dd - o, pYv[:, 0:NB - Dd, :], u2[:, g, Dd:NB, :]))
            for I in range(NB):
                mms.append((TI_SI_RE + I - o, pYv[:, I, :], Hbf[:, g, 0, :]))
                mms.append((TI_SI_IM + I - o, pYv[:, I, :], Hbf[:, g, 1, :]))
            for n_, (ti, oap, rap) in enumerate(mms):
                kb.op("pe", lambda e, ti=ti, oap=oap, rap=rap, n_=n_, tl=tl, gi=gi: e.matmul(
                    oap, tl[:, gi, ti, :], rap, start=(n_ == 0), stop=(n_ == len(mms) - 1)),
                      r=[dtl, d_u2[g], d_Hbf], w=[dpY])
            kb.op("act", lambda e, g=g, pYv=pYv: e.activation(out=u2a_v[:, g], in_=pYv, func=AF.Gelu_apprx_tanh),
                  r=[dpY, d_Hbf], w=[d_u2a])
    zsv = self.zs.rearrange("(b s) (p g) -> (s p) b g", s=8, p=16)
    self.zstore_evs = []
    with nc.allow_non_contiguous_dma(reason="128B runs"):
        for b0 in range(0, NBLK, 16):
            ev = kb.dma("sp", zsv[:, b0:b0 + 16, :], u2a[:, b0:b0 + 16, :], r=[d_u2a], w=[self.d_scr["zs"]])
            self.zstore_evs.append(ev)


Prog.s5_prologue = _s5_prologue
Prog._s5_prologue_layer = _s5_prologue_layer
Prog.mixer_s5 = _mixer_s5


def _mixer_b(self, l, src, src_dep, dst, dst_dep):
    kb = self.kb
    zev = list(getattr(self, "zstore_evs", []))
    self.phase_begin(dma=False)
    sbt = kb.sb
    z_r = Rot(kb, "zb", 2, [128, D], BF16)
    zT = sbt("zT", [128, 8, T], BF16)
    d_zT = Dep()
    gatedT = sbt("gatedT", [128, 8, T], BF16)
    d_gatedT = Dep()
    m1T = sbt("m1T", [128, 8, T], BF16)
    on = sbt("on", [128, D], F32)
    d_on = Dep()
    assert kb.sb_off - self.base_mark >= NBLK * 64 * 2
    first = {"zt": True, "zT": True, "gatedT": True, "m1T": True, "on": True}

    def xtra(name):
        if first.get(name):
            first[name] = False
            return zev
        return ()

    xT_r = Rot(kb, "xTb", 2, [128, 8, T], BF16)
    wslot = [sbt(f"wsl{i}", [128, 8, 512], BF16) for i in range(4)]
    dslot = [Dep() for _ in range(4)]
    rope_r = Rot(kb, "ropeb", 1, [128, 2, T], F32)
    kT_r = Rot(kb, "kTb", 1, [128, 4, T], BF16)
    qT = sbt("qT", [128, 4, T], BF16)
    qxf = sbt("qxf", [128, 4, T], BF16)
    qxb = sbt("qxb", [128, 4, T], BF16)
    d_qT, d_qxf, d_qxb = Dep(), Dep(), Dep()
    tmp_r = Rot(kb, "rtmpb", 3, [128, T], F32)
    v_r = Rot(kb, "vb", 2, [128, D], BF16)
    xres_r = Rot(kb, "xres", 2, [128, D], F32)
    ST_r = Rot(kb, "ST", 2, [128, 1024], BF16)
    kz_r = Rot(kb, "kzb", 2, [128, 512], BF16)
    sg_r = Rot(kb, "sg", 2, [128, D], F32)
    gated_r = Rot(kb, "gated", 2, [128, D], BF16)
    d_m1 = [Dep() for _ in range(8)]
    gr_r = Rot(kb, "gr", 2, [128, T], F32)
    s2_r = Rot(kb, "s2", 2, [128, T], F32)
    gs_r = Rot(kb, "gs", 2, [128, T], F32)
    bg = sbt("bg", [128, 16], F32)
    d_bg = Dep()
    gst_r = Rot(kb, "gst", 3, [128, 80], F32)
    self.alloc_ln(l, 1, nrr=1)
    kb.dma("sp", bg[:], self.bgate[l], w=[d_bg])
    F_, dF = self.fstate, self.d_fstate
    fbf, dfbf = self.fbf, self.d_fbf
    kb.op("pool", lambda e: e.memset(F_[:], 0.0), w=[dF])
    kb.op("pool", lambda e: e.memset(fbf[:], 0.0), w=[dfbf])
    wbf = self.wbf[l]

    def LW(name, h, slot):
        o = W_OFF[name]
        srcap = wbf[:, o:o + 8192].rearrange("p (k c) -> p k c", c=1024)[:, :, 512 * h:512 * h + 512]
        kb.dma("sp", wslot[slot][:], srcap, r=[self.d_wbf[l]], w=[dslot[slot]])

    def evac_T(pTb, dpT, out_ap, out_dep, eng, extra=()):
        src_ = pTb[:, 0:512].rearrange("p (a b) -> p a b", b=128)
        if eng == "act":
            kb.op("act", lambda e: e.activation(out=out_ap, in_=src_, func=AF.Copy), r=[dpT], w=[out_dep], extra=extra)
        else:
            kb.op("dve", lambda e: e.tensor_copy(out=out_ap, in_=src_), r=[dpT], w=[out_dep], extra=extra)

    NTL = L_SEQ // T

    def load_xT(t):
        xT, dxT = xT_r.next()
        kb.dma("sp", xT[:], self.xts[:, :, t * T:(t + 1) * T], r=[self.d_scr["xts"]], w=[dxT])
        return xT, dxT

    def load_rope(t):
        rp, drp = rope_r.next()
        kb.dma("sp", rp[:], self.rope[:, :, t * T:(t + 1) * T].rearrange("a p t -> p a t"), w=[drp])
        return rp, drp

    def load_kT(t):
        kT, dkT = kT_r.next()
        kb.dma("sp", kT[:], self.ks[:, :, t * T:(t + 1) * T], r=[self.d_scr["ks"]], w=[dkT])
        return kT, dkT

    LW("wq", 0, 0)
    LW("wq", 1, 1)
    nx_xT = load_xT(0)
    nx_rp = load_rope(0)
    nx_kT = load_kT(0)
    for t in range(NTL):
        LW("wg", 0, 2)
        LW("wg", 1, 3)
        xT, dxT = nx_xT
        rp, drp = nx_rp
        kT, dkT = nx_kT
        if t + 1 < NTL:
            nx_xT = load_xT(t + 1)
        for f in range(4):
            pA, dpA = self.pA.next()
            pU, dpU = self.pU.next()
            for k in range(8):
                kb.op("pe", lambda e, k=k, f=f, pA=pA: e.matmul(pA[:], wslot[0][:, k, f * 128:(f + 1) * 128], xT[:, k, :],
                                                              start=(k == 0), stop=(k == 7)), r=[dslot[0], dxT], w=[dpA])
            for k in range(8):
                kb.op("pe", lambda e, k=k, f=f, pU=pU: e.matmul(pU[:], wslot[1][:, k, f * 128:(f + 1) * 128], xT[:, k, :],
                                                              start=(k == 0), stop=(k == 7)), r=[dslot[1], dxT], w=[dpU])

            def mk_also(f):
                def fx(t1, dt1):
                    t1v = t1[:].rearrange("p (c i) -> p c i", i=128)
                    xfv = self.ctv(CT_XF + f * 128, 128).unsqueeze(1).broadcast_to([128, 4, 128])
                    xbv = self.ctv(CT_XB + f * 128, 128).unsqueeze(1).broadcast_to([128, 4, 128])
                    kb.op("dve", lambda e: e.tensor_tensor(out=qxf[:, f, :].rearrange("p (c i) -> p c i", i=128),
                                                          in0=t1v, in1=xfv, op=ALU.mult), r=[dt1, self.d_const],
                          w=[d_qxf])
                    kb.op("pool", lambda e: e.tensor_tensor(out=qxb[:, f, :].rearrange("p (c i) -> p c i", i=128),
                                                           in0=t1v, in1=xbv, op=ALU.mult), r=[dt1, self.d_const],
                          w=[d_qxb])
                return [fx]

            self.rotary(pA, dpA, pU, dpU, rp[:, 0, :], rp[:, 1, :], drp, tmp_r, qT[:, f, :], d_qT, also=mk_also(f))
        LW("glu_v", 0, 0)
        LW("glu_g", 0, 1)
        if t + 1 < NTL:
            nx_rp = load_rope(t + 1)
        st_c = {}

        def stage_A(c):
            cg = 4 * t + c
            rows = slice(cg * 128, (cg + 1) * 128)
            cs = slice(c * 128, (c + 1) * 128)
            v, dv = v_r.next()
            kb.dma("sp", v[:], self.vs[rows, :], r=[self.d_scr["vs"]], w=[dv])
            zt, dzt = z_r.next()
            kb.dma("sp", zt[:], self.zs[rows, :], r=[self.d_scr["zs"]], w=[dzt])
            ST, dST = ST_r.next()
            decv = self.ctv(CT_DEC, 1024).rearrange("p (hp h2 i) -> p hp h2 i", h2=2, i=128)
            pSs = [(self.pO.t[0], self.pO.d[0]), (self.pO.t[1], self.pO.d[1])]
            for hp in range(4):
                for h2 in range(2):
                    pS, dpS = pSs[h2]
                    b = h2 * 64
                    kb.op("pe", lambda e, b=b, hp=hp, pS=pS: e.matmul(
                        pS[:, hp * 128:(hp + 1) * 128], kT[b:b + 64, hp, cs], qT[b:b + 64, hp, cs], start=True,
                        stop=True), r=[dkT, d_qT], w=[dpS])
            for h2 in range(2):
                pS, dpS = pSs[h2]
                kb.op("dve", lambda e, h2=h2, pS=pS: e.tensor_tensor(
                    out=ST[:, h2 * 512:(h2 + 1) * 512].rearrange("p (a i) -> p a i", i=128),
                    in0=pS[:].rearrange("p (a i) -> p a i", i=128), in1=decv[:, :, h2, :], op=ALU.mult),
                      r=[dpS, self.d_const], w=[dST])
            kz, dkz = self.k_transpose_scaled(kT, dkT, c, CT_ZF, kz_r)
            sg, d_sg = sg_r.next()
            for half in range(2):
                pG, dpG = (self.pO.t[half], self.pO.d[half])
                for k in range(8):
                    kb.op("pe", lambda e, k=k, pG=pG, half=half: e.matmul(pG[:], xT[:, k, cs], wslot[2 + half][:, k, :],
                                                                         start=(k == 0), stop=(k == 7)),
                          r=[dxT, dslot[2 + half]], w=[dpG])
                kb.op("act", lambda e, pG=pG, half=half, sg=sg: e.activation(out=sg[:, half * 512:(half + 1) * 512],
                                                                             in_=pG[:], func=AF.Silu), r=[dpG], w=[d_sg])
            for grp in range(2):
                pT, dpT = self.pT.next()
                pTb = pT[:].bitcast(BF16)
                for kk in range(4):
                    kf = 4 * grp + kk
                    kb.op("pe", lambda e, kk=kk, kf=kf, pTb=pTb, zt=zt: e.transpose(
                        pTb[:, kk * 128:(kk + 1) * 128], zt[:, kf * 128:(kf + 1) * 128], self.idb[:]),
                          r=[dzt, self.d_const], w=[dpT])
                evac_T(pTb, dpT, zT[:, 4 * grp:4 * grp + 4, cs], d_zT, "act")
            st_c[c] = dict(v=v, dv=dv, ST=ST, dST=dST, kz=kz, dkz=dkz, sg=sg, d_sg=d_sg)

        def stage_B_pe(c):
            cg = 4 * t + c
            cs = slice(c * 128, (c + 1) * 128)
            S = st_c[c]
            v, dv, ST, dST = S["v"], S["dv"], S["ST"], S["dST"]
            pOs = [(self.pA.t[c % 2], self.pA.d[c % 2]), (self.pU.t[c % 2], self.pU.d[c % 2])]
            for hp in range(4):
                for h2 in range(2):
                    pOo, dpOo = pOs[h2]
                    h = 2 * hp + h2
                    b = h2 * 64
                    oap = pOo[:, hp * 128:(hp + 1) * 128]
                    kb.op("pe", lambda e, oap=oap, h2=h2, hp=hp, h=h: e.matmul(
                        oap, ST[:, h2 * 512 + hp * 128:h2 * 512 + (hp + 1) * 128], v[:, h * 128:(h + 1) * 128],
                        start=True, stop=False), r=[dST, dv], w=[dpOo])
                    kb.op("pe", lambda e, oap=oap, b=b, hp=hp: e.matmul(oap, qxf[b:b + 64, hp, cs],
                                                                       fbf[b:b + 64, hp * 128:(hp + 1) * 128],
                                                                       start=False, stop=False),
                          r=[d_qxf, dfbf], w=[dpOo])
                    kb.op("pe", lambda e, oap=oap, b=b, hp=hp, cg=cg: e.matmul(
                        oap, qxb[b:b + 64, hp, cs], self.bs[b:b + 64, cg, hp * 128:(hp + 1) * 128], start=False,
                        stop=True), r=[d_qxb, self.d_bs[cg]], w=[dpOo])
            S["pOs"] = pOs
            self.kv_state_step(S["kz"], S["dkz"], v, dv, F_, dF)
            kb.op("act", lambda e: e.activation(out=fbf[:], in_=F_[:], func=AF.Copy), r=[dF], w=[dfbf])

        def stage_B_post1(c):
            S = st_c[c]
            pOs = S["pOs"]
            gst, dgst = gst_r.next()
            dsth = [Dep() for _ in range(8)]
            dagh = [Dep() for _ in range(8)]
            for hg in range(2):
                pOo, dpOo = pOs[hg]
                for hh in range(4):
                    h = 2 * hh + hg
                    kb.op("dve", lambda e, h=h, hh=hh, pOo=pOo: e.bn_stats(out=gst[:, 6 * h:6 * h + 6],
                                                                           in_=pOo[:, hh * 128:(hh + 1) * 128]),
                          r=[dpOo], w=[dsth[h], dgst])
            for h in range(8):
                kb.op("dve", lambda e, h=h: e.bn_aggr(out=gst[:, 48 + 2 * h:50 + 2 * h],
                                                      in_=gst[:, 6 * h:6 * h + 6]), r=[dsth[h]], w=[dagh[h]])
            varv = gst[:, 48:64].rearrange("p (h t) -> p h t", t=2)[:, :, 1]
            drs = Dep()
            kb.op("dve", lambda e: e.tensor_scalar(out=gst[:, 64:72], in0=varv, scalar1=float(EPS), scalar2=None,
                                                  op0=ALU.add), r=dagh, w=[drs])
            kb.op("act", lambda e: e.activation(out=gst[:, 64:72], in_=gst[:, 64:72], func=AF.Sqrt), r=[drs],
                  w=[drs])
            S.update(gst=gst, dgst=dgst, dagh=dagh, drs=drs)

        def stage_B_post2(c):
            S = st_c[c]
            pOs, gst, dgst, dagh, drs = S["pOs"], S["gst"], S["dgst"], S["dagh"], S["drs"]
            kb.op("dve", lambda e: e.reciprocal(out=gst[:, 72:80], in_=gst[:, 64:72]), r=[drs], w=[drs])
            for hg in range(2):
                pOo, dpOo = pOs[hg]
                for hh in range(4):
                    h = 2 * hh + hg
                    kb.op("dve", lambda e, h=h, hh=hh, pOo=pOo: e.tensor_scalar(
                        out=on[:, h * 128:(h + 1) * 128], in0=pOo[:, hh * 128:(hh + 1) * 128],
                        scalar1=gst[:, 48 + 2 * h:49 + 2 * h], scalar2=gst[:, 72 + h:73 + h], op0=ALU.subtract,
                        op1=ALU.mult), r=[dpOo, drs, dagh[h]], w=[d_on, dgst], extra=xtra("on"))
            gated, dgated = gated_r.next()
            kb.op("pool", lambda e: e.tensor_tensor(out=gated[:], in0=S["sg"][:], in1=on[:], op=ALU.mult),
                  r=[S["d_sg"], d_on], w=[dgated])
            S["gated"], S["dgated"] = gated, dgated

        def stage_C(c):
            cs = slice(c * 128, (c + 1) * 128)
            S = st_c[c]
            gated, dgated = S["gated"], S["dgated"]
            for grp in range(2):
                pT, dpT = self.pT.next()
                pTb = pT[:].bitcast(BF16)
                for kk in range(4):
                    kf = 4 * grp + kk
                    kb.op("pe", lambda e, kk=kk, kf=kf, pTb=pTb: e.transpose(
                        pTb[:, kk * 128:(kk + 1) * 128], gated[:, kf * 128:(kf + 1) * 128], self.idb[:]),
                          r=[dgated, self.d_const], w=[dpT])
                evac_T(pTb, dpT, gatedT[:, 4 * grp:4 * grp + 4, cs], d_gatedT, "dve", extra=xtra("gatedT"))

        stage_A(0)
        for c in range(4):
            stage_B_pe(c)
            if c >= 1:
                stage_B_post2(c - 1)
            if c + 1 < 4:
                stage_A(c + 1)
            stage_B_post1(c)
            if c >= 1:
                stage_C(c - 1)
        stage_B_post2(3)
        LW("wgs", 0, 2)
        LW("glu_v", 1, 3)
        if t + 1 < NTL:
            nx_kT = load_kT(t + 1)
        xrs = []
        for c in range(4):
            xr, dxr = xres_r.next()
            xrs.append((xr, dxr))
        for c in range(2):
            kb.dma("sp", xrs[c][0][:], src(t)[c * 128:(c + 1) * 128, :], r=[src_dep], w=[xrs[c][1]])
        for fo in range(8):
            hs = fo // 4
            fc = slice((fo % 4) * 128, (fo % 4 + 1) * 128)
            gv_s, gg_s, gs_s = (0, 1, 2) if hs == 0 else (3, 0, 1)
            pV2, dpV2 = self.pA.next()
            pG2, dpG2 = self.pU.next()
            pGs, dpGs = self.pO.next()
            for (pX, dpX, sl_, rhsT, drhs) in ((pV2, dpV2, gv_s, zT, d_zT), (pG2, dpG2, gg_s, zT, d_zT),
                                               (pGs, dpGs, gs_s, xT, dxT)):
                for k in range(8):
                    kb.op("pe", lambda e, k=k, pX=pX, sl_=sl_, rhsT=rhsT, fc=fc: e.matmul(
                        pX[:], wslot[sl_][:, k, fc], rhsT[:, k, :], start=(k == 0), stop=(k == 7)),
                          r=[dslot[sl_], drhs], w=[dpX])
            s2, ds2 = s2_r.next()
            kb.op("act", lambda e, s2=s2, pG2=pG2: e.activation(out=s2[:], in_=pG2[:], func=AF.Sigmoid), r=[dpG2],
                  w=[ds2])
            kb.op("dve", lambda e, s2=s2, pV2=pV2: e.tensor_tensor(out=s2[:], in0=s2[:], in1=pV2[:], op=ALU.mult),
                  r=[ds2, dpV2], w=[ds2])
            gs, dgs = gs_r.next()
            kb.op("act", lambda e, gs=gs, pGs=pGs, fo=fo: e.activation(out=gs[:], in_=pGs[:], func=AF.Sigmoid,
                                                                       bias=bg[:, 8 + fo:9 + fo]), r=[dpGs, d_bg],
                  w=[dgs])
            kb.op("pool", lambda e, gs=gs, s2=s2, fo=fo: e.tensor_tensor(out=m1T[:, fo, :], in0=gs[:], in1=s2[:],
                                                                         op=ALU.mult), r=[dgs, ds2], w=[d_m1[fo]])
            if fo == 3:
                LW("glu_g", 1, 0)
                LW("wgs", 1, 1)
                LW("wo", 0, 2)
        stage_C(3)
        LW("wgr", 0, 3)
        LW("wo", 1, 0)
        LW("wgr", 1, 1)
        for fo in range(8):
            hs = fo // 4
            fc = slice((fo % 4) * 128, (fo % 4 + 1) * 128)
            wo_s, wgr_s = (2, 3) if hs == 0 else (0, 1)
            pY, dpY = self.pA.next()
            pGr, dpGr = self.pU.next()
            for k in range(8):
                kb.op("pe", lambda e, k=k, pGr=pGr, wgr_s=wgr_s, fc=fc: e.matmul(pGr[:], wslot[wgr_s][:, k, fc],
                                                                               xT[:, k, :], start=(k == 0),
                                                                               stop=(k == 7)),
                      r=[dslot[wgr_s], dxT], w=[dpGr])
            for k in range(8):
                kb.op("pe", lambda e, k=k, pY=pY, wo_s=wo_s, fc=fc: e.matmul(pY[:], wslot[wo_s][:, k, fc], gatedT[:, k, :],
                                                                           start=(k == 0), stop=(k == 7)),
                      r=[dslot[wo_s], d_gatedT], w=[dpY])
            gr, dgr = gr_r.next()
            kb.op("act", lambda e, gr=gr, pGr=pGr, fo=fo: e.activation(out=gr[:], in_=pGr[:], func=AF.Sigmoid,
                                                                       bias=bg[:, fo:fo + 1]), r=[dpGr, d_bg], w=[dgr])
            kb.op("dve", lambda e, gr=gr, pY=pY: e.tensor_tensor(out=gr[:], in0=gr[:], in1=pY[:], op=ALU.mult),
                  r=[dgr, dpY], w=[dgr])
            kb.op("pool", lambda e, gr=gr, fo=fo: e.tensor_tensor(out=m1T[:, fo, :], in0=m1T[:, fo, :], in1=gr[:],
                                                                  op=ALU.add), r=[dgr, d_m1[fo]], w=[d_m1[fo]])
            if fo == 3:
                LW("wout", 0, 2)
                LW("wout", 1, 3)
        if t + 1 < NTL:
            LW("wq", 0, 0)
            LW("wq", 1, 1)
        for c in range(4):
            cg = 4 * t + c
            cs = slice(c * 128, (c + 1) * 128)
            xr, dxr = xrs[c]
            if c >= 2:
                kb.dma("sp", xr[:], src(t)[c * 128:(c + 1) * 128, :], r=[src_dep], w=[dxr])
            rr, drr = self.rr.next()
            for half in range(2):
                pM, dpM = (self.pA if half == 0 else self.pU).next()
                for k in range(8):
                    kb.op("pe", lambda e, k=k, pM=pM, half=half: e.matmul(
                        pM[:], m1T[:, k, cs], wslot[2 + half][:, k, :], start=(k == 0), stop=(k == 7)),
                          r=[d_m1[k], dslot[2 + half]], w=[dpM])
                kb.op("dve", lambda e, pM=pM, half=half, rr=rr, xr=xr: e.scalar_tensor_tensor(
                    out=rr[:, half * 512:(half + 1) * 512], in0=xr[:, half * 512:(half + 1) * 512], scalar=float(ALPHA),
                    in1=pM[:], op0=ALU.mult, op1=ALU.add), r=[dxr, dpM], w=[drr])
            self.layer_norm(rr, drr, EPS, dst(t)[c * 128:(c + 1) * 128, :], dst_dep)


Prog.mixer_b = _mixer_b


N_CORES = 8
_PROG_CACHE = {}


def kernel(**inputs):
    xp = np.asarray(inputs["x_prompt"], np.float32)
    xs = np.asarray(inputs["x_sample"], np.float32)
    nb_p, nb_s = xp.shape[0], xs.shape[0]
    assert nb_p % N_CORES == 0 and nb_s % N_CORES == 0
    pp, ps_ = nb_p // N_CORES, nb_s // N_CORES
    nseq = pp + ps_
    inp = {k: np.asarray(v) for k, v in inputs.items() if not k.startswith("x_")}
    hi = host_inputs(inp)
    if nseq not in _PROG_CACHE:
        _PROG_CACHE[nseq] = Prog(nseq, {}).build()
    nc = _PROG_CACHE[nseq]
    in_maps = []
    for c in range(N_CORES):
        xc = np.concatenate([xp[c * pp:(c + 1) * pp], xs[c * ps_:(c + 1) * ps_]], axis=0)
        m = dict(hi)
        m["x"] = np.ascontiguousarray(xc)
        in_maps.append(m)
    res = run_bass_kernel_spmd(nc, in_maps, core_ids=list(range(N_CORES)))
    yp = np.concatenate([res.results[c]["y"][:pp] for c in range(N_CORES)], axis=0)
    ys = np.concatenate([res.results[c]["y"][pp:] for c in range(N_CORES)], axis=0)
    return (np.ascontiguousarray(yp, np.float32), np.ascontiguousarray(ys, np.float32))
```

```python
import math
import numpy as np
import concourse.bass as bass
import concourse.mybir as mybir
from concourse.bass_utils import run_bass_kernel_spmd

F32 = mybir.dt.float32
BF16 = mybir.dt.bfloat16
AF = mybir.ActivationFunctionType
ALU = mybir.AluOpType

D = 1024
DFF = 2816
NJ = DFF // 128
L_SEQ = 2048
DEPTH = 2
ALPHA = (2 * DEPTH) ** 0.25
EPS = 1e-5
NH = 8
T = 512


class Dep:
    __slots__ = ("w", "r", "rd", "wd")

    def __init__(self):
        self.w = None
        self.r = {}
        self.rd = []
        self.wd = []


class KB:
    EPOCH = 30000
    KD = 24

    def __init__(self):
        nc = bass.Bass("TRN2", target_bir_lowering=False)
        self.nc = nc
        self.eng = {"pe": nc.tensor, "act": nc.scalar, "dve": nc.vector, "pool": nc.gpsimd, "sp": nc.sync}
        self.cnt = {e: 0 for e in self.eng}
        self.sems = {e: [] for e in self.eng}
        self.waited = {e: {} for e in self.eng}
        self.dsems = [nc.alloc_semaphore(name=f"dma{i}") for i in range(self.KD)]
        self.ndma = 0
        self.ssems = [nc.alloc_semaphore(name=f"sdma{i}") for i in range(self.KD)]
        self.nsdma = 0
        self.nins = 0

    SB_LO = 16512
    SB_HI = 229344

    def sb(self, name, shape, dt):
        if not hasattr(self, "sb_off"):
            self.sb_off = self.SB_LO
            self.sb_peak = self.SB_LO
        n = 1
        for s in shape[1:]:
            n *= s
        nbytes = (n * mybir.dt.size(dt) + 31) // 32 * 32
        off = self.sb_off
        assert off + nbytes <= self.SB_HI, f"SBUF overflow allocating {name}: {off + nbytes - self.SB_HI} bytes over"
        self.sb_off += nbytes
        self.sb_peak = max(self.sb_peak, self.sb_off)
        return self.nc.alloc_sbuf_tensor_at(name, list(shape), dt, offset=off)

    def mark(self):
        if not hasattr(self, "sb_off"):
            self.sb_off = self.SB_LO
            self.sb_peak = self.SB_LO
        return self.sb_off

    def release(self, m):
        self.sb_off = m

    def barrier(self, dma=True):
        evs = [("d", i) for i in range(max(0, self.ndma - self.KD), self.ndma)] if dma else []
        if dma:
            evs += [("s", i) for i in range(max(0, self.nsdma - self.KD), self.nsdma)]
        for e in self.eng:
            if self.cnt[e] > 0:
                evs.append(("c", e, self.cnt[e]))
        for e in self.eng:
            self._wait(e, [ev for ev in evs if not (ev[0] == "c" and ev[1] == e)])

    def ps(self, name, shape, dt=F32):
        return self.nc.alloc_psum_tensor(name, list(shape), dt)

    def _sem(self, e, epoch):
        while len(self.sems[e]) <= epoch:
            self.sems[e].append(self.nc.alloc_semaphore(name=f"s_{e}_{len(self.sems[e])}"))
        return self.sems[e][epoch]

    def _wait(self, e, evs):
        best = {}
        for ev in evs:
            if ev[0] == "c":
                _, e2, c = ev
                if e2 == e and e == "pe":
                    continue
                key = ("c", e2, (c - 1) // self.EPOCH)
                val = (c - 1) % self.EPOCH + 1
            else:
                i = ev[1]
                key = (ev[0], i % self.KD)
                val = 16 * (i // self.KD + 1)
            if val > best.get(key, 0):
                best[key] = val
        wd = self.waited[e]
        for key, val in best.items():
            if wd.get(key, 0) >= val:
                continue
            if key[0] == "c":
                if any(k[0] == "c" and k[1] == key[1] and k[2] > key[2] for k in wd):
                    continue
                wd[key] = val
                self.eng[e].wait_ge(self._sem(key[1], key[2]), val)
            else:
                wd[key] = val
                self.eng[e].wait_ge((self.dsems if key[0] == "d" else self.ssems)[key[1]], val)
            self.nins += 1

    def _deps(self, r, w, e):
        evs = []
        for d in r:
            if d.w is not None:
                evs.append(d.w)
            evs.extend(d.wd)
        for d in w:
            if d.w is not None:
                evs.append(d.w)
            evs.extend(d.wd)
            for k, ev in d.r.items():
                if k != e or e != "pe":
                    evs.append(ev)
            evs.extend(d.rd)
        return evs

    def op(self, e, fn, r=(), w=(), extra=()):
        self._wait(e, self._deps(r, w, e) + list(extra))
        ins = fn(self.eng[e])
        self.cnt[e] += 1
        c = self.cnt[e]
        ins.then_inc(self._sem(e, (c - 1) // self.EPOCH), 1)
        ev = ("c", e, c)
        for d in r:
            d.r[e] = ev
        for d in w:
            d.w = ev
            d.wd = []
            d.r = {}
            d.rd = []
        self.nins += 1
        return ev

    def dma(self, q, out, in_, r=(), w=(), **kw):
        self._wait(q, self._deps(r, w, "dma"))
        sw = (q == "pool")
        tag = "s" if sw else "d"
        sems = self.ssems if sw else self.dsems
        if sw:
            i = self.nsdma
            self.nsdma += 1
        else:
            i = self.ndma
            self.ndma += 1
        if i >= self.KD:
            key = (tag, i % self.KD)
            val = 16 * (i // self.KD)
            if self.waited[q].get(key, 0) < val:
                self.waited[q][key] = val
                self.eng[q].wait_ge(sems[i % self.KD], val)
        self.eng[q].dma_start(out=out, in_=in_, **kw).then_inc(sems[i % self.KD], 16)
        ev = (tag, i)
        for d in r:
            d.rd.append(ev)
            if len(d.rd) > self.KD:
                del d.rd[0]
        for d in w:
            d.wd.append(ev)
            if len(d.wd) > self.KD:
                del d.wd[0]
            d.r = {}
            d.rd = []
        self.nins += 1
        return ev

    def finish(self):
        evs = [("d", i) for i in range(max(0, self.ndma - self.KD), self.ndma)]
        evs += [("s", i) for i in range(max(0, self.nsdma - self.KD), self.nsdma)]
        for e in self.eng:
            if self.cnt[e] > 0 and e != "sp":
                evs.append(("c", e, self.cnt[e]))
        self._wait("sp", evs)


class Rot:
    def __init__(self, kb, name, n, shape, dt, psum=False):
        self.t = [(kb.ps if psum else kb.sb)(f"{name}{i}", shape, dt) for i in range(n)]
        self.d = [Dep() for _ in range(n)]
        self.i = 0

    def next(self):
        k = self.i % len(self.t)
        self.i += 1
        return self.t[k], self.d[k]


def _layout():
    off = {}
    o = 0

    def add(name, n):
        nonlocal o
        off[name] = o
        o += n

    for f in (1, 2):
        for jb in range(11):
            add(f"f{f}gu{jb}", 8 * 512)
        add(f"f{f}d", NJ * 1024)
    for nm in ("wq", "wk", "wv", "wg", "wu", "wgr", "wgs", "wo", "glu_v", "glu_g", "wout"):
        add(nm, 8 * 1024)
    return off, o


W_OFF, W_TOT = _layout()


def _kmajor(w):
    K, C = w.shape
    return w.reshape(K // 128, 128, C).transpose(1, 0, 2)


def pack_weights(inp, l):
    out = np.empty((128, W_TOT), np.float32)

    def put(name, arr):
        a = arr.reshape(128, -1)
        out[:, W_OFF[name]:W_OFF[name] + a.shape[1]] = a

    for f in (1, 2):
        gu = _kmajor(inp[f"ffn{f}_w_gu"][l])
        for jb in range(11):
            g = gu[:, :, jb * 256:(jb + 1) * 256]
            u = gu[:, :, DFF + jb * 256:DFF + (jb + 1) * 256]
            put(f"f{f}gu{jb}", np.concatenate([g, u], axis=2))
        put(f"f{f}d", _kmajor(inp[f"ffn{f}_w_down"][l]))
    win = _kmajor(inp["w_in"][l])
    swap = np.arange(512).reshape(8, 2, 32)[:, ::-1, :].reshape(512)
    q = win[:, :, 0:512]
    k = win[:, :, 512:1024]
    put("wq", np.concatenate([q, q[:, :, swap]], axis=2))
    put("wk", np.concatenate([k, k[:, :, swap]], axis=2))
    put("wv", win[:, :, 1024:2048])
    put("wg", win[:, :, 2048:3072])
    perm = np.arange(1024).reshape(64, 16).T.reshape(1024)
    put("wu", win[:, :, 3072:4096][:, :, perm])
    put("wgr", win[:, :, 4096:5120])
    put("wgs", win[:, :, 5120:6144])
    put("wo", _kmajor(inp["ret_w_o"][l]))
    glu = inp["s5_w_glu"][l][perm, :]
    glu = _kmajor(glu)
    put("glu_v", glu[:, :, 0:1024])
    put("glu_g", glu[:, :, 1024:2048])
    put("wout", _kmajor(inp["w_out"][l]))
    return out


def rep128(v):
    return np.ascontiguousarray(np.broadcast_to(np.asarray(v, np.float32)[None, :], (128, v.shape[0])))


CT_DEC = 0
CT_ZB = CT_DEC + 1024
CT_ZF = CT_ZB + 512
CT_XF = CT_ZF + 512
CT_XB = CT_XF + 512
CT_G128 = CT_XB + 512
CT_MF = CT_G128 + 512
CT_MB = CT_MF + 128
CT_TOT = CT_MB + 128
NB = 4
C1 = 8 * NB
NK = L_SEQ // C1
NBLK = L_SEQ // 8
NT1 = 2 * NB
NT2 = 1 + 2 * (NB - 1) + 2 * NB
S5P_W = 3 * 64 + 4 * 1024


def const_tables():
    lg = np.log1p(-np.exp2(-5.0 - np.arange(NH, dtype=np.float64)))
    ct = np.zeros((128, CT_TOT), np.float64)
    j = np.arange(128)[:, None]
    i = np.arange(128)[None, :]
    for h in range(NH):
        ct[:, CT_DEC + h * 128:CT_DEC + (h + 1) * 128] = 0.125 * np.exp(lg[h] * np.abs(i - j))
        ct[:, CT_ZB + h * 64:CT_ZB + (h + 1) * 64] = 0.125 * np.exp(lg[h] * j)
        ct[:, CT_ZF + h * 64:CT_ZF + (h + 1) * 64] = 0.125 * np.exp(lg[h] * (127 - j))
    for p in range(128):
        h2 = p // 64
        for hp in range(4):
            h = 2 * hp + h2
            ii = np.arange(128)
            ct[p, CT_XF + hp * 128:CT_XF + (hp + 1) * 128] = np.exp(lg[h] * (ii + 1))
            ct[p, CT_XB + hp * 128:CT_XB + (hp + 1) * 128] = np.exp(lg[h] * (128 - ii))
            ct[p, CT_G128 + hp * 128:CT_G128 + (hp + 1) * 128] = np.exp(lg[h] * 128.0)
    m = np.arange(128)[:, None] // 16
    q = np.arange(128)[None, :] // 16
    ct[:, CT_MF:CT_MF + 128] = (q >= m)
    ct[:, CT_MB:CT_MB + 128] = (m >= q)
    half = 32
    inv_freq = (10000.0 ** (-np.arange(half, dtype=np.float32) / half)).astype(np.float32)
    pos = np.arange(L_SEQ, dtype=np.float32)
    ang = pos[None, :] * inv_freq[:, None]
    cos = np.cos(ang.astype(np.float32)).astype(np.float32)
    sin = np.sin(ang.astype(np.float32)).astype(np.float32)
    rope = np.zeros((2, 128, L_SEQ), np.float32)
    for p in range(128):
        d = p % 64
        rope[0, p] = cos[d % 32]
        rope[1, p] = -sin[d % 32] if d < 32 else sin[d % 32]
    return ct.astype(np.float32), rope


def s5_host_layout(inp, l):
    out = np.zeros((128, S5P_W), np.float32)

    def lay(a):
        sh = a.shape
        a = a.reshape(2, 2, 32, 64, -1)
        a = a.transpose(1, 3, 0, 2, 4)
        return a.reshape(128, -1)

    out[:, 0:64] = lay(inp["s5_lam_re"][l])
    out[:, 64:128] = lay(inp["s5_lam_im"][l])
    ldt = np.broadcast_to(inp["s5_log_dt"][l][:, :, None], (2, 64, 64))
    out[:, 128:192] = lay(np.ascontiguousarray(ldt))
    o = 192
    for nm in ("s5_b_re", "s5_b_im"):
        out[:, o:o + 1024] = lay(inp[nm][l])
        o += 1024
    for nm in ("s5_c_re", "s5_c_im"):
        out[:, o:o + 1024] = lay(inp[nm][l].transpose(0, 1, 3, 2))
        o += 1024
    dcol = np.zeros((128, 64), np.float32)
    dd = inp["s5_d"][l].reshape(64, 16)
    for s in range(8):
        dcol[s * 16:(s + 1) * 16, :] = dd.T
    return out, dcol


class Prog:
    def __init__(self, nseq, cfg):
        self.nseq = nseq
        self.cfg = cfg
        kb = self.kb = KB()
        nc = self.nc = kb.nc
        dt = nc.dram_tensor
        self.x = dt("x", [nseq, L_SEQ, D], F32, kind="ExternalInput").ap()
        self.y = dt("y", [nseq, L_SEQ, D], F32, kind="ExternalOutput").ap()
        self.wpack = dt("wpack", [DEPTH, 128, W_TOT], F32, kind="ExternalInput").ap()
        self.lntab = dt("lntab", [DEPTH, 3, 2, 128, D], F32, kind="ExternalInput").ap()
        self.ctab = dt("ctab", [128, CT_TOT], F32, kind="ExternalInput").ap()
        self.rope = dt("rope", [2, 128, L_SEQ], F32, kind="ExternalInput").ap()
        self.bgate = dt("bgate", [DEPTH, 128, 16], F32, kind="ExternalInput").ap()
        self.s5p = dt("s5p", [DEPTH, 128, S5P_W], F32, kind="ExternalInput").ap()
        self.s5d = dt("s5d", [DEPTH, 128, 64], F32, kind="ExternalInput").ap()
        self.wbf = dt("wbf", [DEPTH, 128, W_TOT], BF16, kind="Internal").ap()
        self.s5m = dt("s5m", [DEPTH, 64, 128, NT1 + NT2, 128], BF16, kind="Internal").ap()
        self.xa = dt("xa", [L_SEQ, D], F32, kind="Internal").ap()
        self.xb = dt("xb", [L_SEQ, D], F32, kind="Internal").ap()
        self.xts = dt("xts", [128, 8, L_SEQ], BF16, kind="Internal").ap()
        self.ks = dt("ks", [128, 4, L_SEQ], BF16, kind="Internal").ap()
        self.vs = dt("vs", [L_SEQ, D], BF16, kind="Internal").ap()
        self.us = dt("us", [L_SEQ, D], BF16, kind="Internal").ap()
        self.zs = dt("zs", [L_SEQ, D], BF16, kind="Internal").ap()
        self.d_wbf = [Dep() for _ in range(DEPTH)]
        self.d_s5m = [Dep() for _ in range(DEPTH)]
        self.d_xa = Dep()
        self.d_xb = Dep()
        self.d_in = Dep()
        self.d_scr = {k: Dep() for k in ("xts", "ks", "vs", "us", "zs")}
        self.dbg = {}
        if cfg.get("debug"):
            for nm, shp, dty in (("dbg_ks", [128, 4, L_SEQ], BF16), ("dbg_vs", [L_SEQ, D], BF16),
                                 ("dbg_us", [L_SEQ, D], BF16), ("dbg_zs", [L_SEQ, D], BF16),
                                 ("dbg_bs", [128, 16, 512], BF16),
                                 ("dbg_s5m", [64, 128, NT1 + NT2, 128], BF16)):
                self.dbg[nm] = dt(nm, shp, dty, kind="ExternalOutput").ap()
        self.alloc_persistent()

    def alloc_persistent(self):
        kb = self.kb
        self.idf = kb.sb("idf", [128, 128], F32)
        self.idb = kb.sb("idb", [128, 128], BF16)
        self.ct = kb.sb("ct", [128, CT_TOT], F32)
        self.d_const = Dep()
        self.pA = Rot(kb, "pA", 2, [128, 512], F32, psum=True)
        self.pU = Rot(kb, "pU", 2, [128, 512], F32, psum=True)
        self.pO = Rot(kb, "pO", 2, [128, 512], F32, psum=True)
        self.pT = Rot(kb, "pT", 2, [128, 512], F32, psum=True)
        self.base_mark = kb.mark()

    def ctv(self, off, n):
        return self.ct[:, off:off + n]

    def consts(self):
        kb = self.kb
        for t in (self.idf, self.idb):
            kb.op("pool", lambda e, t=t: e.memset(t[:], 1.0), w=[self.d_const])
            kb.op("pool", lambda e, t=t: e.affine_select(out=t[:], in_=t[:], pattern=[[-1, 128]],
                                                         compare_op=ALU.is_equal, fill=0.0, base=0,
                                                         channel_multiplier=1), r=[self.d_const], w=[self.d_const])
        kb.dma("sp", self.ct[:], self.ctab, w=[self.d_const])

    def cast_weights_gen(self):
        kb = self.kb
        PC = 2048
        pieces = []
        for l in range(DEPTH):
            o = 0
            while o < W_TOT:
                n = min(PC, W_TOT - o)
                pieces.append((l, o, n))
                o += n
        NBUF = 4
        i = 0
        while i < len(pieces):
            fin = Rot(kb, "cst_in", NBUF, [128, PC], F32)
            fout = Rot(kb, "cst_out", 2, [128, PC], BF16)
            self.cast_realloc = False
            pend = []
            while i < len(pieces) and not self.cast_realloc:
                while len(pend) < NBUF - 1 and i + len(pend) < len(pieces):
                    l, o, n = pieces[i + len(pend)]
                    bi, dbi = fin.next()
                    kb.dma("sp", bi[:, 0:n], self.wpack[l, :, o:o + n], w=[dbi])
                    pend.append((bi, dbi))
                l, o, n = pieces[i]
                bi, dbi = pend.pop(0)
                bo, dbo = fout.next()
                kb.op("act", lambda e, bi=bi, bo=bo, n=n: e.activation(out=bo[:, 0:n], in_=bi[:, 0:n], func=AF.Copy),
                      r=[dbi], w=[dbo])
                kb.dma("sp", self.wbf[l, :, o:o + n], bo[:, 0:n], r=[dbo], w=[self.d_wbf[l]])
                i += 1
                yield
        return

    def phase_begin(self, dma=True):
        self._last_phase = None
        self.kb.barrier(dma=dma)
        self.kb.release(self.base_mark)
        self.cast_realloc = True

    def wblock(self, l, name, buf, dbuf, q="sp"):
        o = W_OFF[name]
        for h in range(2):
            self.kb.dma(q, buf[:, 4 * h:4 * h + 4, :],
                        self.wbf[l][:, o + h * 4096:o + (h + 1) * 4096].rearrange("p (k c) -> p k c", c=1024),
                        r=[self.d_wbf[l]], w=[dbuf])

    def transposes_to_xT(self, xt, dxt, xT, dxT):
        kb = self.kb
        for k in range(8):
            pT, dpT = self.pT.next()
            for c in range(4):
                kb.op("pe", lambda e, c=c, k=k, pT=pT: e.transpose(pT[:, c * 128:(c + 1) * 128],
                                                                  xt[:, c, k * 128:(k + 1) * 128], self.idf[:]),
                      r=[dxt, self.d_const], w=[dpT])
            if k % 2 == 0:
                kb.op("act", lambda e, k=k, pT=pT: e.activation(out=xT[:, k, :], in_=pT[:], func=AF.Copy),
                      r=[dpT], w=[dxT])
            else:
                kb.op("dve", lambda e, k=k, pT=pT: e.tensor_copy(out=xT[:, k, :], in_=pT[:]), r=[dpT], w=[dxT])

    def layer_norm(self, rr, drr, eps, dst_ap, dst_dep):
        kb = self.kb
        st, dst_ = self.st.next()
        dsa, dsb = Dep(), Dep()
        kb.op("dve", lambda e: e.bn_stats(out=st[:, 0:6], in_=rr[:, 0:512]), r=[drr], w=[dsa, dst_])
        kb.op("dve", lambda e: e.bn_stats(out=st[:, 6:12], in_=rr[:, 512:1024]), r=[drr], w=[dsb, dst_])
        kb.op("dve", lambda e: e.bn_aggr(out=st[:, 12:14], in_=st[:, 0:12]), r=[dsa, dsb], w=[dst_])
        kb.op("dve", lambda e: e.tensor_scalar(out=st[:, 14:15], in0=st[:, 13:14], scalar1=float(eps), scalar2=None,
                                              op0=ALU.add), r=[dst_], w=[dst_])
        kb.op("act", lambda e: e.activation(out=st[:, 14:15], in_=st[:, 14:15], func=AF.Sqrt), r=[dst_], w=[dst_])
        kb.op("dve", lambda e: e.reciprocal(out=st[:, 15:16], in_=st[:, 14:15]), r=[dst_], w=[dst_])
        kb.op("dve", lambda e: e.tensor_scalar(out=rr[:], in0=rr[:], scalar1=st[:, 12:13], scalar2=st[:, 15:16],
                                              op0=ALU.subtract, op1=ALU.mult), r=[dst_, drr], w=[drr])
        xo, dxo = self.xo.next()
        kb.op("pool", lambda e: e.tensor_tensor(out=rr[:], in0=rr[:], in1=self.lnG[:], op=ALU.mult),
              r=[drr, self.d_ln], w=[drr])
        kb.op("pool", lambda e: e.tensor_tensor(out=xo[:], in0=rr[:], in1=self.lnB[:], op=ALU.add),
              r=[drr, self.d_ln], w=[dxo])
        kb.dma("pool", dst_ap, xo[:], r=[dxo], w=[dst_dep])

    def alloc_ln(self, l, ln_idx, nrr=2):
        kb = self.kb
        self.lnG = kb.sb("lnG", [128, D], F32)
        self.lnB = kb.sb("lnB", [128, D], F32)
        self.d_ln = Dep()
        self.st = Rot(kb, "st", 2, [128, 16], F32)
        self.rr = Rot(kb, "rr", nrr, [128, D], F32)
        self.xo = Rot(kb, "xo", 2, [128, D], F32)
        kb.dma("sp", self.lnG[:], self.lntab[l, ln_idx, 0], w=[self.d_ln])
        kb.dma("sp", self.lnB[:], self.lntab[l, ln_idx, 1], w=[self.d_ln])

    def ffn(self, l, f, src, src_dep, dst, dst_dep, ln_idx):
        kb = self.kb
        wbf = self.wbf[l]
        dW = self.d_wbf[l]
        if getattr(self, "_last_phase", None) == "ffn":
            (xt_r, xT_r, hid, d_hid, wgu_r, wd_r, silu_r) = self._ffn_ctx
            kb.dma("sp", self.lnG[:], self.lntab[l, ln_idx, 0], w=[self.d_ln])
            kb.dma("sp", self.lnB[:], self.lntab[l, ln_idx, 1], w=[self.d_ln])
        else:
            self.phase_begin()
            xt_r = Rot(kb, "xt", 2, [128, 4, D], F32)
            xT_r = Rot(kb, "xT", 1, [128, 8, T], BF16)
            hid = kb.sb("hid", [128, NJ, T], BF16)
            d_hid = [Dep() for _ in range(NJ)]
            wgu_r = Rot(kb, "wgu", 4, [128, 8, 512], BF16)
            wd_r = Rot(kb, "wd", 1, [128, NJ, 1024], BF16)
            silu_r = Rot(kb, "silu", 2, [128, T], F32)
            self.alloc_ln(l, ln_idx)
            self._ffn_ctx = (xt_r, xT_r, hid, d_hid, wgu_r, wd_r, silu_r)
        self._last_phase = "ffn"
        gu_off = [W_OFF[f"f{f}gu{jb}"] for jb in range(11)]
        d_off = W_OFF[f"f{f}d"]

        def load_x(t):
            xt, dxt = xt_r.next()
            kb.dma("sp", xt[:], src(t).rearrange("(c p) f -> p c f", p=128), r=[src_dep], w=[dxt])
            return xt, dxt

        NTL = L_SEQ // T
        NBLK_GU = 11 * NTL
        blks = {}

        def load_gu(i):
            if i >= NBLK_GU or i in blks:
                return
            jb = i % 11
            wg, dwg = wgu_r.next()
            kb.dma("sp", wg[:], wbf[:, gu_off[jb]:gu_off[jb] + 4096].rearrange("p (k c) -> p k c", c=512),
                   r=[dW], w=[dwg])
            blks[i] = (wg, dwg)

        nxt = load_x(0)
        load_gu(0)
        load_gu(1)
        load_gu(2)
        for t in range(NTL):
            xt, dxt = nxt
            xT, dxT = xT_r.next()
            self.transposes_to_xT(xt, dxt, xT, dxT)
            wd, dwd = wd_r.next()
            for jb in range(11):
                i = t * 11 + jb
                if jb == 1:
                    for j0 in range(0, NJ, 2):
                        kb.dma("sp", wd[:, j0:j0 + 2, :],
                               wbf[:, d_off + j0 * 1024:d_off + (j0 + 2) * 1024].rearrange("p (j c) -> p j c", c=1024),
                               r=[dW], w=[dwd])
                    if t + 1 < NTL:
                        nxt = load_x(t + 1)
                load_gu(i + 3)
                wg, dwg = blks.pop(i)
                for jj in range(2):
                    j = 2 * jb + jj
                    pA, dpA = self.pA.next()
                    pU, dpU = self.pU.next()
                    for k in range(8):
                        kb.op("pe", lambda e, k=k, pA=pA, wg=wg, jj=jj: e.matmul(
                            pA[:], wg[:, k, jj * 128:(jj + 1) * 128], xT[:, k, :], start=(k == 0), stop=(k == 7)),
                              r=[dwg, dxT], w=[dpA])
                    for k in range(8):
                        kb.op("pe", lambda e, k=k, pU=pU, wg=wg, jj=jj: e.matmul(
                            pU[:], wg[:, k, 256 + jj * 128:256 + (jj + 1) * 128], xT[:, k, :], start=(k == 0),
                            stop=(k == 7)), r=[dwg, dxT], w=[dpU])
                    sl, dsl = silu_r.next()
                    kb.op("act", lambda e, sl=sl, pA=pA: e.activation(out=sl[:], in_=pA[:], func=AF.Silu),
                          r=[dpA], w=[dsl])
                    kb.op("dve", lambda e, sl=sl, pU=pU, j=j: e.tensor_tensor(out=hid[:, j, :], in0=sl[:],
                                                                             in1=pU[:], op=ALU.mult),
                          r=[dsl, dpU], w=[d_hid[j]])
            for c in range(4):
                pO0, dpO0 = self.pO.next()
                pO1, dpO1 = self.pO.next()
                for half, (pO, dpO) in enumerate(((pO0, dpO0), (pO1, dpO1))):
                    for j in range(NJ):
                        kb.op("pe", lambda e, j=j, pO=pO, half=half, c=c: e.matmul(
                            pO[:], hid[:, j, c * 128:(c + 1) * 128], wd[:, j, half * 512:(half + 1) * 512],
                            start=(j == 0), stop=(j == NJ - 1)), r=[d_hid[j], dwd], w=[dpO])
                rr, drr = self.rr.next()
                for half, (pO, dpO) in enumerate(((pO0, dpO0), (pO1, dpO1))):
                    kb.op("dve", lambda e, pO=pO, half=half, rr=rr, c=c: e.scalar_tensor_tensor(
                        out=rr[:, half * 512:(half + 1) * 512], in0=xt[:, c, half * 512:(half + 1) * 512],
                        scalar=2.0 * ALPHA, in1=pO[:], op0=ALU.mult, op1=ALU.add), r=[dxt, dpO], w=[drr])
                self.layer_norm(rr, drr, 4.0 * EPS, dst(t)[c * 128:(c + 1) * 128, :], dst_dep)

    def rotary(self, pX, dpX, pXs, dpXs, cos, sin, dtab, tmp_r, out_ap, out_dep, also=None):
        kb = self.kb
        t1, dt1 = tmp_r.next()
        t2, dt2 = tmp_r.next()
        kb.op("dve", lambda e: e.tensor_tensor(out=t1[:], in0=pX[:], in1=cos[:], op=ALU.mult), r=[dpX, dtab], w=[dt1])
        kb.op("dve", lambda e: e.tensor_tensor(out=t2[:], in0=pXs[:], in1=sin[:], op=ALU.mult), r=[dpXs, dtab],
              w=[dt2])
        if also is None:
            kb.op("pool", lambda e: e.tensor_tensor(out=out_ap, in0=t1[:], in1=t2[:], op=ALU.add), r=[dt1, dt2],
                  w=[out_dep])
        else:
            kb.op("pool", lambda e: e.tensor_tensor(out=t1[:], in0=t1[:], in1=t2[:], op=ALU.add), r=[dt1, dt2],
                  w=[dt1])
            kb.op("act", lambda e: e.activation(out=out_ap, in_=t1[:], func=AF.Copy), r=[dt1], w=[out_dep])
            for fn in also:
                fn(t1, dt1)

    def kv_state_step(self, kz, dkz, v, dv, st, dst_):
        kb = self.kb
        pT, dpT = self.pT.next()
        for h in range(NH):
            b = (h % 2) * 64
            hp = h // 2
            kb.op("pe", lambda e, h=h, b=b, hp=hp: e.matmul(pT[b:b + 64, hp * 128:(hp + 1) * 128],
                                                           kz[:, h * 64:(h + 1) * 64], v[:, h * 128:(h + 1) * 128],
                                                           start=True, stop=True), r=[dkz, dv], w=[dpT])
        kb.op("pool", lambda e: e.tensor_tensor(out=st[:], in0=st[:], in1=self.ctv(CT_G128, 512), op=ALU.mult),
              r=[dst_, self.d_const], w=[dst_])
        kb.op("dve", lambda e: e.tensor_tensor(out=st[:], in0=st[:], in1=pT[:], op=ALU.add), r=[dst_, dpT],
              w=[dst_])

    def k_transpose_scaled(self, kT, dkT, c, ztab_off, kz_r):
        kb = self.kb
        pT, dpT = self.pT.next()
        pTb = pT[:].bitcast(BF16)
        for f in range(4):
            kb.op("pe", lambda e, f=f: e.transpose(pTb[:, f * 128:(f + 1) * 128], kT[:, f, c * 128:(c + 1) * 128],
                                                    self.idb[:]), r=[dkT, self.d_const], w=[dpT])
        kz, dkz = kz_r.next()
        kb.op("dve", lambda e: e.tensor_tensor(out=kz[:], in0=pTb[:, 0:512], in1=self.ctv(ztab_off, 512),
                                              op=ALU.mult), r=[dpT, self.d_const], w=[dkz])
        return kz, dkz

    def mixer_a(self, l, src, src_dep):
        kb = self.kb
        self.phase_begin()
        self.u2a_off = kb.mark()
        u2a = self.u2a = kb.sb("u2a", [128, NBLK, 64], BF16)
        self.d_u2a = Dep()
        d_us_tile = [Dep() for _ in range(L_SEQ // T)]
        usv = self.us.rearrange("(b s) (p g) -> (s p) b g", s=8, p=16)
        xt_r = Rot(kb, "xt", 2, [128, 4, D], F32)
        xT_r = Rot(kb, "xT", 2, [128, 8, T], BF16)
        wk = kb.sb("wk", [128, 8, 1024], BF16)
        wv = kb.sb("wv", [128, 8, 1024], BF16)
        wu = kb.sb("wu", [128, 8, 1024], BF16)
        dwk, dwv, dwu = Dep(), Dep(), Dep()
        rope_r = Rot(kb, "rope", 2, [128, 2, T], F32)
        kT_r = Rot(kb, "kT", 2, [128, 4, T], BF16)
        tmp_r = Rot(kb, "rtmp", 4, [128, T], F32)
        v_r = Rot(kb, "v", 2, [128, D], BF16)
        u_r = Rot(kb, "u", 2, [128, D], BF16)
        kz_r = Rot(kb, "kz", 2, [128, 512], BF16)
        st = self.bstate
        dst_ = self.d_bstate
        kb.op("pool", lambda e: e.memset(st[:], 0.0), w=[dst_])

        def load_x(t):
            xt, dxt = xt_r.next()
            kb.dma("sp", xt[:], src(t).rearrange("(c p) f -> p c f", p=128), r=[src_dep], w=[dxt])
            rp, drp = rope_r.next()
            kb.dma("sp", rp[:], self.rope[:, :, t * T:(t + 1) * T].rearrange("a p t -> p a t"), w=[drp])
            return xt, dxt, rp, drp

        NTL = L_SEQ // T
        nxt = load_x(NTL - 1)
        self.wblock(l, "wk", wk, dwk)
        self.wblock(l, "wv", wv, dwv)
        self.wblock(l, "wu", wu, dwu)
        for t in range(NTL - 1, -1, -1):
            xt, dxt, rp, drp = nxt
            if t > 0:
                nxt = load_x(t - 1)
            xT, dxT = xT_r.next()
            self.transposes_to_xT(xt, dxt, xT, dxT)
            kb.dma("pool", self.xts[:, :, t * T:(t + 1) * T], xT[:], r=[dxT], w=[self.d_scr["xts"]])
            kT, dkT = kT_r.next()
            for f in range(4):
                pA, dpA = self.pA.next()
                pU, dpU = self.pU.next()
                for k in range(8):
                    kb.op("pe", lambda e, k=k, f=f, pA=pA: e.matmul(pA[:], wk[:, k, f * 128:(f + 1) * 128], xT[:, k, :],
                                                                  start=(k == 0), stop=(k == 7)), r=[dwk, dxT], w=[dpA])
                for k in range(8):
                    kb.op("pe", lambda e, k=k, f=f, pU=pU: e.matmul(pU[:], wk[:, k, 512 + f * 128:512 + (f + 1) * 128],
                                                                  xT[:, k, :], start=(k == 0), stop=(k == 7)),
                          r=[dwk, dxT], w=[dpU])
                self.rotary(pA, dpA, pU, dpU, rp[:, 0, :], rp[:, 1, :], drp, tmp_r, kT[:, f, :], dkT)
            kb.dma("pool", self.ks[:, :, t * T:(t + 1) * T], kT[:], r=[dkT], w=[self.d_scr["ks"]])
            for c in range(3, -1, -1):
                cg = 4 * t + c
                rows = slice(cg * 128, (cg + 1) * 128)
                v, dv = v_r.next()
                u, du = u_r.next()
                for (wX, dwX, dstt, ddst, eng) in ((wv, dwv, v, dv, "act"), (wu, dwu, u, du, "dve")):
                    for half in range(2):
                        pO, dpO = self.pO.next()
                        for k in range(8):
                            kb.op("pe", lambda e, k=k, pO=pO, wX=wX, half=half, c=c: e.matmul(
                                pO[:], xT[:, k, c * 128:(c + 1) * 128], wX[:, k, half * 512:(half + 1) * 512],
                                start=(k == 0), stop=(k == 7)), r=[dxT, dwX], w=[dpO])
                        if eng == "act":
                            kb.op("act", lambda e, pO=pO, dstt=dstt, half=half: e.activation(
                                out=dstt[:, half * 512:(half + 1) * 512], in_=pO[:], func=AF.Copy), r=[dpO], w=[ddst])
                        else:
                            kb.op("dve", lambda e, pO=pO, dstt=dstt, half=half: e.tensor_copy(
                                out=dstt[:, half * 512:(half + 1) * 512], in_=pO[:]), r=[dpO], w=[ddst])
                kb.dma("pool", self.vs[rows, :], v[:], r=[dv], w=[self.d_scr["vs"]])
                kb.dma("pool", self.us[rows, :], u[:], r=[du], w=[self.d_scr["us"], d_us_tile[t]])
                kz, dkz = self.k_transpose_scaled(kT, dkT, c, CT_ZB, kz_r)
                kb.op("act", lambda e, cg=cg: e.activation(out=self.bs[:, cg, :], in_=st[:], func=AF.Copy),
                      r=[dst_], w=[self.d_bs[cg]])
                self.kv_state_step(kz, dkz, v, dv, st, dst_)
            with self.nc.allow_non_contiguous_dma(reason="128B runs"):
                for b0 in range(t * 64, (t + 1) * 64, 16):
                    kb.dma("sp", u2a[:, b0:b0 + 16, :], usv[:, b0:b0 + 16, :], r=[d_us_tile[t]], w=[self.d_u2a])

    def alloc_seq_persistent(self):
        kb = self.kb
        kb.barrier()
        kb.release(self.base_mark)
        self.bs = kb.sb("bs", [128, 16, 512], BF16)
        self.d_bs = [Dep() for _ in range(16)]
        self.bstate = kb.sb("bstate", [128, 512], F32)
        self.d_bstate = Dep()
        self.fstate = kb.sb("fstate", [128, 512], F32)
        self.d_fstate = Dep()
        self.fbf = kb.sb("fbf", [128, 512], BF16)
        self.d_fbf = Dep()
        self.base_mark = kb.mark()

    def dbg_copy(self, name, src_ap, dep):
        if name in self.dbg:
            self.kb.dma("sp", self.dbg[name], src_ap, r=[dep])

    def build(self):
        cfg = self.cfg
        only = cfg.get("only")
        self.consts()
        self.cast_gen = self.cast_weights_gen()
        if only in (None, "s5pre", "mixer", "m2"):
            self.s5_prologue()
        for _ in self.cast_gen:
            pass
        self.alloc_seq_persistent()
        for s in range(self.nseq):
            xin = lambda t, s=s: self.x[s, t * T:(t + 1) * T, :]
            yout = lambda t, s=s: self.y[s, t * T:(t + 1) * T, :]
            xa = lambda t: self.xa[t * T:(t + 1) * T, :]
            xb = lambda t: self.xb[t * T:(t + 1) * T, :]
            dy = Dep()
            if only == "ffn1":
                self.ffn(0, 1, xin, self.d_in, yout, dy, 0)
                continue
            if only == "m1":
                self.mixer_a(0, xin, self.d_in)
                continue
            if only == "mixer":
                self.mixer_a(0, xin, self.d_in)
                self.mixer_s5(0)
                self.mixer_b(0, xin, self.d_in, yout, dy)
                continue
            if only == "m2":
                self.mixer_a(0, xin, self.d_in)
                self.mixer_s5(0)
                continue
            if only == "s5pre":
                continue
            A = (xa, self.d_xa)
            Bf = (xb, self.d_xb)
            for l in range(DEPTH):
                src, sdep = (xin, self.d_in) if l == 0 else A
                P, Q = (A, Bf) if l == 0 else (Bf, A)
                last = l == DEPTH - 1
                self.ffn(l, 1, src, sdep, P[0], P[1], 0)
                self.mixer_a(l, P[0], P[1])
                self.mixer_s5(l)
                self.mixer_b(l, P[0], P[1], Q[0], Q[1])
                if last:
                    self.ffn(l, 2, Q[0], Q[1], yout, dy, 2)
                else:
                    self.ffn(l, 2, Q[0], Q[1], A[0], A[1], 2)
        self.kb.barrier()
        self.dbg_copy("dbg_ks", self.ks, self.d_scr["ks"])
        self.dbg_copy("dbg_vs", self.vs, self.d_scr["vs"])
        self.dbg_copy("dbg_us", self.us, self.d_scr["us"])
        self.dbg_copy("dbg_zs", self.zs, self.d_scr["zs"])
        if "dbg_bs" in self.dbg:
            self.kb.dma("sp", self.dbg["dbg_bs"], self.bs[:], r=self.d_bs)
        self.dbg_copy("dbg_s5m", self.s5m[0], self.d_s5m[0])
        self.kb.finish()
        return self.nc


def host_inputs(inp):
    wpack = np.stack([pack_weights(inp, l) for l in range(DEPTH)])
    lntab = np.empty((DEPTH, 3, 2, 128, D), np.float32)
    for l in range(DEPTH):
        for i, nm in enumerate(("ln1", "ln2", "ln3")):
            lntab[l, i, 0] = rep128(inp[f"{nm}_g"][l])
            lntab[l, i, 1] = rep128(inp[f"{nm}_b"][l])
    ct, rope = const_tables()
    bgate = np.stack([np.ascontiguousarray(inp["b_gate"][l].reshape(16, 128).T) for l in range(DEPTH)])
    s5 = [s5_host_layout(inp, l) for l in range(DEPTH)]
    return {"wpack": wpack, "lntab": lntab, "ctab": ct, "rope": rope, "bgate": bgate.astype(np.float32),
            "s5p": np.stack([a for a, _ in s5]), "s5d": np.stack([b for _, b in s5])}

TI_SO_RE = 0
TI_SO_IM = NB
TI_T0 = 2 * NB
TI_TF = TI_T0
TI_TB = TI_T0 + NB - 1
TI_SI_RE = TI_T0 + 1 + 2 * (NB - 1)
TI_SI_IM = TI_SI_RE + NB
GCH = 4


def _s5_prologue(self):
    kb = self.kb
    self.s5a = self.nc.dram_tensor("s5a", [DEPTH, 2, 128, 64], F32, kind="Internal").ap()
    self.d_s5a = [Dep() for _ in range(DEPTH)]
    for l in range(DEPTH):
        self._s5_prologue_layer(l)


def _s5_prologue_layer(self, l):
    kb = self.kb
    self.phase_begin()
    NDG = 64
    prm = kb.sb("prm", [128, S5P_W], F32)
    dprm = Dep()
    kb.dma("sp", prm[:], self.s5p[l], w=[dprm])
    dcol = kb.sb("dcol", [128, 64], F32)
    ddcol = Dep()
    kb.dma("sp", dcol[:], self.s5d[l], w=[ddcol])
    lre = prm[:, 0:64]
    lim = prm[:, 64:128]
    ldt = prm[:, 128:192]
    bre = prm[:, 192:192 + 1024].rearrange("n (a p) -> n a p", p=16)
    bim = prm[:, 192 + 1024:192 + 2048].rearrange("n (a p) -> n a p", p=16)
    cre = prm[:, 192 + 2048:192 + 3072].rearrange("n (a p) -> n a p", p=16)
    cim = prm[:, 192 + 3072:192 + 4096].rearrange("n (a p) -> n a p", p=16)

    W = {}
    dW = {}

    def wt(name, shape=(128, NDG)):
        W[name] = kb.sb("s5w_" + name, list(shape), F32)
        dW[name] = Dep()
        return W[name]

    for nm in ("lr", "dt", "lrdt", "mag", "imag", "th", "c", "s", "t1", "t2", "abr", "abi", "ivr", "ivi", "den",
               "fre", "fim", "nre", "hpi"):
        wt(nm)
    wt("bbr", (128, NDG, 16))
    wt("bbi", (128, NDG, 16))
    wt("b1", (128, NDG, 16))
    wt("b2", (128, NDG, 16))
    PAr = kb.sb("PAr", [128, C1 + 1, NDG], F32)
    PAi = kb.sb("PAi", [128, C1 + 1, NDG], F32)
    PDr = kb.sb("PDr", [128, C1 + 1, NDG], F32)
    PDi = kb.sb("PDi", [128, C1 + 1, NDG], F32)
    Nr = kb.sb("Nr", [128, 8, NDG], F32)
    Ni = kb.sb("Ni", [128, 8, NDG], F32)
    dP = Dep()

    def tt(eng, out, a, b, op, r, w):
        kb.op(eng, lambda e: e.tensor_tensor(out=out, in0=a, in1=b, op=op), r=r, w=w)

    def ts(eng, out, a, s1, op0, r, w):
        kb.op(eng, lambda e: e.tensor_scalar(out=out, in0=a, scalar1=s1, scalar2=None, op0=op0), r=r, w=w)

    def act(out, a, func, r, w, scale=None, bias=None):
        kw = {}
        if scale is not None:
            kw["scale"] = scale
        if bias is not None:
            kw["bias"] = bias
        kb.op("act", lambda e: e.activation(out=out, in_=a, func=func, **kw), r=r, w=w)

    A_ = lambda n: W[n][:]
    d_ = lambda n: dW[n]
    ts("dve", A_("lr"), lre, -1e-4, ALU.min, [dprm], [d_("lr")])
    act(A_("dt"), ldt, AF.Exp, [dprm], [d_("dt")])
    tt("dve", A_("lrdt"), A_("lr"), A_("dt"), ALU.mult, [d_("lr"), d_("dt")], [d_("lrdt")])
    act(A_("mag"), A_("lrdt"), AF.Exp, [d_("lrdt")], [d_("mag")])
    act(A_("imag"), A_("lrdt"), AF.Exp, [d_("lrdt")], [d_("imag")], scale=-1.0)
    tt("dve", A_("th"), lim, A_("dt"), ALU.mult, [dprm, d_("dt")], [d_("th")])
    kb.op("pool", lambda e: e.memset(W["hpi"][:], math.pi / 2), w=[d_("hpi")])
    act(A_("s"), A_("th"), AF.Sin, [d_("th")], [d_("s")], scale=1.0 / 16)
    act(A_("c"), A_("th"), AF.Sin, [d_("th"), d_("hpi")], [d_("c")], scale=1.0 / 16, bias=W["hpi"][:, 0:1])
    for _ in range(4):
        tt("dve", A_("t1"), A_("c"), A_("c"), ALU.mult, [d_("c")], [d_("t1")])
        tt("pool", A_("t2"), A_("s"), A_("s"), ALU.mult, [d_("s")], [d_("t2")])
        tt("dve", A_("t2"), A_("t1"), A_("t2"), ALU.subtract, [d_("t1"), d_("t2")], [d_("t2")])
        tt("pool", A_("t1"), A_("c"), A_("s"), ALU.mult, [d_("c"), d_("s")], [d_("t1")])
        ts("dve", A_("s"), A_("t1"), 2.0, ALU.mult, [d_("t1")], [d_("s")])
        kb.op("act", lambda e: e.activation(out=W["c"][:], in_=W["t2"][:], func=AF.Copy), r=[d_("t2")], w=[d_("c")])
    tt("dve", A_("abr"), A_("mag"), A_("c"), ALU.mult, [d_("mag"), d_("c")], [d_("abr")])
    tt("pool", A_("abi"), A_("mag"), A_("s"), ALU.mult, [d_("mag"), d_("s")], [d_("abi")])
    tt("dve", A_("ivr"), A_("imag"), A_("c"), ALU.mult, [d_("imag"), d_("c")], [d_("ivr")])
    tt("pool", A_("ivi"), A_("imag"), A_("s"), ALU.mult, [d_("imag"), d_("s")], [d_("ivi")])
    ts("dve", A_("ivi"), A_("ivi"), -1.0, ALU.mult, [d_("ivi")], [d_("ivi")])
    tt("dve", A_("t1"), A_("lr"), A_("lr"), ALU.mult, [d_("lr")], [d_("t1")])
    tt("pool", A_("t2"), lim, lim, ALU.mult, [dprm], [d_("t2")])
    tt("dve", A_("den"), A_("t1"), A_("t2"), ALU.add, [d_("t1"), d_("t2")], [d_("den")])
    kb.op("dve", lambda e: e.reciprocal(out=W["den"][:], in_=W["den"][:]), r=[d_("den")], w=[d_("den")])
    ts("dve", A_("nre"), A_("abr"), -1.0, ALU.add, [d_("abr")], [d_("nre")])
    tt("dve", A_("t1"), A_("nre"), A_("lr"), ALU.mult, [d_("nre"), d_("lr")], [d_("t1")])
    tt("pool", A_("t2"), A_("abi"), lim, ALU.mult, [d_("abi"), dprm], [d_("t2")])
    tt("dve", A_("t1"), A_("t1"), A_("t2"), ALU.add, [d_("t1"), d_("t2")], [d_("t1")])
    tt("dve", A_("fre"), A_("t1"), A_("den"), ALU.mult, [d_("t1"), d_("den")], [d_("fre")])
    tt("dve", A_("t1"), A_("abi"), A_("lr"), ALU.mult, [d_("abi"), d_("lr")], [d_("t1")])
    tt("pool", A_("t2"), A_("nre"), lim, ALU.mult, [d_("nre"), dprm], [d_("t2")])
    tt("dve", A_("t1"), A_("t1"), A_("t2"), ALU.subtract, [d_("t1"), d_("t2")], [d_("t1")])
    tt("dve", A_("fim"), A_("t1"), A_("den"), ALU.mult, [d_("t1"), d_("den")], [d_("fim")])
    fre_b = W["fre"][:].unsqueeze(2).broadcast_to([128, NDG, 16])
    fim_b = W["fim"][:].unsqueeze(2).broadcast_to([128, NDG, 16])
    tt("dve", A_("b1"), bre, fre_b, ALU.mult, [dprm, d_("fre")], [d_("b1")])
    tt("pool", A_("b2"), bim, fim_b, ALU.mult, [dprm, d_("fim")], [d_("b2")])
    tt("dve", A_("bbr"), A_("b1"), A_("b2"), ALU.subtract, [d_("b1"), d_("b2")], [d_("bbr")])
    tt("dve", A_("b1"), bim, fre_b, ALU.mult, [dprm, d_("fre")], [d_("b1")])
    tt("pool", A_("b2"), bre, fim_b, ALU.mult, [dprm, d_("fim")], [d_("b2")])
    tt("dve", A_("bbi"), A_("b1"), A_("b2"), ALU.add, [d_("b1"), d_("b2")], [d_("bbi")])

    def powers(Pr, Pi, n, br, bi, dbr, dbi):
        kb.op("pool", lambda e: e.memset(Pr[:, 0, :], 1.0), w=[dP])
        kb.op("pool", lambda e: e.memset(Pi[:, 0, :], 0.0), w=[dP])
        kb.op("act", lambda e: e.activation(out=Pr[:, 1, :], in_=br, func=AF.Copy), r=[dbr], w=[dP])
        kb.op("act", lambda e: e.activation(out=Pi[:, 1, :], in_=bi, func=AF.Copy), r=[dbi], w=[dP])
        for j in range(1, n - 1):
            tt("dve", A_("t1"), Pr[:, j, :], br, ALU.mult, [dP, dbr], [d_("t1")])
            tt("pool", A_("t2"), Pi[:, j, :], bi, ALU.mult, [dP, dbi], [d_("t2")])
            tt("dve", Pr[:, j + 1, :], A_("t1"), A_("t2"), ALU.subtract, [d_("t1"), d_("t2")], [dP])
            tt("dve", A_("t1"), Pr[:, j, :], bi, ALU.mult, [dP, dbi], [d_("t1")])
            tt("pool", A_("t2"), Pi[:, j, :], br, ALU.mult, [dP, dbr], [d_("t2")])
            tt("dve", Pi[:, j + 1, :], A_("t1"), A_("t2"), ALU.add, [d_("t1"), d_("t2")], [dP])

    powers(PAr, PAi, C1 + 1, A_("abr"), A_("abi"), d_("abr"), d_("abi"))
    powers(Nr, Ni, 8, A_("ivr"), A_("ivi"), d_("ivr"), d_("ivi"))
    for j in range(C1 + 1):
        kb.op("act", lambda e, j=j: e.activation(out=PDr[:, j, :], in_=PAr[:, C1 - j, :], func=AF.Copy), r=[dP],
              w=[dP])
        kb.op("pool", lambda e, j=j: e.tensor_copy(out=PDi[:, j, :], in_=PAi[:, C1 - j, :]), r=[dP], w=[dP])
    for ri, PX in enumerate((PAr, PAi)):
        for gh in range(2):
            for dr in range(2):
                kb.dma("sp", self.s5a[l, ri, dr * 64:(dr + 1) * 64, gh * 32:(gh + 1) * 32],
                       PX[gh * 64:(gh + 1) * 64, C1, dr * 32:(dr + 1) * 32], r=[dP], w=[self.d_s5a[l]])

    G = GCH
    pl_r = Rot(kb, "pl", 8, [128, G, 8, 16], F32)
    pk_r = Rot(kb, "plk", 4, [128, G, 8, 16], F32)
    tm_r = Rot(kb, "ptm", 8, [128, G, 8, 16], F32)
    stg = [kb.sb(f"stg{gh}", [128, G, TI_SI_RE, 128], BF16) for gh in range(2)]
    dstg = [Dep(), Dep()]
    sib_r = Rot(kb, "sib", 4, [128, G, 128], BF16)
    t0_r = [Rot(kb, f"t0t{gh}", 2, [128, 128], F32) for gh in range(2)]
    pX = [self.pT, self.pO]

    pending = []

    def flush_pending():
        while pending:
            pending.pop(0)()

    def outer(Mr, Mi, dM, Pr, Pi, j0, dg0, neg_im=False, keep=False):
        next(self.cast_gen, None)
        mr = Mr[:, dg0:dg0 + G, :].unsqueeze(2).broadcast_to([128, G, 8, 16])
        mi = Mi[:, dg0:dg0 + G, :].unsqueeze(2).broadcast_to([128, G, 8, 16])
        pr = Pr[:, j0:j0 + 8, dg0:dg0 + G].rearrange("n j g -> n g j").unsqueeze(3).broadcast_to([128, G, 8, 16])
        pi = Pi[:, j0:j0 + 8, dg0:dg0 + G].rearrange("n j g -> n g j").unsqueeze(3).broadcast_to([128, G, 8, 16])
        re, dre = (pk_r if keep else pl_r).next()
        im, dim = (pk_r if keep else pl_r).next()
        a, da = tm_r.next()
        b, db = tm_r.next()
        a2, da2 = tm_r.next()
        b2, db2 = tm_r.next()
        tt("dve", a[:], mr, pr, ALU.mult, dM + [dP], [da])
        tt("pool", b[:], mi, pi, ALU.mult, dM + [dP], [db])
        tt("pool", a2[:], mr, pi, ALU.mult, dM + [dP], [da2])
        tt("dve", b2[:], mi, pr, ALU.mult, dM + [dP], [db2])
        flush_pending()

        def fin():
            tt("dve", re[:], a[:], b[:], ALU.subtract, [da, db], [dre])
            if neg_im:
                kb.op("dve", lambda e: e.scalar_tensor_tensor(out=im[:], in0=a2[:], scalar=-1.0, in1=b2[:],
                                                             op0=ALU.mult, op1=ALU.subtract), r=[da2, db2], w=[dim])
            else:
                tt("dve", im[:], a2[:], b2[:], ALU.add, [da2, db2], [dim])

        pending.append(fin)
        return re, dre, im, dim

    BB = (W["bbr"], W["bbi"], [dW["bbr"], dW["bbi"]])
    CC = (cre, cim, [dprm])

    def fl(p, gh, gi):
        return p[gh * 64:(gh + 1) * 64, gi].rearrange("n s c -> n (s c)")

    def toep(K, Q, gh, gi):
        flush_pending()
        kre, dkre, kim, dkim = K
        qre, dqre, qim, dqim = Q
        pT, dpT = pX[gh].next()
        kb.op("pe", lambda e: e.matmul(pT[:, 0:128], fl(kre, gh, gi), fl(qre, gh, gi), start=True, stop=False),
              r=[dkre, dqre], w=[dpT])
        kb.op("pe", lambda e: e.matmul(pT[:, 0:128], fl(kim, gh, gi), fl(qim, gh, gi), start=False, stop=True),
              r=[dkim, dqim], w=[dpT])
        return pT, dpT

    castgen = self.cast_gen
    for g0 in range(0, 32, G):
        f0, b0 = g0, 32 + g0
        K0f = outer(*BB[:2], BB[2], Nr, Ni, 0, f0)
        Q0f = outer(*CC[:2], CC[2], PAr, PAi, 0, f0, neg_im=True)
        K0b = outer(*BB[:2], BB[2], PAr, PAi, 0, b0)
        Q0b = outer(*CC[:2], CC[2], Nr, Ni, 0, b0, neg_im=True)
        for gi in range(G):
            for gh in range(2):
                g = gh * 32 + g0 + gi
                pT, dpT = toep(K0f, Q0f, gh, gi)
                t0, dt0 = t0_r[gh].next()
                tt("dve", t0[:], pT[:, 0:128], self.ctv(CT_MF, 128), ALU.mult, [dpT, self.d_const], [dt0])
                pT2, dpT2 = toep(K0b, Q0b, gh, gi)
                t1_, dt1_ = t0_r[gh].next()
                tt("dve", t1_[:], pT2[:, 0:128], self.ctv(CT_MB, 128), ALU.mult, [dpT2, self.d_const], [dt1_])
                tt("pool", t0[:], t0[:], t1_[:], ALU.add, [dt0, dt1_], [dt0])
                kb.op("dve", lambda e, gi=gi, gh=gh, g=g, t0=t0: e.scalar_tensor_tensor(
                    out=stg[gh][:, gi, TI_T0, :], in0=self.idf[:], scalar=dcol[:, g:g + 1], in1=t0[:],
                    op0=ALU.mult, op1=ALU.add), r=[dt0, ddcol, self.d_const], w=[dstg[gh]])
        def so_tiles(X, dr, J):
            flush_pending()
            for ri in range(2):
                pl, dpl = X[2 * ri], X[2 * ri + 1]
                for gi in range(G):
                    for gh in range(2):
                        pT, dpT = pX[gh].next()
                        kb.op("pe", lambda e, pl=pl, gi=gi, gh=gh, pT=pT: e.transpose(
                            pT[:, 0:64], fl(pl, gh, gi), self.idf[gh * 64:(gh + 1) * 64, gh * 64:(gh + 1) * 64]),
                              r=[dpl, self.d_const], w=[dpT])
                        ti = (TI_SO_RE if ri == 0 else TI_SO_IM) + J
                        if gi % 2 == 0:
                            kb.op("act", lambda e, gi=gi, gh=gh, pT=pT, ti=ti, dr=dr: e.activation(
                                out=stg[gh][:, gi, ti, dr * 64:(dr + 1) * 64], in_=pT[:, 0:64], func=AF.Copy),
                                  r=[dpT], w=[dstg[gh]])
                        else:
                            kb.op("dve", lambda e, gi=gi, gh=gh, pT=pT, ti=ti, dr=dr: e.tensor_copy(
                                out=stg[gh][:, gi, ti, dr * 64:(dr + 1) * 64], in_=pT[:, 0:64]), r=[dpT],
                                  w=[dstg[gh]])

        def si_tiles(Y, dr, I):
            flush_pending()
            for ri in range(2):
                pl, dpl = Y[2 * ri], Y[2 * ri + 1]
                sb_, dsb = sib_r.next()
                kb.op("act", lambda e, pl=pl, sb_=sb_: e.activation(
                    out=sb_[:], in_=pl[:].rearrange("n g s c -> n g (s c)"), func=AF.Copy), r=[dpl], w=[dsb])
                ti = (TI_SI_RE if ri == 0 else TI_SI_IM) + I
                for gh in range(2):
                    ga = gh * 32 + g0
                    kb.dma("sp", self.s5m[l, ga:ga + G, dr * 64:(dr + 1) * 64, ti, :].rearrange("g p c -> p g c"),
                           sb_[gh * 64:(gh + 1) * 64], r=[dsb], w=[self.d_s5m[l]])

        so_tiles(K0b, 1, 0)
        KAf = outer(*BB[:2], BB[2], PDr, PDi, C1 - 7, f0, keep=True)
        QAb = outer(*CC[:2], CC[2], PDr, PDi, C1 - 7, b0, neg_im=True, keep=True)
        for Dd in range(1, NB):
            QD = outer(*CC[:2], CC[2], PAr, PAi, 8 * (Dd - 1) + 1, f0, neg_im=True)
            KD_ = outer(*BB[:2], BB[2], PAr, PAi, 8 * (Dd - 1) + 1, b0)
            for gi in range(G):
                for gh in range(2):
                    pT, dpT = toep(KAf, QD, gh, gi)
                    kb.op("act", lambda e, gi=gi, gh=gh, pT=pT, Dd=Dd: e.activation(
                        out=stg[gh][:, gi, TI_TF + Dd, :], in_=pT[:, 0:128], func=AF.Copy), r=[dpT], w=[dstg[gh]])
                    pT2, dpT2 = toep(KD_, QAb, gh, gi)
                    kb.op("act", lambda e, gi=gi, gh=gh, pT2=pT2, Dd=Dd: e.activation(
                        out=stg[gh][:, gi, TI_TB + Dd, :], in_=pT2[:, 0:128], func=AF.Copy), r=[dpT2], w=[dstg[gh]])
            si_tiles(QD, 0, Dd - 1)
        for J in range(NB):
            if J < NB - 1:
                Xf = outer(*BB[:2], BB[2], PDr, PDi, 1 + 8 * J, f0)
            else:
                Xf = KAf
            so_tiles(Xf, 0, J)
            if J >= 1:
                Xb = outer(*BB[:2], BB[2], PAr, PAi, 8 * J, b0)
                so_tiles(Xb, 1, J)
        for gh in range(2):
            ga = gh * 32 + g0
            kb.dma("sp", self.s5m[l, ga:ga + G, :, 0:TI_SI_RE, :].rearrange("g p t c -> p g t c"), stg[gh][:],
                   r=[dstg[gh]], w=[self.d_s5m[l]])
        Yf = outer(*CC[:2], CC[2], PAr, PAi, 8 * (NB - 1) + 1, f0, neg_im=True)
        si_tiles(Yf, 0, NB - 1)
        for I in range(NB):
            Yb = outer(*CC[:2], CC[2], PDr, PDi, 8 * I, b0, neg_im=True)
            si_tiles(Yb, 1, I)


def _mixer_s5(self, l):
    kb = self.kb
    self.phase_begin()
    m0 = kb.mark()
    assert m0 == self.u2a_off
    u2a = self.u2a
    kb.sb_off += NBLK * 64 * 2
    m1 = kb.mark()
    kb.release(m0)
    Hb = kb.sb("Hb", [128, NK, 64, 2], F32)
    kb.release(m1)
    d_u2a = self.d_u2a
    u2 = kb.sb("u2", [128, 64, NB, NK], BF16)
    d_u2 = [Dep() for _ in range(64)]
    Sb = kb.sb("Sb", [128, NK, 64, 2], F32)
    Hbf = kb.sb("Hbf", [128, 64, 2, NK], BF16)
    d_Sall = Dep()
    d_Hf, d_Hb2, d_Hbf = Dep(), Dep(), Dep()
    Ar = kb.sb("Ar", [128, 64], F32)
    Ai = kb.sb("Ai", [128, 64], F32)
    dA = Dep()
    G1, G2 = 8, 2
    t1_r = Rot(kb, "s5t1", 2, [128, G1, NT1, 128], BF16)
    t2_r = Rot(kb, "s5t2", 3, [128, G2, NT2, 128], BF16)
    tmpf = [kb.sb(f"scf{i}", [128, 64, 2], F32) for i in range(2)]
    tmpb = [kb.sb(f"scb{i}", [128, 64, 2], F32) for i in range(2)]
    dtf = [Dep(), Dep()]
    dtb = [Dep(), Dep()]
    kb.dma("sp", Ar[:], self.s5a[l, 0], r=[self.d_s5a[l]], w=[dA])
    kb.dma("sp", Ai[:], self.s5a[l, 1], r=[self.d_s5a[l]], w=[dA])
    nc = self.nc

    def load_t1(i):
        t, dt_ = t1_r.next()
        kb.dma("sp", t[:], self.s5m[l, i * G1:(i + 1) * G1, :, 0:NT1, :].rearrange("g p t c -> p g t c"),
               r=[self.d_s5m[l]], w=[dt_])
        return t, dt_

    def load_t2(i):
        t, dt_ = t2_r.next()
        kb.dma("sp", t[:], self.s5m[l, i * G2:(i + 1) * G2, :, NT1:NT1 + NT2, :].rearrange("g p t c -> p g t c"),
               r=[self.d_s5m[l]], w=[dt_])
        return t, dt_

    nx1 = [load_t1(0)]
    u2a_v = u2a[:].rearrange("p (k j) g -> p g j k", j=NB)

    for i in range(64 // G1):
        tl, dtl = nx1[i]
        if i + 1 < 64 // G1:
            nx1.append(load_t1(i + 1))
        for gi in range(G1):
            g = i * G1 + gi
            pGa, dpGa = self.pA.next() if g % 2 == 0 else self.pU.next()
            kb.op("pe", lambda e, g=g, pGa=pGa: e.matmul(pGa[:, 0:NBLK].rearrange("p (j k) -> p j k", k=NK),
                                                        self.idb[:], u2a_v[:, g], start=True, stop=True),
                  r=[d_u2a, self.d_const], w=[dpGa])
            if g % 2 == 0:
                kb.op("act", lambda e, g=g, pGa=pGa: e.activation(out=u2[:, g].rearrange("p j k -> p (j k)"),
                                                                  in_=pGa[:, 0:NBLK], func=AF.Copy), r=[dpGa],
                      w=[d_u2[g]])
            else:
                kb.op("dve", lambda e, g=g, pGa=pGa: e.tensor_copy(out=u2[:, g].rearrange("p j k -> p (j k)"),
                                                                   in_=pGa[:, 0:NBLK]), r=[dpGa], w=[d_u2[g]])
            pS, dpS = self.pO.next()
            for ri in range(2):
                for hf in range(2):
                    for J in range(NB):
                        ti = (TI_SO_RE if ri == 0 else TI_SO_IM) + J
                        rhs = u2[:, g, J, :] if hf == 0 else u2[:, g, J, ::-1]
                        kb.op("pe", lambda e, ri=ri, J=J, ti=ti, gi=gi, pS=pS, tl=tl, hf=hf, rhs=rhs: e.matmul(
                            pS[hf * 64:(hf + 1) * 64, ri * NK:(ri + 1) * NK], tl[:, gi, ti, hf * 64:(hf + 1) * 64], rhs,
                            start=(J == 0), stop=(J == NB - 1)), r=[dtl, d_u2[g]], w=[dpS])
            src_ = pS[:, 0:2 * NK].rearrange("p (r k) -> p k r", k=NK)
            if g % 2 == 1:
                kb.op("act", lambda e, g=g, src_=src_: e.activation(out=Sb[:, :, g, :], in_=src_, func=AF.Copy),
                      r=[dpS], w=[d_Sall])
            else:
                kb.op("dve", lambda e, g=g, src_=src_: e.tensor_copy(out=Sb[:, :, g, :], in_=src_), r=[dpS],
                      w=[d_Sall])
    nx2 = [load_t2(0), load_t2(1)]

    dHs = [d_Hf, d_Hb2]
    GH = [slice(0, 32), slice(32, 64)]
    tmps = [tmpf, tmpb]
    dtmps = [dtf, dtb]
    kb.op("pool", lambda e: e.memset(Hb[:, 0], 0.0), w=[d_Hf, d_Hb2, d_u2a])
    pSc, dpSc = self.pT.t[0], self.pT.d[0]
    ArP = pSc[:, 0:64]
    AiP = pSc[:, 64:128]
    T13P = [pSc[:, 128:256].rearrange("p (g r) -> p g r", r=2), pSc[:, 256:384].rearrange("p (g r) -> p g r", r=2)]
    kb.op("act", lambda e: e.activation(out=ArP, in_=Ar[:], func=AF.Copy), r=[dA], w=[dpSc])
    kb.op("act", lambda e: e.activation(out=AiP, in_=Ai[:], func=AF.Copy), r=[dA], w=[dpSc])
    dT13P = [Dep(), Dep()]
    for k in range(NK - 1):
        ops = [[], []]
        for ci in range(2):
            gs_ = GH[ci]
            T13, T24 = T13P[ci], tmps[ci][1]
            dtmp = [dT13P[ci], dtmps[ci][1]]
            dH = dHs[ci]
            Arb = ArP[:, gs_].unsqueeze(2).broadcast_to([128, 32, 2])
            Aib = AiP[:, gs_].unsqueeze(2).broadcast_to([128, 32, 2])
            ops[ci] = [
                (lambda e, k=k, T13=T13, Arb=Arb, gs_=gs_: e.tensor_tensor(out=T13[:, gs_], in0=Hb[:, k, gs_], in1=Arb,
                                                                           op=ALU.mult), [dH, dpSc], [dtmp[0]]),
                (lambda e, k=k, T24=T24, Aib=Aib, gs_=gs_: e.tensor_tensor(out=T24[:, gs_], in0=Hb[:, k, gs_], in1=Aib,
                                                                           op=ALU.mult), [dH, dpSc], [dtmp[1]]),
                (lambda e, k=k, T13=T13, gs_=gs_: e.tensor_tensor(out=T13[:, gs_], in0=T13[:, gs_], in1=Sb[:, k, gs_],
                                                                  op=ALU.add), [dtmp[0], d_Sall], [dtmp[0]]),
                (lambda e, k=k, T13=T13, T24=T24, gs_=gs_: e.tensor_tensor(
                    out=Hb[:, k + 1, gs_, 0], in0=T13[:, gs_, 0], in1=T24[:, gs_, 1], op=ALU.subtract),
                 [dtmp[0], dtmp[1]], [dH, d_u2a]),
                (lambda e, k=k, T13=T13, T24=T24, gs_=gs_: e.tensor_tensor(
                    out=Hb[:, k + 1, gs_, 1], in0=T13[:, gs_, 1], in1=T24[:, gs_, 0], op=ALU.add),
                 [dtmp[0], dtmp[1]], [dH, d_u2a]),
            ]
        for j in range(5):
            for ci in range(2):
                fn, r_, w_ = ops[ci][j]
                kb.op("dve", fn, r=r_, w=w_)
    kb.op("act", lambda e: e.activation(out=Hbf[0:64], in_=Hb[0:64].rearrange("p k g r -> p g r k"), func=AF.Copy),
          r=[d_Hf, d_Hb2, d_u2a], w=[d_Hbf])
    kb.op("pool", lambda e: e.tensor_copy(out=Hbf[64:128], in_=Hb[64:128, ::-1].rearrange("p k g r -> p g r k")),
          r=[d_Hf, d_Hb2, d_u2a], w=[d_Hbf])

    for i in range(64 // G2):
        tl, dtl = nx2[i]
        if i + 2 < 64 // G2:
            nx2.append(load_t2(i + 2))
        for gi in range(G2):
            g = i * G2 + gi
            pY, dpY = self.pA.next() if g % 2 == 0 else self.pU.next()
            pYv = pY[:, 0:NBLK].rearrange("p (i k) -> p i k", k=NK)
            o = NT1
            mms = [(TI_T0 - o, pYv, u2[:, g, :, :])]
            for Dd in range(1, NB):
                mms.append((TI_TF + Dd - o, pYv[:, Dd:NB, :], u2[:, g, 0:NB - Dd, :]))
                mms.append((TI_TB + Dd - o, pYv[:, 0:NB - Dd, :], u2[:, g, Dd:NB, :]))
            for I in range(NB):
                mms.append((TI_SI_RE + I - o, pYv[:, I, :], Hbf[:, g, 0, :]))
                mms.append((TI_SI_IM + I - o, pYv[:, I, :], Hbf[:, g, 1, :]))
            for n_, (ti, oap, rap) in enumerate(mms):
                kb.op("pe", lambda e, ti=ti, oap=oap, rap=rap, n_=n_, tl=tl, gi=gi: e.matmul(
                    oap, tl[:, gi, ti, :], rap, start=(n_ == 0), stop=(n_ == len(mms) - 1)),
                      r=[dtl, d_u2[g], d_Hbf], w=[dpY])
            kb.op("act", lambda e, g=g, pYv=pYv: e.activation(out=u2a_v[:, g], in_=pYv, func=AF.Gelu_apprx_tanh),
                  r=[dpY, d_Hbf], w=[d_u2a])
    zsv = self.zs.rearrange("(b s) (p g) -> (s p) b g", s=8, p=16)
    self.zstore_evs = []
    with nc.allow_non_contiguous_dma(reason="128B runs"):
        for b0 in range(0, NBLK, 16):
            ev = kb.dma("sp", zsv[:, b0:b0 + 16, :], u2a[:, b0:b0 + 16, :], r=[d_u2a], w=[self.d_scr["zs"]])
            self.zstore_evs.append(ev)


Prog.s5_prologue = _s5_prologue
Prog._s5_prologue_layer = _s5_prologue_layer
Prog.mixer_s5 = _mixer_s5


def _mixer_b(self, l, src, src_dep, dst, dst_dep):
    kb = self.kb
    zev = list(getattr(self, "zstore_evs", []))
    self.phase_begin(dma=False)
    sbt = kb.sb
    z_r = Rot(kb, "zb", 2, [128, D], BF16)
    zT = sbt("zT", [128, 8, T], BF16)
    d_zT = Dep()
    gatedT = sbt("gatedT", [128, 8, T], BF16)
    d_gatedT = Dep()
    m1T = sbt("m1T", [128, 8, T], BF16)
    on = sbt("on", [128, D], F32)
    d_on = Dep()
    assert kb.sb_off - self.base_mark >= NBLK * 64 * 2
    first = {"zt": True, "zT": True, "gatedT": True, "m1T": True, "on": True}

    def xtra(name):
        if first.get(name):
            first[name] = False
            return zev
        return ()

    xT_r = Rot(kb, "xTb", 2, [128, 8, T], BF16)
    wslot = [sbt(f"wsl{i}", [128, 8, 512], BF16) for i in range(4)]
    dslot = [Dep() for _ in range(4)]
    rope_r = Rot(kb, "ropeb", 1, [128, 2, T], F32)
    kT_r = Rot(kb, "kTb", 1, [128, 4, T], BF16)
    qT = sbt("qT", [128, 4, T], BF16)
    qxf = sbt("qxf", [128, 4, T], BF16)
    qxb = sbt("qxb", [128, 4, T], BF16)
    d_qT, d_qxf, d_qxb = Dep(), Dep(), Dep()
    tmp_r = Rot(kb, "rtmpb", 3, [128, T], F32)
    v_r = Rot(kb, "vb", 2, [128, D], BF16)
    xres_r = Rot(kb, "xres", 2, [128, D], F32)
    ST_r = Rot(kb, "ST", 2, [128, 1024], BF16)
    kz_r = Rot(kb, "kzb", 2, [128, 512], BF16)
    sg_r = Rot(kb, "sg", 2, [128, D], F32)
    gated_r = Rot(kb, "gated", 2, [128, D], BF16)
    d_m1 = [Dep() for _ in range(8)]
    gr_r = Rot(kb, "gr", 2, [128, T], F32)
    s2_r = Rot(kb, "s2", 2, [128, T], F32)
    gs_r = Rot(kb, "gs", 2, [128, T], F32)
    bg = sbt("bg", [128, 16], F32)
    d_bg = Dep()
    gst_r = Rot(kb, "gst", 3, [128, 80], F32)
    self.alloc_ln(l, 1, nrr=1)
    kb.dma("sp", bg[:], self.bgate[l], w=[d_bg])
    F_, dF = self.fstate, self.d_fstate
    fbf, dfbf = self.fbf, self.d_fbf
    kb.op("pool", lambda e: e.memset(F_[:], 0.0), w=[dF])
    kb.op("pool", lambda e: e.memset(fbf[:], 0.0), w=[dfbf])
    wbf = self.wbf[l]

    def LW(name, h, slot):
        o = W_OFF[name]
        srcap = wbf[:, o:o + 8192].rearrange("p (k c) -> p k c", c=1024)[:, :, 512 * h:512 * h + 512]
        kb.dma("sp", wslot[slot][:], srcap, r=[self.d_wbf[l]], w=[dslot[slot]])

    def evac_T(pTb, dpT, out_ap, out_dep, eng, extra=()):
        src_ = pTb[:, 0:512].rearrange("p (a b) -> p a b", b=128)
        if eng == "act":
            kb.op("act", lambda e: e.activation(out=out_ap, in_=src_, func=AF.Copy), r=[dpT], w=[out_dep], extra=extra)
        else:
            kb.op("dve", lambda e: e.tensor_copy(out=out_ap, in_=src_), r=[dpT], w=[out_dep], extra=extra)

    NTL = L_SEQ // T

    def load_xT(t):
        xT, dxT = xT_r.next()
        kb.dma("sp", xT[:], self.xts[:, :, t * T:(t + 1) * T], r=[self.d_scr["xts"]], w=[dxT])
        return xT, dxT

    def load_rope(t):
        rp, drp = rope_r.next()
        kb.dma("sp", rp[:], self.rope[:, :, t * T:(t + 1) * T].rearrange("a p t -> p a t"), w=[drp])
        return rp, drp

    def load_kT(t):
        kT, dkT = kT_r.next()
        kb.dma("sp", kT[:], self.ks[:, :, t * T:(t + 1) * T], r=[self.d_scr["ks"]], w=[dkT])
        return kT, dkT

    LW("wq", 0, 0)
    LW("wq", 1, 1)
    nx_xT = load_xT(0)
    nx_rp = load_rope(0)
    nx_kT = load_kT(0)
    for t in range(NTL):
        LW("wg", 0, 2)
        LW("wg", 1, 3)
        xT, dxT = nx_xT
        rp, drp = nx_rp
        kT, dkT = nx_kT
        if t + 1 < NTL:
            nx_xT = load_xT(t + 1)
        for f in range(4):
            pA, dpA = self.pA.next()
            pU, dpU = self.pU.next()
            for k in range(8):
                kb.op("pe", lambda e, k=k, f=f, pA=pA: e.matmul(pA[:], wslot[0][:, k, f * 128:(f + 1) * 128], xT[:, k, :],
                                                              start=(k == 0), stop=(k == 7)), r=[dslot[0], dxT], w=[dpA])
            for k in range(8):
                kb.op("pe", lambda e, k=k, f=f, pU=pU: e.matmul(pU[:], wslot[1][:, k, f * 128:(f + 1) * 128], xT[:, k, :],
                                                              start=(k == 0), stop=(k == 7)), r=[dslot[1], dxT], w=[dpU])

            def mk_also(f):
                def fx(t1, dt1):
                    t1v = t1[:].rearrange("p (c i) -> p c i", i=128)
                    xfv = self.ctv(CT_XF + f * 128, 128).unsqueeze(1).broadcast_to([128, 4, 128])
                    xbv = self.ctv(CT_XB + f * 128, 128).unsqueeze(1).broadcast_to([128, 4, 128])
                    kb.op("dve", lambda e: e.tensor_tensor(out=qxf[:, f, :].rearrange("p (c i) -> p c i", i=128),
                                                          in0=t1v, in1=xfv, op=ALU.mult), r=[dt1, self.d_const],
                          w=[d_qxf])
                    kb.op("pool", lambda e: e.tensor_tensor(out=qxb[:, f, :].rearrange("p (c i) -> p c i", i=128),
                                                           in0=t1v, in1=xbv, op=ALU.mult), r=[dt1, self.d_const],
                          w=[d_qxb])
                return [fx]

            self.rotary(pA, dpA, pU, dpU, rp[:, 0, :], rp[:, 1, :], drp, tmp_r, qT[:, f, :], d_qT, also=mk_also(f))
        LW("glu_v", 0, 0)
        LW("glu_g", 0, 1)
        if t + 1 < NTL:
            nx_rp = load_rope(t + 1)
        st_c = {}

        def stage_A(c):
            cg = 4 * t + c
            rows = slice(cg * 128, (cg + 1) * 128)
            cs = slice(c * 128, (c + 1) * 128)
            v, dv = v_r.next()
            kb.dma("sp", v[:], self.vs[rows, :], r=[self.d_scr["vs"]], w=[dv])
            zt, dzt = z_r.next()
            kb.dma("sp", zt[:], self.zs[rows, :], r=[self.d_scr["zs"]], w=[dzt])
            ST, dST = ST_r.next()
            decv = self.ctv(CT_DEC, 1024).rearrange("p (hp h2 i) -> p hp h2 i", h2=2, i=128)
            pSs = [(self.pO.t[0], self.pO.d[0]), (self.pO.t[1], self.pO.d[1])]
            for hp in range(4):
                for h2 in range(2):
                    pS, dpS = pSs[h2]
                    b = h2 * 64
                    kb.op("pe", lambda e, b=b, hp=hp, pS=pS: e.matmul(
                        pS[:, hp * 128:(hp + 1) * 128], kT[b:b + 64, hp, cs], qT[b:b + 64, hp, cs], start=True,
                        stop=True), r=[dkT, d_qT], w=[dpS])
            for h2 in range(2):
                pS, dpS = pSs[h2]
                kb.op("dve", lambda e, h2=h2, pS=pS: e.tensor_tensor(
                    out=ST[:, h2 * 512:(h2 + 1) * 512].rearrange("p (a i) -> p a i", i=128),
                    in0=pS[:].rearrange("p (a i) -> p a i", i=128), in1=decv[:, :, h2, :], op=ALU.mult),
                      r=[dpS, self.d_const], w=[dST])
            kz, dkz = self.k_transpose_scaled(kT, dkT, c, CT_ZF, kz_r)
            sg, d_sg = sg_r.next()
            for half in range(2):
                pG, dpG = (self.pO.t[half], self.pO.d[half])
                for k in range(8):
                    kb.op("pe", lambda e, k=k, pG=pG, half=half: e.matmul(pG[:], xT[:, k, cs], wslot[2 + half][:, k, :],
                                                                         start=(k == 0), stop=(k == 7)),
                          r=[dxT, dslot[2 + half]], w=[dpG])
                kb.op("act", lambda e, pG=pG, half=half, sg=sg: e.activation(out=sg[:, half * 512:(half + 1) * 512],
                                                                             in_=pG[:], func=AF.Silu), r=[dpG], w=[d_sg])
            for grp in range(2):
                pT, dpT = self.pT.next()
                pTb = pT[:].bitcast(BF16)
                for kk in range(4):
                    kf = 4 * grp + kk
                    kb.op("pe", lambda e, kk=kk, kf=kf, pTb=pTb, zt=zt: e.transpose(
                        pTb[:, kk * 128:(kk + 1) * 128], zt[:, kf * 128:(kf + 1) * 128], self.idb[:]),
                          r=[dzt, self.d_const], w=[dpT])
                evac_T(pTb, dpT, zT[:, 4 * grp:4 * grp + 4, cs], d_zT, "act")
            st_c[c] = dict(v=v, dv=dv, ST=ST, dST=dST, kz=kz, dkz=dkz, sg=sg, d_sg=d_sg)

        def stage_B_pe(c):
            cg = 4 * t + c
            cs = slice(c * 128, (c + 1) * 128)
            S = st_c[c]
            v, dv, ST, dST = S["v"], S["dv"], S["ST"], S["dST"]
            pOs = [(self.pA.t[c % 2], self.pA.d[c % 2]), (self.pU.t[c % 2], self.pU.d[c % 2])]
            for hp in range(4):
                for h2 in range(2):
                    pOo, dpOo = pOs[h2]
                    h = 2 * hp + h2
                    b = h2 * 64
                    oap = pOo[:, hp * 128:(hp + 1) * 128]
                    kb.op("pe", lambda e, oap=oap, h2=h2, hp=hp, h=h: e.matmul(
                        oap, ST[:, h2 * 512 + hp * 128:h2 * 512 + (hp + 1) * 128], v[:, h * 128:(h + 1) * 128],
                        start=True, stop=False), r=[dST, dv], w=[dpOo])
                    kb.op("pe", lambda e, oap=oap, b=b, hp=hp: e.matmul(oap, qxf[b:b + 64, hp, cs],
                                                                       fbf[b:b + 64, hp * 128:(hp + 1) * 128],
                                                                       start=False, stop=False),
                          r=[d_qxf, dfbf], w=[dpOo])
                    kb.op("pe", lambda e, oap=oap, b=b, hp=hp, cg=cg: e.matmul(
                        oap, qxb[b:b + 64, hp, cs], self.bs[b:b + 64, cg, hp * 128:(hp + 1) * 128], start=False,
                        stop=True), r=[d_qxb, self.d_bs[cg]], w=[dpOo])
            S["pOs"] = pOs
            self.kv_state_step(S["kz"], S["dkz"], v, dv, F_, dF)
            kb.op("act", lambda e: e.activation(out=fbf[:], in_=F_[:], func=AF.Copy), r=[dF], w=[dfbf])

        def stage_B_post1(c):
            S = st_c[c]
            pOs = S["pOs"]
            gst, dgst = gst_r.next()
            dsth = [Dep() for _ in range(8)]
            dagh = [Dep() for _ in range(8)]
            for hg in range(2):
                pOo, dpOo = pOs[hg]
                for hh in range(4):
                    h = 2 * hh + hg
                    kb.op("dve", lambda e, h=h, hh=hh, pOo=pOo: e.bn_stats(out=gst[:, 6 * h:6 * h + 6],
                                                                           in_=pOo[:, hh * 128:(hh + 1) * 128]),
                          r=[dpOo], w=[dsth[h], dgst])
            for h in range(8):
                kb.op("dve", lambda e, h=h: e.bn_aggr(out=gst[:, 48 + 2 * h:50 + 2 * h],
                                                      in_=gst[:, 6 * h:6 * h + 6]), r=[dsth[h]], w=[dagh[h]])
            varv = gst[:, 48:64].rearrange("p (h t) -> p h t", t=2)[:, :, 1]
            drs = Dep()
            kb.op("dve", lambda e: e.tensor_scalar(out=gst[:, 64:72], in0=varv, scalar1=float(EPS), scalar2=None,
                                                  op0=ALU.add), r=dagh, w=[drs])
            kb.op("act", lambda e: e.activation(out=gst[:, 64:72], in_=gst[:, 64:72], func=AF.Sqrt), r=[drs],
                  w=[drs])
            S.update(gst=gst, dgst=dgst, dagh=dagh, drs=drs)

        def stage_B_post2(c):
            S = st_c[c]
            pOs, gst, dgst, dagh, drs = S["pOs"], S["gst"], S["dgst"], S["dagh"], S["drs"]
            kb.op("dve", lambda e: e.reciprocal(out=gst[:, 72:80], in_=gst[:, 64:72]), r=[drs], w=[drs])
            for hg in range(2):
                pOo, dpOo = pOs[hg]
                for hh in range(4):
                    h = 2 * hh + hg
                    kb.op("dve", lambda e, h=h, hh=hh, pOo=pOo: e.tensor_scalar(
                        out=on[:, h * 128:(h + 1) * 128], in0=pOo[:, hh * 128:(hh + 1) * 128],
                        scalar1=gst[:, 48 + 2 * h:49 + 2 * h], scalar2=gst[:, 72 + h:73 + h], op0=ALU.subtract,
                        op1=ALU.mult), r=[dpOo, drs, dagh[h]], w=[d_on, dgst], extra=xtra("on"))
            gated, dgated = gated_r.next()
            kb.op("pool", lambda e: e.tensor_tensor(out=gated[:], in0=S["sg"][:], in1=on[:], op=ALU.mult),
                  r=[S["d_sg"], d_on], w=[dgated])
            S["gated"], S["dgated"] = gated, dgated

        def stage_C(c):
            cs = slice(c * 128, (c + 1) * 128)
            S = st_c[c]
            gated, dgated = S["gated"], S["dgated"]
            for grp in range(2):
                pT, dpT = self.pT.next()
                pTb = pT[:].bitcast(BF16)
                for kk in range(4):
                    kf = 4 * grp + kk
                    kb.op("pe", lambda e, kk=kk, kf=kf, pTb=pTb: e.transpose(
                        pTb[:, kk * 128:(kk + 1) * 128], gated[:, kf * 128:(kf + 1) * 128], self.idb[:]),
                          r=[dgated, self.d_const], w=[dpT])
                evac_T(pTb, dpT, gatedT[:, 4 * grp:4 * grp + 4, cs], d_gatedT, "dve", extra=xtra("gatedT"))

        stage_A(0)
        for c in range(4):
            stage_B_pe(c)
            if c >= 1:
                stage_B_post2(c - 1)
            if c + 1 < 4:
                stage_A(c + 1)
            stage_B_post1(c)
            if c >= 1:
                stage_C(c - 1)
        stage_B_post2(3)
        LW("wgs", 0, 2)
        LW("glu_v", 1, 3)
        if t + 1 < NTL:
            nx_kT = load_kT(t + 1)
        xrs = []
        for c in range(4):
            xr, dxr = xres_r.next()
            xrs.append((xr, dxr))
        for c in range(2):
            kb.dma("sp", xrs[c][0][:], src(t)[c * 128:(c + 1) * 128, :], r=[src_dep], w=[xrs[c][1]])
        for fo in range(8):
            hs = fo // 4
            fc = slice((fo % 4) * 128, (fo % 4 + 1) * 128)
            gv_s, gg_s, gs_s = (0, 1, 2) if hs == 0 else (3, 0, 1)
            pV2, dpV2 = self.pA.next()
            pG2, dpG2 = self.pU.next()
            pGs, dpGs = self.pO.next()
            for (pX, dpX, sl_, rhsT, drhs) in ((pV2, dpV2, gv_s, zT, d_zT), (pG2, dpG2, gg_s, zT, d_zT),
                                               (pGs, dpGs, gs_s, xT, dxT)):
                for k in range(8):
                    kb.op("pe", lambda e, k=k, pX=pX, sl_=sl_, rhsT=rhsT, fc=fc: e.matmul(
                        pX[:], wslot[sl_][:, k, fc], rhsT[:, k, :], start=(k == 0), stop=(k == 7)),
                          r=[dslot[sl_], drhs], w=[dpX])
            s2, ds2 = s2_r.next()
            kb.op("act", lambda e, s2=s2, pG2=pG2: e.activation(out=s2[:], in_=pG2[:], func=AF.Sigmoid), r=[dpG2],
                  w=[ds2])
            kb.op("dve", lambda e, s2=s2, pV2=pV2: e.tensor_tensor(out=s2[:], in0=s2[:], in1=pV2[:], op=ALU.mult),
                  r=[ds2, dpV2], w=[ds2])
            gs, dgs = gs_r.next()
            kb.op("act", lambda e, gs=gs, pGs=pGs, fo=fo: e.activation(out=gs[:], in_=pGs[:], func=AF.Sigmoid,
                                                                       bias=bg[:, 8 + fo:9 + fo]), r=[dpGs, d_bg],
                  w=[dgs])
            kb.op("pool", lambda e, gs=gs, s2=s2, fo=fo: e.tensor_tensor(out=m1T[:, fo, :], in0=gs[:], in1=s2[:],
                                                                         op=ALU.mult), r=[dgs, ds2], w=[d_m1[fo]])
            if fo == 3:
                LW("glu_g", 1, 0)
                LW("wgs", 1, 1)
                LW("wo", 0, 2)
        stage_C(3)
        LW("wgr", 0, 3)
        LW("wo", 1, 0)
        LW("wgr", 1, 1)
        for fo in range(8):
            hs = fo // 4
            fc = slice((fo % 4) * 128, (fo % 4 + 1) * 128)
            wo_s, wgr_s = (2, 3) if hs == 0 else (0, 1)
            pY, dpY = self.pA.next()
            pGr, dpGr = self.pU.next()
            for k in range(8):
                kb.op("pe", lambda e, k=k, pGr=pGr, wgr_s=wgr_s, fc=fc: e.matmul(pGr[:], wslot[wgr_s][:, k, fc],
                                                                               xT[:, k, :], start=(k == 0),
                                                                               stop=(k == 7)),
                      r=[dslot[wgr_s], dxT], w=[dpGr])
            for k in range(8):
                kb.op("pe", lambda e, k=k, pY=pY, wo_s=wo_s, fc=fc: e.matmul(pY[:], wslot[wo_s][:, k, fc], gatedT[:, k, :],
                                                                           start=(k == 0), stop=(k == 7)),
                      r=[dslot[wo_s], d_gatedT], w=[dpY])
            gr, dgr = gr_r.next()
            kb.op("act", lambda e, gr=gr, pGr=pGr, fo=fo: e.activation(out=gr[:], in_=pGr[:], func=AF.Sigmoid,
                                                                       bias=bg[:, fo:fo + 1]), r=[dpGr, d_bg], w=[dgr])
            kb.op("dve", lambda e, gr=gr, pY=pY: e.tensor_tensor(out=gr[:], in0=gr[:], in1=pY[:], op=ALU.mult),
                  r=[dgr, dpY], w=[dgr])
            kb.op("pool", lambda e, gr=gr, fo=fo: e.tensor_tensor(out=m1T[:, fo, :], in0=m1T[:, fo, :], in1=gr[:],
                                                                  op=ALU.add), r=[dgr, d_m1[fo]], w=[d_m1[fo]])
            if fo == 3:
                LW("wout", 0, 2)
                LW("wout", 1, 3)
        if t + 1 < NTL:
            LW("wq", 0, 0)
            LW("wq", 1, 1)
        for c in range(4):
            cg = 4 * t + c
            cs = slice(c * 128, (c + 1) * 128)
            xr, dxr = xrs[c]
            if c >= 2:
                kb.dma("sp", xr[:], src(t)[c * 128:(c + 1) * 128, :], r=[src_dep], w=[dxr])
            rr, drr = self.rr.next()
            for half in range(2):
                pM, dpM = (self.pA if half == 0 else self.pU).next()
                for k in range(8):
                    kb.op("pe", lambda e, k=k, pM=pM, half=half: e.matmul(
                        pM[:], m1T[:, k, cs], wslot[2 + half][:, k, :], start=(k == 0), stop=(k == 7)),
                          r=[d_m1[k], dslot[2 + half]], w=[dpM])
                kb.op("dve", lambda e, pM=pM, half=half, rr=rr, xr=xr: e.scalar_tensor_tensor(
                    out=rr[:, half * 512:(half + 1) * 512], in0=xr[:, half * 512:(half + 1) * 512], scalar=float(ALPHA),
                    in1=pM[:], op0=ALU.mult, op1=ALU.add), r=[dxr, dpM], w=[drr])
            self.layer_norm(rr, drr, EPS, dst(t)[c * 128:(c + 1) * 128, :], dst_dep)


Prog.mixer_b = _mixer_b


N_CORES = 8
_PROG_CACHE = {}


def kernel(**inputs):
    xp = np.asarray(inputs["x_prompt"], np.float32)
    xs = np.asarray(inputs["x_sample"], np.float32)
    nb_p, nb_s = xp.shape[0], xs.shape[0]
    assert nb_p % N_CORES == 0 and nb_s % N_CORES == 0
    pp, ps_ = nb_p // N_CORES, nb_s // N_CORES
    nseq = pp + ps_
    inp = {k: np.asarray(v) for k, v in inputs.items() if not k.startswith("x_")}
    hi = host_inputs(inp)
    if nseq not in _PROG_CACHE:
        _PROG_CACHE[nseq] = Prog(nseq, {}).build()
    nc = _PROG_CACHE[nseq]
    in_maps = []
    for c in range(N_CORES):
        xc = np.concatenate([xp[c * pp:(c + 1) * pp], xs[c * ps_:(c + 1) * ps_]], axis=0)
        m = dict(hi)
        m["x"] = np.ascontiguousarray(xc)
        in_maps.append(m)
    res = run_bass_kernel_spmd(nc, in_maps, core_ids=list(range(N_CORES)))
    yp = np.concatenate([res.results[c]["y"][:pp] for c in range(N_CORES)], axis=0)
    ys = np.concatenate([res.results[c]["y"][pp:] for c in range(N_CORES)], axis=0)
    return (np.ascontiguousarray(yp, np.float32), np.ascontiguousarray(ys, np.float32))
```
